# Optimizing a Trainium2 kernel written in Bass

```python
import math
import jax
import jax.numpy as jnp
from jax import lax
import numpy as np

D_MODEL = 1024
BATCH = 8
SEQ = 2048
DEPTH = 2
DEC_BATCH = 128
DEC_SEQ = 1
PAST_LEN = 16384
PAGE_SIZE = 128

N_EVEN = (DEPTH + 1) // 2
N_ODD = DEPTH // 2
EXPAND = 2
W_MIX = EXPAND * D_MODEL
H_A = 8
DK_A = 128
DV_A = 128
CONV_W = 4
CONV_CH_A = H_A * (2 * DK_A + DV_A)
CHUNK_A = 64
H_B = 8
DK_B = 64
DV_B = 128
CHUNK_B = 64
H_C = 4
DK_C = 128
DV_C = 256
GLA_RANK = 16
GLA_TAU = 16.0
CHUNK_C = 16
H_D = 4
DK_D = 128
DV_D = 256
CHUNK_D = 64
ROPE_BASE = 10000.0
EPS = 1e-6

EVEN_SPLIT = [CONV_CH_A, H_A, H_A, H_B * DK_B, H_B * DK_B, H_B * DV_B, 2 * H_B, H_B * DV_B, W_MIX]
ODD_SPLIT = [H_C * DK_C, H_C * DK_C, H_C * DV_C, GLA_RANK, H_D * DK_D, H_D * DK_D, H_D * DV_D, W_MIX]
P_EVEN = sum(EVEN_SPLIT)
P_ODD = sum(ODD_SPLIT)

kernel_name = 'hybrid_gdn_mlstm_gla_retnet_step'


def _split(x, sizes):
    idx = [int(i) for i in np.cumsum(sizes)[:-1]]
    return jnp.split(x, idx, axis=-1)


def _rms(x, w):
    xf = x.astype(jnp.float32)
    y = xf * lax.rsqrt(jnp.mean(xf * xf, axis=-1, keepdims=True) + EPS)
    return (y * w.astype(jnp.float32)).astype(x.dtype)


def _l2n(x):
    return x * lax.rsqrt(jnp.sum(x * x, axis=-1, keepdims=True) + EPS)


def _to_chunks(x, L):
    B, T = x.shape[:2]
    return jnp.moveaxis(x.reshape((B, T // L, L) + x.shape[2:]), 1, 0)


def _from_chunks(y):
    y = jnp.moveaxis(y, 0, 1)
    return y.reshape((y.shape[0], y.shape[1] * y.shape[2]) + y.shape[3:])


def _causal_conv(u, buf, w):
    T = u.shape[1]
    full = jnp.concatenate([buf, u], axis=1)
    y = full[:, 0:T] * w[0]
    for j in range(1, CONV_W):
        y = y + full[:, j:j + T] * w[j]
    return y, full[:, T:]


def _rotary(x, pos):
    half = x.shape[-1] // 2
    inv = ROPE_BASE ** (-jnp.arange(half, dtype=jnp.float32) / half)
    ang = pos.astype(jnp.float32)[:, None] * inv[None, :]
    cos = jnp.cos(ang)[:, None, :]
    sin = jnp.sin(ang)[:, None, :]
    x1, x2 = x[..., :half], x[..., half:]
    return jnp.concatenate([x1 * cos - x2 * sin, x1 * sin + x2 * cos], axis=-1)


def _gated_delta(q, k, v, beta, g, S0):
    L = math.gcd(q.shape[1], CHUNK_A)
    tri_incl = jnp.tril(jnp.ones((L, L), bool))
    tri_strict = jnp.tril(jnp.ones((L, L), bool), -1)
    eye = jnp.eye(L, dtype=jnp.float32)

    def step(S, inp):
        qc, kc, vc, bc, gc = inp
        bh = jnp.moveaxis(jnp.cumsum(gc, axis=1), 1, 2)
        decay = jnp.exp(jnp.where(tri_incl, bh[..., :, None] - bh[..., None, :], -jnp.inf))
        beta_h = jnp.moveaxis(bc, 1, 2)
        kk = jnp.einsum('bthd,bshd->bhts', kc, kc)
        A = jnp.where(tri_strict, decay * kk, 0.0) * beta_h[..., :, None]
        gam = jnp.exp(bh)[..., None]
        rhs = beta_h[..., None] * (jnp.moveaxis(vc, 1, 2) - gam * jnp.einsum('bthd,bhde->bhte', kc, S))
        U = lax.linalg.triangular_solve(A + eye, rhs, left_side=True, lower=True, unit_diagonal=True)
        qk = jnp.einsum('bthd,bshd->bhts', qc, kc) * decay
        o = gam * jnp.einsum('bthd,bhde->bhte', qc, S) + jnp.einsum('bhts,bhse->bhte', qk, U)
        last = bh[..., -1:]
        S_new = jnp.exp(last)[..., None] * S + jnp.einsum('bshd,bhs,bhse->bhde', kc, jnp.exp(last - bh), U)
        return S_new, jnp.moveaxis(o, 1, 2)

    S, o = lax.scan(step, S0, tuple(_to_chunks(a, L) for a in (q, k, v, beta, g)))
    return _from_chunks(o), S


def _mlstm(q, k, v, ig, fg, C0, n0, m0):
    L = math.gcd(q.shape[1], CHUNK_B)
    tri = jnp.tril(jnp.ones((L, L), bool))

    def step(carry, inp):
        C, n, m = carry
        qc, kc, vc, ic, fc = inp
        b = jnp.moveaxis(jnp.cumsum(jax.nn.log_sigmoid(fc), axis=1), 1, 2)
        ih = jnp.moveaxis(ic, 1, 2)
        D = jnp.where(tri, b[..., :, None] - b[..., None, :] + ih[..., None, :], -jnp.inf)
        w0 = b + m[..., None]
        m_t = jnp.maximum(w0, jnp.max(D, axis=-1))
        P = jnp.exp(D - m_t[..., None]) * jnp.einsum('bthd,bshd->bhts', qc, kc)
        s0 = jnp.exp(w0 - m_t)
        num = s0[..., None] * jnp.einsum('bthd,bhde->bhte', qc, C) + jnp.einsum('bhts,bshe->bhte', P, vc)
        den = s0 * jnp.einsum('bthd,bhd->bht', qc, n) + jnp.sum(P, axis=-1)
        h = num / jnp.maximum(jnp.abs(den), jnp.exp(-m_t))[..., None]
        m_end = m_t[..., -1]
        we = jnp.exp(b[..., -1:] - b + ih - m_end[..., None])
        se = jnp.exp(b[..., -1] + m - m_end)
        C_new = se[..., None, None] * C + jnp.einsum('bshd,bhs,bshe->bhde', kc, we, vc)
        n_new = se[..., None] * n + jnp.einsum('bshd,bhs->bhd', kc, we)
        return (C_new, n_new, m_end), jnp.moveaxis(h, 1, 2)

    (C, n, m), h = lax.scan(step, (C0, n0, m0), tuple(_to_chunks(a, L) for a in (q, k, v, ig, fg)))
    return _from_chunks(h), C, n, m


def _gla(q, k, v, g, S0):
    L = math.gcd(q.shape[1], CHUNK_C)
    tri = jnp.tril(jnp.ones((L, L), bool))[None, :, :, None, None]

    def step(S, inp):
        qc, kc, vc, gc = inp
        b = jnp.cumsum(gc, axis=1)
        dec = jnp.exp(jnp.where(tri, b[:, :, None] - b[:, None, :], -jnp.inf))
        att = jnp.sum(qc[:, :, None] * kc[:, None] * dec, axis=-1)
        o = jnp.einsum('bthd,bhde->bthe', qc * jnp.exp(b), S) + jnp.einsum('btsh,bshe->bthe', att, vc)
        bl = b[:, -1]
        S_new = jnp.exp(bl)[..., None] * S + jnp.einsum('bshd,bshe->bhde', kc * jnp.exp(bl[:, None] - b), vc)
        return S_new, o

    S, o = lax.scan(step, S0, tuple(_to_chunks(a, L) for a in (q, k, v, g)))
    return _from_chunks(o), S


def _retention(q, k, v, S0):
    L = math.gcd(q.shape[1], CHUNK_D)
    lg = jnp.log(1.0 - 2.0 ** (-5.0 - jnp.arange(H_D, dtype=jnp.float32)))
    idx = jnp.arange(L, dtype=jnp.float32)
    rel = idx[:, None] - idx[None, :]
    Dm = jnp.where(rel >= 0, jnp.exp(lg[:, None, None] * jnp.maximum(rel, 0.0)), 0.0)
    inner = jnp.exp(lg[None, :] * (idx[:, None] + 1.0))
    end = jnp.exp(lg[None, :] * (L - 1.0 - idx[:, None]))
    gL = jnp.exp(lg * L)

    def step(S, inp):
        qc, kc, vc = inp
        att = jnp.einsum('bthd,bshd->bhts', qc, kc) * Dm
        o = jnp.einsum('bthd,bhde->bthe', qc, S) * inner[None, :, :, None] + jnp.einsum('bhts,bshe->bthe', att, vc)
        S_new = gL[:, None, None] * S + jnp.einsum('bshd,bshe->bhde', kc * end[None, :, :, None], vc)
        return S_new, o

    S, o = lax.scan(step, S0, tuple(_to_chunks(a, L) for a in (q, k, v)))
    return _from_chunks(o), S


def _even_mixer(h, s_gdn, s_conv, s_mc, s_mn, s_mm, w_in, w_out, conv_w, a_log, dt_bias,
                gdn_norm_w, gate_b, mlstm_norm_w):
    B, T, _ = h.shape
    f32 = jnp.float32
    proj = (h @ w_in).astype(f32)
    u_a, beta_pre, a_pre, q_b, k_b, v_b, if_pre, o_pre, z = _split(proj, EVEN_SPLIT)
    u_conv, new_conv = _causal_conv(u_a, s_conv.astype(f32), conv_w.astype(f32))
    q_a, k_a, v_a = _split(jax.nn.silu(u_conv), [H_A * DK_A, H_A * DK_A, H_A * DV_A])
    q_a = _l2n(q_a.reshape(B, T, H_A, DK_A)) * (DK_A ** -0.5)
    k_a = _l2n(k_a.reshape(B, T, H_A, DK_A))
    v_a = v_a.reshape(B, T, H_A, DV_A)
    beta = jax.nn.sigmoid(beta_pre)
    g_a = -jnp.exp(a_log.astype(f32)) * jax.nn.softplus(a_pre + dt_bias.astype(f32))
    o_a, new_gdn = _gated_delta(q_a, k_a, v_a, beta, g_a, s_gdn.astype(f32))
    o_a = _rms(o_a, gdn_norm_w).reshape(B, T, H_A * DV_A)
    if_pre = if_pre + gate_b.astype(f32)
    q_b = q_b.reshape(B, T, H_B, DK_B)
    k_b = k_b.reshape(B, T, H_B, DK_B) * (DK_B ** -0.5)
    v_b = v_b.reshape(B, T, H_B, DV_B)
    h_b, new_mc, new_mn, new_mm = _mlstm(q_b, k_b, v_b, if_pre[..., :H_B], if_pre[..., H_B:],
                                         s_mc.astype(f32), s_mn.astype(f32), s_mm.astype(f32))
    h_b = jax.nn.sigmoid(o_pre) * _rms(h_b, mlstm_norm_w).reshape(B, T, H_B * DV_B)
    y = jnp.concatenate([o_a, h_b], axis=-1) * jax.nn.silu(z)
    return y.astype(h.dtype) @ w_out, new_gdn, new_conv, new_mc, new_mn, new_mm


def _odd_mixer(h, pos, s_gla, s_ret, w_in, w_out, gla_w2, gla_b2, gla_norm_w, ret_norm_w):
    B, T, _ = h.shape
    f32 = jnp.float32
    proj = (h @ w_in).astype(f32)
    q_c, k_c, v_c, g_lr, q_d, k_d, v_d, z = _split(proj, ODD_SPLIT)
    g_c = jax.nn.log_sigmoid(g_lr @ gla_w2.astype(f32) + gla_b2.astype(f32)) / GLA_TAU
    o_c, new_gla = _gla(q_c.reshape(B, T, H_C, DK_C) * (DK_C ** -0.5), k_c.reshape(B, T, H_C, DK_C),
                        v_c.reshape(B, T, H_C, DV_C), g_c.reshape(B, T, H_C, DK_C), s_gla.astype(f32))
    o_c = _rms(o_c, gla_norm_w).reshape(B, T, H_C * DV_C)
    q_d = _rotary(q_d.reshape(B, T, H_D, DK_D), pos)
    k_d = _rotary(k_d.reshape(B, T, H_D, DK_D), pos) * (DK_D ** -0.5)
    o_d, new_ret = _retention(q_d, k_d, v_d.reshape(B, T, H_D, DV_D), s_ret.astype(f32))
    o_d = _rms(o_d, ret_norm_w).reshape(B, T, H_D * DV_D)
    y = jnp.concatenate([o_c, o_d], axis=-1) * jax.nn.silu(z)
    return y.astype(h.dtype) @ w_out, new_gla, new_ret


def _trunk(x, c, pos, st_gdn, st_conv, st_mc, st_mn, st_mm, st_gla, st_ret,
           ada_w, ada_b, norm_w, ev_w_in, ev_w_out, gdn_conv_w, gdn_a_log, gdn_dt_bias, gdn_norm_w,
           mlstm_gate_b, mlstm_norm_w, od_w_in, od_w_out, gla_w2, gla_b2, gla_norm_w, ret_norm_w,
           final_norm_w):
    dt = x.dtype
    n_gdn, n_conv, n_mc, n_mn, n_mm, n_gla, n_ret = [], [], [], [], [], [], []
    cs = jax.nn.silu(c)
    for layer in range(DEPTH):
        mod = cs @ ada_w[layer] + ada_b[layer]
        shift, scale, gate = jnp.split(mod[:, None, :], 3, axis=-1)
        h = _rms(x, norm_w[layer]) * (1.0 + scale) + shift
        j = layer // 2
        if layer % 2 == 0:
            y, s1, s2, s3, s4, s5 = _even_mixer(h, st_gdn[j], st_conv[j], st_mc[j], st_mn[j], st_mm[j],
                                                ev_w_in[j], ev_w_out[j], gdn_conv_w[j], gdn_a_log[j],
                                                gdn_dt_bias[j], gdn_norm_w[j], mlstm_gate_b[j], mlstm_norm_w[j])
            n_gdn.append(s1.astype(dt))
            n_conv.append(s2.astype(dt))
            n_mc.append(s3.astype(dt))
            n_mn.append(s4.astype(dt))
            n_mm.append(s5.astype(dt))
        else:
            y, s6, s7 = _odd_mixer(h, pos, st_gla[j], st_ret[j], od_w_in[j], od_w_out[j], gla_w2[j],
                                   gla_b2[j], gla_norm_w[j], ret_norm_w[j])
            n_gla.append(s6.astype(dt))
            n_ret.append(s7.astype(dt))
        x = x + gate * y
    return (_rms(x, final_norm_w), jnp.stack(n_gdn), jnp.stack(n_conv), jnp.stack(n_mc),
            jnp.stack(n_mn), jnp.stack(n_mm), jnp.stack(n_gla), jnp.stack(n_ret))


def setup_inputs(seed: int = 0) -> dict:
    key = jax.random.key(seed)
    k = jax.random.split(key, 32)
    f32 = jnp.float32

    def nrm(i, shape, s=1.0):
        return s * jax.random.normal(k[i], shape, f32)

    dt0 = jnp.exp(jax.random.uniform(k[18], (N_EVEN, H_A), f32, math.log(1e-3), math.log(1e-1)))
    return {
        'x_prompt': nrm(0, (BATCH, SEQ, D_MODEL)),
        'x_sample': nrm(1, (DEC_BATCH, DEC_SEQ, D_MODEL)),
        'c_prompt': nrm(2, (BATCH, D_MODEL)),
        'c_sample': nrm(3, (DEC_BATCH, D_MODEL)),
        'state_gdn': nrm(4, (N_EVEN, DEC_BATCH, H_A, DK_A, DV_A), 0.1),
        'state_gdn_conv': nrm(5, (N_EVEN, DEC_BATCH, CONV_W - 1, CONV_CH_A)),
        'state_mlstm_c': nrm(6, (N_EVEN, DEC_BATCH, H_B, DK_B, DV_B), 0.1),
        'state_mlstm_n': nrm(7, (N_EVEN, DEC_BATCH, H_B, DK_B), 0.3),
        'state_mlstm_m': nrm(8, (N_EVEN, DEC_BATCH, H_B)),
        'state_gla': nrm(9, (N_ODD, DEC_BATCH, H_C, DK_C, DV_C), 0.1),
        'state_ret': nrm(10, (N_ODD, DEC_BATCH, H_D, DK_D, DV_D), 0.1),
        'ada_w': nrm(11, (DEPTH, D_MODEL, 3 * D_MODEL), 0.5 * D_MODEL ** -0.5),
        'ada_b': nrm(12, (DEPTH, 3 * D_MODEL), 0.02),
        'norm_w': 1.0 + nrm(13, (DEPTH, D_MODEL), 0.05),
        'ev_w_in': nrm(14, (N_EVEN, D_MODEL, P_EVEN), D_MODEL ** -0.5),
        'ev_w_out': nrm(15, (N_EVEN, W_MIX, D_MODEL), W_MIX ** -0.5),
        'gdn_conv_w': nrm(16, (N_EVEN, CONV_W, CONV_CH_A), 0.5),
        'gdn_a_log': jnp.log(jax.random.uniform(k[17], (N_EVEN, H_A), f32, 1.0, 16.0)),
        'gdn_dt_bias': dt0 + jnp.log(-jnp.expm1(-dt0)),
        'gdn_norm_w': 1.0 + nrm(19, (N_EVEN, DV_A), 0.05),
        'mlstm_gate_b': jnp.concatenate([nrm(20, (N_EVEN, H_B), 0.1),
                                         jnp.linspace(3.0, 6.0, H_B, dtype=f32)[None, :] + nrm(21, (N_EVEN, H_B), 0.1)], axis=-1),
        'mlstm_norm_w': 1.0 + nrm(22, (N_EVEN, DV_B), 0.05),
        'od_w_in': nrm(23, (N_ODD, D_MODEL, P_ODD), D_MODEL ** -0.5),
        'od_w_out': nrm(24, (N_ODD, W_MIX, D_MODEL), W_MIX ** -0.5),
        'gla_w2': nrm(25, (N_ODD, GLA_RANK, H_C * DK_C), GLA_RANK ** -0.5),
        'gla_b2': nrm(26, (N_ODD, H_C * DK_C), 0.1),
        'gla_norm_w': 1.0 + nrm(27, (N_ODD, DV_C), 0.05),
        'ret_norm_w': 1.0 + nrm(28, (N_ODD, DV_D), 0.05),
        'final_norm_w': 1.0 + nrm(29, (D_MODEL,), 0.05),
    }


def reference(x_prompt, x_sample, c_prompt, c_sample, state_gdn, state_gdn_conv, state_mlstm_c,
              state_mlstm_n, state_mlstm_m, state_gla, state_ret, ada_w, ada_b, norm_w, ev_w_in,
              ev_w_out, gdn_conv_w, gdn_a_log, gdn_dt_bias, gdn_norm_w, mlstm_gate_b, mlstm_norm_w,
              od_w_in, od_w_out, gla_w2, gla_b2, gla_norm_w, ret_norm_w, final_norm_w):
    weights = (ada_w, ada_b, norm_w, ev_w_in, ev_w_out, gdn_conv_w, gdn_a_log, gdn_dt_bias, gdn_norm_w,
               mlstm_gate_b, mlstm_norm_w, od_w_in, od_w_out, gla_w2, gla_b2, gla_norm_w, ret_norm_w,
               final_norm_w)
    f32 = jnp.float32
    Bp, Tp = x_prompt.shape[0], x_prompt.shape[1]
    z_gdn = jnp.zeros((N_EVEN, Bp, H_A, DK_A, DV_A), f32)
    z_conv = jnp.zeros((N_EVEN, Bp, CONV_W - 1, CONV_CH_A), f32)
    z_mc = jnp.zeros((N_EVEN, Bp, H_B, DK_B, DV_B), f32)
    z_mn = jnp.zeros((N_EVEN, Bp, H_B, DK_B), f32)
    z_mm = jnp.zeros((N_EVEN, Bp, H_B), f32)
    z_gla = jnp.zeros((N_ODD, Bp, H_C, DK_C, DV_C), f32)
    z_ret = jnp.zeros((N_ODD, Bp, H_D, DK_D, DV_D), f32)
    pos_p = jnp.arange(Tp)
    pos_s = PAST_LEN + jnp.arange(x_sample.shape[1])
    y_prompt, p_gdn, p_conv, p_mc, p_mn, p_mm, p_gla, p_ret = _trunk(
        x_prompt, c_prompt, pos_p, z_gdn, z_conv, z_mc, z_mn, z_mm, z_gla, z_ret, *weights)
    y_sample, s_gdn, s_conv, s_mc, s_mn, s_mm, s_gla, s_ret = _trunk(
        x_sample, c_sample, pos_s, state_gdn, state_gdn_conv, state_mlstm_c, state_mlstm_n,
        state_mlstm_m, state_gla, state_ret, *weights)
    return (y_prompt, y_sample, p_gdn, p_conv, p_mc, p_mn, p_mm, p_gla, p_ret,
            s_gdn, s_conv, s_mc, s_mn, s_mm, s_gla, s_ret)
```

```python
import numpy as np
import concourse.bass as bass
import concourse.mybir as mybir

F32 = mybir.dt.float32
BF16 = mybir.dt.bfloat16
ALU = mybir.AluOpType
AF = mybir.ActivationFunctionType
AX = mybir.AxisListType

ENGS = ("pe", "dve", "act", "pool", "sp")


class Prog:
    def __init__(self, nc, same_engine_sync=True):
        self.nc = nc
        self.q = {e: [] for e in ENGS}
        self.cnt = {e: 0 for e in ENGS}
        self.seen = {e: {} for e in ENGS}
        self.res = {}
        self.dcnt = {}
        self.sems = {}
        self._ctx = []
        self._semctx = []
        self.same = same_engine_sync
        self.out_events = []
        for e in ENGS:
            self._sem(("eng", e))

    def _enter(self, cm):
        v = cm.__enter__()
        self._ctx.append(cm)
        return v

    def _sem(self, key):
        if key not in self.sems:
            cm = self.nc.semaphore("s_" + "_".join(str(k) for k in key).replace("(", "").replace(")", "").replace(",", "_").replace(" ", "").replace("'", ""))
            self.sems[key] = cm.__enter__()
            self._semctx.append(cm)
        return self.sems[key]

    def sb(self, name, shape, dt=F32):
        return self._enter(self.nc.sbuf_tensor(name, list(shape), dt))

    def ps(self, name, shape, dt=F32):
        return self._enter(self.nc.psum_tensor(name, list(shape), dt))

    def scope_begin(self):
        return len(self._ctx)

    def barrier(self):
        evs = []
        for e in ENGS:
            if self.cnt[e] > 0:
                evs.append((("eng", e), self.cnt[e]))
        for key, c in self.dcnt.items():
            evs.append((("dma", key), 16 * c))
        for e in ENGS:
            waits = []
            for sk, val in evs:
                if sk == ("eng", e):
                    continue
                if self.seen[e].get(sk, 0) >= val:
                    continue
                self.seen[e][sk] = val
                waits.append((sk, val))
            if waits:
                self.q[e].append((waits, None, None, 0))

    def scope_end(self, marker):
        self.barrier()
        while len(self._ctx) > marker:
            self._ctx.pop().__exit__(None, None, None)

    def close(self):
        for cm in reversed(self._ctx):
            cm.__exit__(None, None, None)
        self._ctx = []
        for cm in reversed(self._semctx):
            cm.__exit__(None, None, None)
        self._semctx = []

    @staticmethod
    def _nm(ap):
        return ap.tensor.name

    def _deps(self, eng, reads, writes):
        waits = {}

        def need(ev):
            if ev is None:
                return
            sk, val, src = ev
            if src == eng and (eng == "pe" or not self.same) and sk == ("eng", eng):
                return
            if self.seen[eng].get(sk, 0) >= val:
                return
            if waits.get(sk, 0) < val:
                waits[sk] = val

        for r in reads:
            st = self.res.get(r)
            if st:
                need(st["w"])
        for w in writes:
            st = self.res.get(w)
            if st:
                need(st["w"])
                for ev in st["r"].values():
                    need(ev)
        for sk, val in waits.items():
            self.seen[eng][sk] = val
        return list(waits.items())

    def _commit(self, ev, reads, writes):
        for r in reads:
            st = self.res.setdefault(r, {"w": None, "r": {}})
            st["r"][ev[0]] = ev
        for w in writes:
            self.res[w] = {"w": ev, "r": {}}

    def op(self, eng, fn, reads, writes):
        rn = [r if isinstance(r, str) else self._nm(r) for r in reads if r is not None and not isinstance(r, (int, float))]
        wn = [w if isinstance(w, str) else self._nm(w) for w in writes]
        waits = self._deps(eng, rn, wn)
        self.cnt[eng] += 1
        sk = ("eng", eng)
        ev = (sk, self.cnt[eng], eng)
        self.q[eng].append((waits, fn, sk, 1))
        self._commit(ev, rn, wn)
        return ev

    def dma(self, eng, key, out, in_, final_only=False, **kw):
        key = (self._nm(out) + "__" + self._nm(in_)) if final_only else self._nm(out)
        sk = ("dma", key)
        self._sem(sk)
        rn = [self._nm(in_)]
        wn = [] if final_only else [self._nm(out)]
        waits = self._deps(eng, rn, wn)
        self.dcnt[key] = self.dcnt.get(key, 0) + 1
        ev = (sk, 16 * self.dcnt[key], "dma")
        self.q[eng].append((waits, lambda e: e.dma_start(out=out, in_=in_, **kw), sk, 16))
        self._commit(ev, rn, wn)
        return ev

    def mm(self, out, lhsT, rhs, start=True, stop=True, **kw):
        return self.op("pe", lambda e: e.matmul(out, lhsT, rhs, start=start, stop=stop, **kw), [lhsT, rhs], [out])

    def tr(self, out, in_, ident):
        return self.op("pe", lambda e: e.transpose(out, in_, ident), [in_, ident], [out])

    def tt(self, eng, out, in0, in1, op):
        return self.op(eng, lambda e: e.tensor_tensor(out, in0, in1, op), [in0, in1], [out])

    def ts(self, eng, out, in0, s1, op0, s2=None, op1=None, accum_out=None):
        rd = [in0] + [s for s in (s1, s2) if s is not None and not isinstance(s, (int, float))]
        wr = [out] + ([accum_out] if accum_out is not None else [])
        if op1 is None:
            return self.op(eng, lambda e: e.tensor_scalar(out, in0, s1, None, op0), rd, wr)
        if accum_out is None:
            return self.op(eng, lambda e: e.tensor_scalar(out, in0, s1, s2, op0, op1), rd, wr)
        return self.op(eng, lambda e: e.tensor_scalar(out, in0, s1, s2, op0, op1, accum_out), rd, wr)

    def stt(self, out, in0, scalar, in1, op0, op1, accum_out=None, eng="dve"):
        rd = [in0, in1] + ([scalar] if not isinstance(scalar, (int, float)) else [])
        wr = [out] + ([accum_out] if accum_out is not None else [])
        if accum_out is None:
            return self.op(eng, lambda e: e.scalar_tensor_tensor(out, in0, scalar, in1, op0, op1), rd, wr)
        return self.op(eng, lambda e: e.scalar_tensor_tensor(out, in0, scalar, in1, op0, op1, accum_out), rd, wr)

    def act(self, out, in_, func, bias=None, scale=None, accum_out=None):
        rd = [in_] + [s for s in (bias, scale) if s is not None and not isinstance(s, (int, float))]
        wr = [out] + ([accum_out] if accum_out is not None else [])
        kw = {}
        if bias is not None:
            kw["bias"] = bias
        if scale is not None:
            kw["scale"] = scale
        if accum_out is not None:
            kw["accum_out"] = accum_out
        return self.op("act", lambda e: e.activation(out, in_, func, **kw), rd, wr)

    def copy(self, eng, out, in_):
        if eng == "act":
            return self.op("act", lambda e: e.copy(out, in_), [in_], [out])
        return self.op(eng, lambda e: e.tensor_copy(out, in_), [in_], [out])

    def memset(self, eng, ap, val):
        return self.op(eng, lambda e: e.memset(ap, val), [], [ap])

    def recip(self, out, in_):
        return self.op("dve", lambda e: e.reciprocal(out, in_), [in_], [out])

    def scan(self, out, d0, d1, init, op0, op1):
        rd = [d0, d1] + ([init] if not isinstance(init, (int, float)) else [])
        return self.op("dve", lambda e: e.tensor_tensor_scan(out, d0, d1, init, op0, op1), rd, [out])

    def treduce(self, eng, out, in_, axis, op):
        return self.op(eng, lambda e: e.tensor_reduce(out, in_, axis, op), [in_], [out])

    def emit(self, final_waits_engine="sp"):
        nc = self.nc
        fin = []
        for key, c in self.dcnt.items():
            fin.append((("dma", key), 16 * c))
        for e in ENGS:
            if e != final_waits_engine and self.cnt[e] > 0:
                fin.append((("eng", e), self.cnt[e]))
        sems = self.sems
        q = self.q
        with nc.Block() as block:
            def run(ename, handle, extra=None):
                for waits, fn, sk, inc in q[ename]:
                    for wk, wv in waits:
                        handle.wait_ge(sems[wk], wv)
                    if fn is not None:
                        fn(handle).then_inc(sems[sk], inc)
                if extra:
                    for wk, wv in extra:
                        handle.wait_ge(sems[wk], wv)

            @block.sync
            def _(h):
                run("sp", h, fin if final_waits_engine == "sp" else None)

            @block.tensor
            def _(h):
                run("pe", h)

            @block.vector
            def _(h):
                run("dve", h)

            @block.scalar
            def _(h):
                run("act", h)

            @block.gpsimd
            def _(h):
                run("pool", h)

from concourse.bass_utils import run_bass_kernel_spmd

import math

D = 1024
T = 2048
NS = 16
LCH = 64
SEG = 512
EPS = 1e-6
NEG = -30000.0
import os
TM128 = os.environ.get('KV_TM128', '1') == '1'
EXPGATE = os.environ.get('KV_EXPGATE', '1') == '1'


class Seq:
    def __init__(self, n, L, hT, sample, base=0):
        self.n, self.L, self.hT, self.sample, self.base = n, L, hT, sample, base
        self.nch = n // L
        self.blocks = [(b, min(b + 512, n)) for b in range(0, n, 512)]


def v3(ap, L):
    return ap.rearrange("p (c l) -> p c l", l=L)


def build_nc():
    nc = bass.Bass("TRN2", target_bir_lowering=False)
    P = Prog(nc, same_engine_sync=True)

    def din(name, shape):
        return nc.dram_tensor(name, list(shape), F32, kind="ExternalInput").ap()

    def dout(name, shape):
        return nc.dram_tensor(name, list(shape), F32, kind="ExternalOutput").ap()

    xp = din("xp", [T, D]); xs_d = din("xs", [NS, D]); cc = din("cc", [17, D])
    sg = din("sg", [NS, 8, 128, 128]); sconv = din("sconv", [NS, 3, 3072]); smcn = din("smcn", [NS, 8, 64, 129])
    smm = din("smm", [1, 128]); sgla = din("sgla", [NS, 4, 128, 256]); sret = din("sret", [NS, 4, 128, 256])
    ada_w = din("ada_w", [2, D, 3072]); ada_b = din("ada_b", [2, 3072]); norm_w = din("norm_w", [2, D])
    ev_w_in = din("ev_w_in", [D, 8224]); ev_w_out = din("ev_w_out", [2048, D]); conv_w = din("conv_w", [4, 3072])
    a_log = din("a_log", [1, 8]); dt_bias = din("dt_bias", [1, 8]); gdn_nw = din("gdn_nw", [1, 128]); gate_b = din("gate_b", [1, 16])
    ml_nw = din("ml_nw", [1, 128]); od_w_in = din("od_w_in", [D, 6160]); od_w_out = din("od_w_out", [2048, D])
    gla_w2 = din("gla_w2", [16, 512]); gla_b2 = din("gla_b2", [1, 512]); gla_nw = din("gla_nw", [1, 256]); ret_nw = din("ret_nw", [1, 256])
    fin_nw = din("fin_nw", [1, D])
    c_ident = din("c_ident", [128, 128]); c_tri = din("c_tri", [64, 64]); c_negU = din("c_negU", [64, 64]); c_negLs = din("c_negLs", [64, 64])
    c_cos = din("c_cos", [128, T]); c_sin = din("c_sin", [128, T]); c_cosS = din("c_cosS", [128, NS]); c_sinS = din("c_sinS", [128, NS])
    c_sel = din("c_sel", [17, 128])

    yp = dout("yp", [T, D]); ys = dout("ys", [NS, D])
    p_gdn = dout("p_gdn", [8, 128, 128]); p_conv = dout("p_conv", [3, 3072]); p_mcn = dout("p_mcn", [8, 64, 129])
    p_mm = dout("p_mm", [1, 8]); p_gla = dout("p_gla", [4, 128, 256]); p_ret = dout("p_ret", [4, 128, 256])
    s_gdn = dout("s_gdn", [NS, 8, 128, 128]); s_conv = dout("s_conv", [NS, 3, 3072]); s_mcn = dout("s_mcn", [NS, 8, 64, 129])
    s_mm = dout("s_mm", [1, 128]); s_gla = dout("s_gla", [NS, 4, 128, 256]); s_ret = dout("s_ret", [NS, 4, 128, 256])
    x1 = nc.dram_tensor("x1scr", [T, D], F32).ap()
    scrF = [nc.dram_tensor("scrF%d" % i, [16, 512], F32).ap() for i in range(4)]
    scrB = [nc.dram_tensor("scrB%d" % i, [16, 512], BF16).ap() for i in range(4)]
    scr_i = [0]

    ident = P.sb("ident", [128, 128]); identb = P.sb("identb", [128, 128], BF16); ones32 = P.sb("ones32", [128, 128]); onesb = P.sb("onesb", [128, 128], BF16)
    tri = P.sb("tri", [64, 64]); negU = P.sb("negU", [64, 64]); negLs = P.sb("negLs", [64, 64]); sel = P.sb("sel", [17, 128])
    hT = P.sb("hT", [128, 8, T], BF16); hsT = P.sb("hsT", [128, 8, NS], BF16)
    yT = P.sb("yT", [128, 16, T], BF16); ysT = P.sb("ysT", [128, 16, NS], BF16)
    xs = P.sb("xs_sb", [NS, D]); grow = P.sb("grow", [17, D]); modh = [None]
    wpT = P.sb("wpT", [128, 8]); shT = P.sb("shT", [128, 8]); nwT = P.sb("nwT", [128, 8])
    B = [P.ps("B%d" % i, [128, 512]) for i in range(7)]
    ptb = P.ps("ptb", [128, 1024], BF16)
    pj_i = [0]

    def pj():
        pj_i[0] ^= 1
        return B[pj_i[0]]

    psb = [B[2], B[3]]
    pm = B[4]; po = B[5]; pst = B[6]
    pk = pm[:, 128:256]; pu = pm[:, 256:384]; pa = pm[:, 384:448]

    for t_, d_ in ((ident, c_ident), (tri, c_tri), (negU, c_negU), (negLs, c_negLs), (sel, c_sel), (xs, xs_d)):
        P.dma("sp", "c0", t_[:], d_)
    P.copy("dve", identb[:], ident[:])
    P.memset("pool", ones32[:], 1.0)
    P.memset("pool", onesb[:], 1.0)

    def rms_scale(out_col, ssq_col, n):
        P.ts("dve", out_col, ssq_col, 1.0 / n, ALU.mult, EPS, ALU.add)
        P.act(out_col, out_col, AF.Ln)
        P.act(out_col, out_col, AF.Exp, scale=-0.5)

    def wload(dst, src2d):
        P.dma("pool", "w_" + dst.tensor.name, dst, src2d.rearrange("(c p) n -> p c n", p=128))

    def mod_phase(layer, sc):
        mod = P.sb("mod%d" % layer, [17, 3072]); modh[0] = mod
        cs_ = P.sb("cs%d" % layer, [17, D]); csT = P.sb("csT%d" % layer, [128, 8, 17], BF16)
        wa = P.sb("wa%d" % layer, [128, 8, 512], BF16); ab = P.sb("ab%d" % layer, [17, 512]); modT = P.sb("modT%d" % layer, [128, 16, 17])
        P.dma("sp", "cc", cs_[:], cc)
        P.act(cs_[:], cs_[:], AF.Silu)
        for j in range(8):
            P.tr(B[0][:, 0:17], cs_[0:17, j * 128:(j + 1) * 128], ident[0:17, 0:17])
            P.copy("dve", csT[:, j, :], B[0][:, 0:17])
        for blk in range(6):
            wload(wa[:], ada_w[layer][:, blk * 512:(blk + 1) * 512])
            P.dma("sp", "ab", ab[:], ada_b[layer:layer + 1, blk * 512:(blk + 1) * 512].broadcast_to([17, 512]))
            ps = pj()
            for c in range(8):
                P.mm(ps[0:17, :], csT[:, c, :], wa[:, c, :], start=(c == 0), stop=(c == 7))
            P.tt("dve", mod[0:17, blk * 512:(blk + 1) * 512], ps[0:17, :], ab[:], ALU.add)
        for j in range(16):
            P.tr(B[0][:, 0:17], mod[0:17, j * 128:(j + 1) * 128], ident[0:17, 0:17])
            P.copy("dve", modT[:, j, :], B[0][:, 0:17])
        P.dma("sp", "nwT", nwT[:], norm_w[layer].rearrange("(c p) -> p c", p=128), allow_slow_non_contiguous=True)
        P.stt(wpT[:], modT[:, 8:16, 16], 1.0, nwT[:], ALU.add, ALU.mult)
        P.copy("dve", shT[:], modT[:, 0:8, 16])
        P.copy("dve", grow[:], mod[0:17, 2048:3072])

    def phase0(layer, src):
        xt = [P.sb("p0x%d_%d" % (layer, i), [128, D]) for i in range(2)]
        junk = P.sb("p0j%d" % layer, [128, D], BF16); col = P.sb("p0c%d" % layer, [128, 2]); nwb = P.sb("p0n%d" % layer, [NS, D])
        hs = P.sb("p0h%d" % layer, [NS, D])
        for i in range(16):
            x_ = xt[i % 2]
            P.dma("sp", ("p0", i % 2), x_[:], src[i * 128:(i + 1) * 128, :])
            P.act(junk[:], x_[:], AF.Square, accum_out=col[:, 0:1])
            rms_scale(col[:, 1:2], col[:, 0:1], D)
            P.ts("dve", x_[:], x_[:], col[:, 1:2], ALU.mult)
            for j in range(8):
                ps = pj()
                P.tr(ps[:, 0:128], x_[:, j * 128:(j + 1) * 128], ident[:])
                if j % 2:
                    P.ts("dve", hT[:, j, i * 128:(i + 1) * 128], ps[:, 0:128], wpT[:, j:j + 1], ALU.mult, shT[:, j:j + 1], ALU.add)
                else:
                    P.act(hT[:, j, i * 128:(i + 1) * 128], ps[:, 0:128], AF.Identity, bias=shT[:, j:j + 1], scale=wpT[:, j:j + 1])
        P.dma("sp", "p0n", nwb[:], norm_w[layer:layer + 1, :].broadcast_to([NS, D]))
        P.act(junk[0:NS, :], xs[:], AF.Square, accum_out=col[0:NS, 0:1])
        rms_scale(col[0:NS, 1:2], col[0:NS, 0:1], D)
        P.ts("dve", hs[:], xs[:], col[0:NS, 1:2], ALU.mult)
        P.tt("dve", hs[:], hs[:], nwb[:], ALU.mult)
        mod = modh[0]
        P.ts("dve", nwb[:], mod[0:NS, 1024:2048], 1.0, ALU.add)
        P.tt("dve", hs[:], hs[:], nwb[:], ALU.mult)
        P.tt("dve", hs[:], hs[:], mod[0:NS, 0:1024], ALU.add)
        for j in range(8):
            ps = pj()
            P.tr(ps[:, 0:NS], hs[0:NS, j * 128:(j + 1) * 128], ident[0:NS, 0:NS])
            P.copy("dve", hsT[:, j, :], ps[:, 0:NS])

    def phaseB(layer, w_out, src, last):
        wo = P.sb("wo%d" % layer, [128, 16, D], BF16)
        xt = [P.sb("pbx%d_%d" % (layer, i), [128, D]) for i in range(2)]
        tmp = P.sb("pbt%d" % layer, [128, 512]); junk = P.sb("pbj%d" % layer, [128, D], BF16); col = P.sb("pbc%d" % layer, [128, 2])
        fnb = P.sb("pbf%d" % layer, [128, D]); gate_bc = P.sb("gate_bc%d" % layer, [128, D])
        for half in range(2):
            ps = pj()
            P.mm(ps[:, :], sel[0:17, :], grow[0:17, half * 512:(half + 1) * 512])
            P.copy("act", gate_bc[:, half * 512:(half + 1) * 512], ps[:, :])
        for q4 in range(4):
            P.dma("pool", ("wo", q4), wo[:, q4 * 4:(q4 + 1) * 4, :], w_out[q4 * 512:(q4 + 1) * 512, :].rearrange("(c p) n -> p c n", p=128))
        if last:
            P.dma("sp", "fnb", fnb[:], fin_nw[0:1, :].broadcast_to([128, D]))
        for i in range(16):
            x_ = xt[i % 2]
            P.dma("sp", ("pb", i % 2), x_[:], src[i * 128:(i + 1) * 128, :])
            for half in range(2):
                ps = pj()
                for f in range(16):
                    P.mm(ps[:, :], yT[:, f, i * 128:(i + 1) * 128], wo[:, f, half * 512:(half + 1) * 512], start=(f == 0), stop=(f == 15))
                P.tt("dve", tmp[:], ps[:, :], gate_bc[:, half * 512:(half + 1) * 512], ALU.mult)
                P.tt("dve", x_[:, half * 512:(half + 1) * 512], x_[:, half * 512:(half + 1) * 512], tmp[:], ALU.add)
            if not last:
                P.dma("sp", ("pbo", i % 2), x1[i * 128:(i + 1) * 128, :], x_[:])
            else:
                P.act(junk[:], x_[:], AF.Square, accum_out=col[:, 0:1])
                rms_scale(col[:, 1:2], col[:, 0:1], D)
                P.stt(x_[:], x_[:], col[:, 1:2], fnb[:], ALU.mult, ALU.mult)
                P.dma("sp", ("pbo", i % 2), yp[i * 128:(i + 1) * 128, :], x_[:], final_only=True)
        for half in range(2):
            ps = pj()
            for f in range(16):
                P.mm(ps[0:NS, :], ysT[:, f, :], wo[:, f, half * 512:(half + 1) * 512], start=(f == 0), stop=(f == 15))
            P.tt("dve", tmp[0:NS, :], ps[0:NS, :], grow[0:NS, half * 512:(half + 1) * 512], ALU.mult)
            P.tt("dve", xs[:, half * 512:(half + 1) * 512], xs[:, half * 512:(half + 1) * 512], tmp[0:NS, :], ALU.add)
        if last:
            P.act(junk[0:NS, :], xs[:], AF.Square, accum_out=col[0:NS, 0:1])
            rms_scale(col[0:NS, 1:2], col[0:NS, 0:1], D)
            P.stt(tmp2 := xt[0][0:NS, :], xs[:], col[0:NS, 1:2], fnb[0:NS, :], ALU.mult, ALU.mult)
            P.dma("sp", "yso", ys, tmp2, final_only=True)

    def mixers(layer):
        SEGL = SEG if layer == 0 else 1024
        W_in = ev_w_in if layer == 0 else od_w_in
        DVM = 128 if layer == 0 else 256
        wh = P.sb("wh%d" % layer, [128, 8, 512 if layer == 0 else 768], BF16); wsm = P.sb("wsm%d" % layer, [128, 8, 32], BF16)
        qT = P.sb("qT%d" % layer, [128, SEGL], BF16); qgT = P.sb("qgT%d" % layer, [128, SEGL], BF16); kT = P.sb("kT%d" % layer, [128, SEGL], BF16)
        khT = P.sb("khT%d" % layer, [128, SEGL if layer == 1 else 2], BF16); kT32 = P.sb("kT32%d" % layer, [128, SEGL if layer == 0 else 2])
        ktok = P.sb("ktok%d" % layer, [64, 16, 128], BF16); vtok = P.sb("vtok%d" % layer, [64, 16, DVM + 2], BF16)
        szw = P.sb("szw%d" % layer, [64, 16, DVM], BF16); bv = P.sb("bv%d" % layer, [64, 16, 128 if layer == 0 else 2], BF16)
        G0 = 64 if layer == 0 else 2
        decT = P.sb("decT%d" % layer, [64, 16, 64]); decS = P.sb("decS%d" % layer, [64, 8, G0]); TT = P.sb("TT%d" % layer, [64, 8, G0])
        Nn = P.sb("Nn%d" % layer, [64, 8, G0]); Mm = P.sb("Mm%d" % layer, [64, 8, G0]); Xx = P.sb("Xx%d" % layer, [64, 8, G0])
        tmpA = P.sb("tmpA%d" % layer, [128, SEGL]); tmpB = P.sb("tmpB%d" % layer, [128, SEGL]); tmpC = P.sb("tmpC%d" % layer, [128, SEGL])
        gamb = P.sb("gamb%d" % layer, [128, SEGL]); ubuf = P.sb("ubuf%d" % layer, [128, (SEGL + 8) if layer == 0 else 2])
        bht = P.sb("bht%d" % layer, [64, 16]); nbg = P.sb("nbg%d" % layer, [64, 16]); gtok = P.sb("gtok%d" % layer, [64, 16])
        Sf = [P.sb("Sf%d_%d" % (layer, i), [128, DVM + 2]) for i in range(4)]; Sb2 = [P.sb("Sb%d_%d" % (layer, i), [128, DVM + 2], BF16) for i in range(2)]
        Sb = Sb2[0]; So4 = [P.sb("So%d_%d" % (layer, i), [64, 130 if layer == 0 else 2]) for i in range(4)]; So = So4[0]
        PT2 = [P.sb("PT%d_%d" % (layer, i), [64, 64], BF16) for i in range(2)]; Rp = P.sb("Rp%d" % layer, [64, 128])
        ot = P.sb("ot%d" % layer, [64, 8, DVM], BF16); colg = P.sb("colg%d" % layer, [64, 32])
        patt2 = [B[0], B[1]]; po2 = [B[5], B[3]]; pst2 = [B[6], B[2]]; pk2 = [B[4][:, 128:256], B[1][:, 128:256]]
        junk = P.sb("mj%d" % layer, [64, 256], BF16); tz = P.sb("tz%d" % layer, [64, 256]); tz2 = P.sb("tz2_%d" % layer, [64, 128])
        t16f = P.sb("t16f%d" % layer, [16, 32]); t16b = P.sb("t16b%d" % layer, [16, 512], BF16)
        nwbc = P.sb("nwbc%d" % layer, [128, 256]); cst = P.sb("cst%d" % layer, [128, SEGL if layer == 1 else 2]); snt = P.sb("snt%d" % layer, [128, SEGL if layer == 1 else 2])
        G1 = 32 if layer == 0 else 1
        gates = P.sb("gates%d" % layer, [64, G1, 16]); gA = P.sb("gA%d" % layer, [64, G1, 8]); gB = P.sb("gB%d" % layer, [64, G1, 8]); gC = P.sb("gC%d" % layer, [64, G1, 8])
        gates_s = P.sb("gates_s%d" % layer, [1, 16, 16]); sA = P.sb("sA%d" % layer, [1, 16, 8]); sB = P.sb("sB%d" % layer, [1, 16, 8]); sC = P.sb("sC%d" % layer, [1, 16, 8])
        sD = P.sb("sD%d" % layer, [1, 16, 8]); sE = P.sb("sE%d" % layer, [1, 16, 8]); sm0 = P.sb("sm0%d" % layer, [1, 16, 8])
        small = P.sb("small%d" % layer, [128, 64]); fsc = P.sb("fsc%d" % layer, [64, 136])

        full = Seq(T, LCH, hT[:, :, :], False)
        NCS = SEGL // LCH
        segs = [Seq(SEGL, LCH, hT[:, :, s * SEGL:(s + 1) * SEGL], False, base=s * SEGL) for s in range(T // SEGL)]
        samp = Seq(NS, 1, hsT[:, :, :], True)

        def proj_fm(seq, wt, c0, M, evac):
            for (b0, b1) in seq.blocks:
                ps = pj()
                for c in range(8):
                    P.mm(ps[0:M, 0:b1 - b0], wt[:, c, c0:c0 + M], seq.hT[:, c, b0:b1], start=(c == 0), stop=(c == 7))
                evac(ps[0:M, 0:b1 - b0], b0, b1)

        def tm(seq, wt, c0, ncols, outs):
            if not seq.sample and not TM128:
                for ch in range(seq.nch):
                    ps = pj()
                    for c in range(8):
                        P.mm(ps[0:64, 0:ncols], seq.hT[:, c, ch * 64:(ch + 1) * 64], wt[:, c, c0:c0 + ncols], start=(c == 0), stop=(c == 7))
                    for dst, lo, hi, w, fn in outs:
                        fn(dst[0:64, ch, 0:w], ps[0:64, lo:hi])
            elif not seq.sample:
                for cp in range(seq.nch // 2):
                    ps = pj()
                    for c in range(8):
                        P.mm(ps[0:128, 0:ncols], seq.hT[:, c, cp * 128:(cp + 1) * 128], wt[:, c, c0:c0 + ncols], start=(c == 0), stop=(c == 7))
                    for half in range(2):
                        for dst, lo, hi, w, fn in outs:
                            fn(dst[0:64, 2 * cp + half, 0:w], ps[half * 64:(half + 1) * 64, lo:hi])
            else:
                ps = pj()
                for c in range(8):
                    P.mm(ps[0:NS, 0:ncols], seq.hT[:, c, 0:NS], wt[:, c, c0:c0 + ncols], start=(c == 0), stop=(c == 7))
                for dst, lo, hi, w, fn in outs:
                    isb = dst.dtype == BF16
                    t16 = t16b if isb else t16f
                    fn(t16[0:NS, 0:w], ps[0:NS, lo:hi])
                    k = scr_i[0] % 4; scr_i[0] += 1
                    scr = (scrB if isb else scrF)[k]
                    P.dma("sp", ("bo", isb, k), scr[0:NS, 0:w], t16[0:NS, 0:w])
                    P.dma("sp", ("bi", isb, k), dst[0:1, 0:NS, 0:w], scr[0:NS, 0:w].unsqueeze(0))

        def fcopy(eng="act"):
            return lambda o, i: P.copy(eng, o, i)

        def fgate(nw_lo, w, sig=False):
            def fn(o, i):
                r = i.shape[0]
                bp = i.base_partition()
                if sig and not EXPGATE:
                    P.act(tz[0:r, 0:w], i[:, 0:w], AF.Sigmoid)
                    P.act(tz[0:r, 128:128 + w], i[:, w:2 * w], AF.Silu)
                    P.tt("dve", tz[0:r, 0:w], tz[0:r, 0:w], tz[0:r, 128:128 + w], ALU.mult)
                    P.tt("dve", o, tz[0:r, 0:w], nwbc[0:r, nw_lo:nw_lo + w], ALU.mult)
                elif sig:
                    P.copy("act", tz2[0:r, 0:w], i[:, w:2 * w])
                    P.act(tz[0:r, 0:2 * w], i[:, 0:2 * w], AF.Exp, scale=-1.0)
                    P.tt("dve", tz2[0:r, 0:w], tz2[0:r, 0:w], nwbc[0:r, nw_lo:nw_lo + w], ALU.mult)
                    P.ts("dve", tz[0:r, 0:2 * w], tz[0:r, 0:2 * w], 1.0, ALU.add)
                    P.tt("dve", tz[0:r, 0:w], tz[0:r, 0:w], tz[0:r, w:2 * w], ALU.mult)
                    P.recip(tz[0:r, 0:w], tz[0:r, 0:w])
                    P.tt("dve", o, tz[0:r, 0:w], tz2[0:r, 0:w], ALU.mult)
                else:
                    P.act(tz[0:r, 0:w], i, AF.Silu)
                    P.tt("dve", o, tz[0:r, 0:w], nwbc[0:r, nw_lo:nw_lo + w], ALU.mult)
            return fn

        def decay(seq, g_tok, strict=False):
            L, nch, n = seq.L, seq.nch, seq.n
            P.tt("dve", v3(tmpB[0:L, 0:n], L), tri[0:L, 0:L].unsqueeze(1).broadcast_to([L, nch, L]),
                 g_tok.unsqueeze(2).broadcast_to([L, nch, L]), ALU.mult)
            for bi, (b0, b1) in enumerate(seq.blocks):
                P.mm(psb[bi][:, 0:b1 - b0], ones32[0:L, :], tmpB[0:L, b0:b1])
            P.mm(pa[0:L, 0:nch], tri[0:L, 0:L], g_tok)
            P.copy("act", bht[0:L, 0:nch], pa[0:L, 0:nch])
            for bi, (b0, b1) in enumerate(seq.blocks):
                c0, c1 = b0 // L, b1 // L
                pv = v3(psb[bi][0:L, 0:b1 - b0], L)
                bb = bht[0:L, c0:c1].unsqueeze(2).broadcast_to([L, c1 - c0, L])
                d = v3(tmpC[0:L, b0:b1], L)
                P.tt("dve", d, pv, bb, ALU.subtract)
                P.tt("dve", d, d, negU[0:L, 0:L].unsqueeze(1).broadcast_to([L, c1 - c0, L]), ALU.min)
                P.act(decT[0:L, c0:c1, 0:L], d, AF.Exp)
                if strict:
                    P.stt(d, pv, -1.0, bb, ALU.mult, ALU.add)
                    P.tt("dve", d, d, negLs[0:L, 0:L].unsqueeze(1).broadcast_to([L, c1 - c0, L]), ALU.min)
                    P.act(decS[0:L, c0:c1, 0:L], d, AF.Exp)
                P.act(gamb[:, b0:b1], psb[bi][:, 0:b1 - b0], AF.Exp)

        NB = 4

        def make_ktok(seq, dk, src, scale_fn):
            L = seq.L
            G = 512 // dk
            for g0 in range(0, seq.nch, G):
                g1 = min(seq.nch, g0 + G)
                for c in range(g0, g1):
                    P.tr(ptb[0:L, (c - g0) * dk:(c - g0 + 1) * dk], src[0:dk, c * L:(c + 1) * L], identb[0:dk, 0:dk])
                pv = ptb[0:L, 0:(g1 - g0) * dk].rearrange("p (c d) -> p c d", d=dk)
                if scale_fn is None:
                    P.copy("act", ktok[0:L, g0:g1, 0:dk], pv)
                else:
                    P.tt("dve", ktok[0:L, g0:g1, 0:dk], pv, scale_fn(g0, g1).unsqueeze(2).broadcast_to([L, g1 - g0, dk]), ALU.mult)

        GP = 8
        pending = []

        def flush_pending():
            while pending:
                fn_, a_, g_ = pending.pop(0)
                fn_(a_, g_)

        def post_rows(seq, c0, G, dv, f0, scale_ap):
            L = seq.L
            ov = ot[0:L, 0:G, 0:dv]
            P.tt("dve", ov, ov, scale_ap.unsqueeze(2).broadcast_to([L, G, dv]), ALU.mult)
            P.tt("dve", ov, ov, szw[0:L, c0:c0 + G, 0:dv], ALU.mult)
            nj = dv // 128
            LL = max(L, 2)
            for j in range(nj):
                for g in range(G):
                    o_ = (j * G + g) * LL
                    P.tr(ptb[:, o_:o_ + L], ot[0:L, g, j * 128:(j + 1) * 128], identb[0:L, 0:L])
            src = ptb[:, 0:nj * G * LL].rearrange("p (j g l) -> p j g l", j=nj, g=G)[:, :, :, 0:L]
            if seq.sample:
                P.copy("act", ysT[:, f0:f0 + nj, c0:c0 + G].unsqueeze(3), src)
            else:
                P.copy("act", yT[:, f0:f0 + nj, seq.base + c0 * L:seq.base + (c0 + G) * L].rearrange("p j (g l) -> p j g l", l=L), src)

        def post_std(seq, dv, f0):
            def fn(c0, G):
                L = seq.L
                rms_scale(colg[0:L, 8:8 + G], colg[0:L, 0:G], dv)
                post_rows(seq, c0, G, dv, f0, colg[0:L, 8:8 + G])
            return fn

        def chunk_loop(seq, dk, dv, dvp, wt_fn, eL_fn, post_fn, S_, gdn=False, st_in=None, st_out=None, den=False):
            L, n = seq.L, seq.nch

            def stA(c):
                cb = slice(c * L, (c + 1) * L)
                P.mm(patt2[c % 2][0:L, 0:L], kT[0:dk, cb], qT[0:dk, cb])
                P.tt("dve", PT2[c % 2][0:L, 0:L], patt2[c % 2][0:L, 0:L], wt_fn(c), ALU.mult)

            def stB(c):
                cb = slice(c * L, (c + 1) * L)
                if seq.sample:
                    S = Sf[c % NB]; Sb_ = Sb2[c % 2]
                    P.copy("act", Sb_[0:dk, 0:dvp], S[0:dk, 0:dvp])
                else:
                    S = S_; Sb_ = Sb2[0]
                pst_ = pst2[c % 2] if seq.sample else pst
                if gdn:
                    pk_ = pk2[c % 2] if seq.sample else pk
                    P.mm(pk_[0:L, 0:dv], kT[0:dk, cb], Sb_[0:dk, 0:dv])
                    if L == 1:
                        P.stt(vtok[0:L, c, 0:dv], pk_[0:L, 0:dv], nbg[0:L, c:c + 1], bv[0:L, c, 0:dv], ALU.mult, ALU.add)
                    else:
                        P.stt(Rp[0:L, 0:dv], pk_[0:L, 0:dv], nbg[0:L, c:c + 1], bv[0:L, c, 0:dv], ALU.mult, ALU.add)
                        P.mm(pu[0:L, 0:dv], TT[0:L, c, 0:L], Rp[0:L, 0:dv])
                        P.copy("act", vtok[0:L, c, 0:dv], pu[0:L, 0:dv])
                po_ = po2[c % 2]
                P.mm(po_[0:L, 0:dvp], qgT[0:dk, cb], Sb_[0:dk, 0:dvp], start=True, stop=False)
                P.mm(po_[0:L, 0:dvp], PT2[c % 2][0:L, 0:L], vtok[0:L, c, 0:dvp], start=False, stop=True)
                P.mm(pst_[0:dk, 0:dvp], ktok[0:L, c, 0:dk], vtok[0:L, c, 0:dvp])
                P.stt(S[0:dk, 0:dvp], S[0:dk, 0:dvp], eL_fn(c), pst_[0:dk, 0:dvp], ALU.mult, ALU.add)
                if not seq.sample:
                    P.copy("act", Sb_[0:dk, 0:dvp], S[0:dk, 0:dvp])
                else:
                    st_out(c, S)
                g = c % GP
                P.copy("act", ot[0:L, g, 0:dv], po_[0:L, 0:dv])
                P.act(junk[0:L, 0:dv], po_[0:L, 0:dv], AF.Square, accum_out=colg[0:L, g:g + 1])
                if den:
                    P.act(colg[0:L, 16 + g:17 + g], po_[0:L, dv:dv + 1], AF.Abs)

            if seq.sample:
                for c in range(min(NB - 1, n)):
                    st_in(c, Sf[c % NB])
            stA(0)
            for c in range(n):
                if seq.sample and c + NB - 1 < n:
                    st_in(c + NB - 1, Sf[(c + NB - 1) % NB])
                if c + 1 < n:
                    stA(c + 1)
                stB(c)
                if c % GP == GP - 1:
                    if c == n - 1:
                        pending.append((post_fn, c - GP + 1, GP))
                    else:
                        post_fn(c - GP + 1, GP)

        def eL_gam(seq, dk):
            return lambda c: gamb[0:dk, c * seq.L + seq.L - 1:c * seq.L + seq.L]

        def kdec(seq):
            return lambda g0, g1: decT[0:seq.L, g0:g1, seq.L - 1]

        def load_nw(src, w):
            P.dma("sp", "nwbc", nwbc[:, 0:w], src[0:1, 0:w].broadcast_to([128, w]))

        def rot(ps, scale, b0, b1, out):
            n = b1 - b0
            P.stt(tmpA[:, 0:n], ps, scale, cst[:, b0:b1], ALU.mult, ALU.mult)
            P.stt(tmpB[0:64, 0:n], ps[64:128, :], scale, snt[64:128, b0:b1], ALU.mult, ALU.mult)
            P.stt(tmpB[64:128, 0:n], ps[0:64, :], scale, snt[0:64, b0:b1], ALU.mult, ALU.mult)
            P.tt("dve", out, tmpA[:, 0:n], tmpB[:, 0:n], ALU.add)

        if layer == 1:
            w2 = P.sb("glaw2", [16, 512]); nb2T = P.sb("nb2T", [128, 4]); glrT = P.sb("glrT", [16, SEGL]); eblc = P.sb("eblc", [128, 16])
            gcon = P.sb("gcon", [64, 16])
            P.dma("sp", "lw", w2[:], gla_w2)
            P.dma("sp", "lw", nb2T[:], gla_b2[0].rearrange("(h p) -> p h", p=128), allow_slow_non_contiguous=True)
            P.ts("dve", nb2T[:], nb2T[:], -1.0, ALU.mult)
            wload(wsm[:, :, 0:16], W_in[:, 2048:2064])
            load_nw(gla_nw, 256)
            for h in range(4):
                wload(wh[:, :, 0:128], W_in[:, h * 128:(h + 1) * 128]); wload(wh[:, :, 128:256], W_in[:, 512 + h * 128:512 + (h + 1) * 128])
                wload(wh[:, :, 256:512], W_in[:, 1024 + h * 256:1024 + (h + 1) * 256]); wload(wh[:, :, 512:768], W_in[:, 4112 + h * 256:4112 + (h + 1) * 256])
                S_ = Sf[0]
                P.memset("pool", S_[:, 0:256], 0.0); P.memset("pool", Sb[:, 0:256], 0.0)
                for seq in segs + [samp]:
                    L, n = seq.L, seq.n
                    proj_fm(seq, wsm, 0, 16, lambda ps, b0, b1: P.copy("act", glrT[0:16, b0:b1], ps))
                    for bi, (b0, b1) in enumerate(seq.blocks):
                        ps = psb[bi]
                        P.mm(ps[:, 0:b1 - b0], w2[0:16, h * 128:(h + 1) * 128], glrT[0:16, b0:b1])
                        P.act(tmpA[:, b0:b1], ps[:, 0:b1 - b0], AF.Exp, bias=nb2T[:, h:h + 1], scale=-1.0)
                        P.ts("dve", tmpA[:, b0:b1], tmpA[:, b0:b1], 1.0, ALU.add)
                        P.act(tmpA[:, b0:b1], tmpA[:, b0:b1], AF.Ln)
                        P.ts("dve", tmpC[:, b0:b1], tmpA[:, b0:b1], -1.0 / 16.0, ALU.mult)
                    if L > 1:
                        for c in range(seq.nch):
                            P.scan(tmpB[:, c * L:(c + 1) * L], ones32[:, 0:L], tmpC[:, c * L:(c + 1) * L], 0.0, ALU.mult, ALU.add)
                    else:
                        P.copy("dve", tmpB[:, 0:n], tmpC[:, 0:n])
                    P.act(gamb[:, 0:n], tmpB[:, 0:n], AF.Exp)
                    P.act(tmpC[:, 0:n], tmpB[:, 0:n], AF.Exp, scale=-1.0)
                    P.copy("dve", eblc[:, 0:seq.nch], v3(gamb[:, 0:n], L)[:, :, L - 1])
                    proj_fm(seq, wh, 0, 128, lambda ps, b0, b1: P.stt(qT[:, b0:b1], ps, 128 ** -0.5, gamb[:, b0:b1], ALU.mult, ALU.mult))
                    proj_fm(seq, wh, 128, 128, lambda ps, b0, b1: P.tt("dve", kT[:, b0:b1], ps, tmpC[:, b0:b1], ALU.mult))
                    flush_pending()
                    P.tt("dve", v3(khT[:, 0:n], L), v3(kT[:, 0:n], L), eblc[:, 0:seq.nch].unsqueeze(2).broadcast_to([128, seq.nch, L]), ALU.mult)
                    make_ktok(seq, 128, khT, None)
                    tm(seq, wh, 256, 512, [(vtok, 0, 256, 256, fcopy()), (szw, 256, 512, 256, fgate(0, 256))])
                    _qg = qgT
                    chunk_loop_q = qT
                    P.copy("pool", qgT[:, 0:n], qT[:, 0:n])
                    chunk_loop(seq, 128, 256, 256, lambda c, L=L: tri[0:L, 0:L], lambda c: eblc[:, c:c + 1], post_std(seq, 256, h * 2), S_,
                               st_in=lambda c, S, h=h: P.dma("sp", ("sti", c % 2), S[:, 0:256], sgla[c, h]),
                               st_out=lambda c, S, h=h: P.dma("pool", ("sto", c % 2), s_gla[c, h], S[:, 0:256], final_only=True))
                    if seq is segs[-1]:
                        P.dma("sp", "pst", p_gla[h], S_[:, 0:256], final_only=True)
            load_nw(ret_nw, 256)
            for h in range(4):
                o = 2064
                wload(wh[:, :, 0:128], W_in[:, o + h * 128:o + (h + 1) * 128]); wload(wh[:, :, 128:256], W_in[:, 2576 + h * 128:2576 + (h + 1) * 128])
                wload(wh[:, :, 256:512], W_in[:, 3088 + h * 256:3088 + (h + 1) * 256]); wload(wh[:, :, 512:768], W_in[:, 5136 + h * 256:5136 + (h + 1) * 256])
                S_ = Sf[0]
                P.memset("pool", S_[:, 0:256], 0.0); P.memset("pool", Sb[:, 0:256], 0.0)
                P.memset("pool", gcon[:], math.log(1.0 - 2.0 ** (-5.0 - h)))
                for seq in segs + [samp]:
                    L, n = seq.L, seq.n
                    if seq.sample:
                        P.dma("sp", "cs", cst[:, 0:NS], c_cosS); P.dma("sp", "cs", snt[:, 0:NS], c_sinS)
                    else:
                        P.dma("sp", "cs", cst[:, :], c_cos[:, seq.base:seq.base + SEGL]); P.dma("sp", "cs", snt[:, :], c_sin[:, seq.base:seq.base + SEGL])
                    if seq.sample or seq is segs[0]:
                        decay(seq, gcon[0:L, 0:seq.nch])
                    proj_fm(seq, wh, 0, 128, lambda ps, b0, b1: rot(ps, 1.0, b0, b1, qT[:, b0:b1]))
                    proj_fm(seq, wh, 128, 128, lambda ps, b0, b1: rot(ps, 128 ** -0.5, b0, b1, kT[:, b0:b1]))
                    flush_pending()
                    P.tt("dve", qgT[:, 0:n], qT[:, 0:n], gamb[:, 0:n], ALU.mult)
                    make_ktok(seq, 128, kT, kdec(seq))
                    tm(seq, wh, 256, 512, [(vtok, 0, 256, 256, fcopy()), (szw, 256, 512, 256, fgate(0, 256))])
                    chunk_loop(seq, 128, 256, 256, lambda c, L=L: decT[0:L, c, 0:L], eL_gam(seq, 128), post_std(seq, 256, 8 + h * 2), S_,
                               st_in=lambda c, S, h=h: P.dma("sp", ("sti", c % 2), S[:, 0:256], sret[c, h]),
                               st_out=lambda c, S, h=h: P.dma("pool", ("sto", c % 2), s_ret[c, h], S[:, 0:256], final_only=True))
                    if seq is segs[-1]:
                        P.dma("sp", "pst", p_ret[h], S_[:, 0:256], final_only=True)
            flush_pending()
            return

        sqb2 = [P.sb("sqb%d" % i, [128, SEGL], BF16) for i in range(2)]
        ubufK = P.sb("ubufK", [128, SEGL + 8]); ubufV = P.sb("ubufV", [128, SEGL + 8]); rawK = P.sb("rawK", [128, SEGL]); rawV = P.sb("rawV", [128, SEGL])
        rinvK = P.sb("rinvK", [128, SEGL]); bufT3 = P.sb("bufT3", [128, 3, 3, 16]); urow3 = P.sb("urow3", [16, 3, 128])
        bcst = P.sb("bcst", [128, 40]); cw = P.sb("cw", [128, 12]); pfx = P.sb("pfx", [128, 3, 3]); scv = P.sb("scv", [16, 3, 3, 128])
        gb8 = P.sb("gb8", [8, 2]); mblk = tmpC[0:8, 0:512]
        iT = tmpA[0:8, 0:512]; fTt = tmpB[0:8, 0:512]; mlast = P.sb("mlast", [8, 1]); mrow = P.sb("mrow", [1, 8])
        P.dma("sp", "lw", bcst[:, 0:8], a_log[0:1, :].broadcast_to([128, 8])); P.dma("sp", "lw", bcst[:, 8:16], dt_bias[0:1, :].broadcast_to([128, 8]))
        P.dma("sp", "lw", bcst[:, 16:32], gate_b[0:1, :].broadcast_to([128, 16]))
        P.dma("sp", "lw", gb8[:, 0:1], gate_b[0, 0:8].rearrange("(p o) -> p o", o=1), allow_slow_non_contiguous=True)
        P.dma("sp", "lw", gb8[:, 1:2], gate_b[0, 8:16].rearrange("(p o) -> p o", o=1), allow_slow_non_contiguous=True)
        P.act(bcst[:, 32:40], bcst[:, 0:8], AF.Exp)
        P.dma("sp", "scv", s_conv[:, 0:2, :], sconv[:, 1:3, :], final_only=True)
        wload(wsm[:, :, 0:16], W_in[:, 3072:3088])
        wload(wsm[:, :, 16:32], W_in[:, 5136:5152])
        tm(full, wsm, 0, 16, [(gates, 0, 16, 16, fcopy("dve"))])
        tm(samp, wsm, 0, 16, [(gates_s, 0, 16, 16, fcopy("dve"))])

        def gdn_gates(gt, r, nch, beta_o, nbeta_o, g_o):
            P.act(beta_o, gt[0:r, 0:nch, 0:8], AF.Sigmoid)
            P.ts("dve", nbeta_o, beta_o, -1.0, ALU.mult)
            P.tt("dve", g_o, gt[0:r, 0:nch, 8:16], bcst[0:r, 8:16].unsqueeze(1).broadcast_to([r, nch, 8]), ALU.add)
            P.act(g_o, g_o, AF.Exp)
            P.ts("dve", g_o, g_o, 1.0, ALU.add)
            P.act(g_o, g_o, AF.Ln)
            P.tt("dve", g_o, g_o, bcst[0:r, 32:40].unsqueeze(1).broadcast_to([r, nch, 8]), ALU.mult)
            P.ts("dve", g_o, g_o, -1.0, ALU.mult)
        gdn_gates(gates, 64, 32, gA[:], gB[:], gC[:])
        gdn_gates(gates_s, 1, 16, sA[:], sB[:], sC[:])
        load_nw(gdn_nw, 128)
        for h in range(8):
            for i in range(3):
                wload(wh[:, :, i * 128:(i + 1) * 128], W_in[:, i * 1024 + h * 128:i * 1024 + (h + 1) * 128])
                P.dma("sp", "cw", cw[:, i * 4:(i + 1) * 4], conv_w[:, i * 1024 + h * 128:i * 1024 + (h + 1) * 128].rearrange("j p -> p j"), allow_slow_non_contiguous=True)
                P.dma("sp", "scvl", scv[:, i, :, :], sconv[:, :, i * 1024 + h * 128:i * 1024 + (h + 1) * 128])
            wload(wh[:, :, 384:512], W_in[:, 6176 + h * 128:6176 + (h + 1) * 128])
            S_ = Sf[0]
            P.memset("pool", S_[:, 0:128], 0.0); P.memset("pool", Sb[:, 0:128], 0.0)
            for si, seq in enumerate(segs + [samp]):
                L, n, nch = seq.L, seq.n, seq.nch
                if seq.sample:
                    beta_h, nbeta_h, g_h = sA[0:1, :, h], sB[0:1, :, h], sC[0:1, :, h]
                else:
                    c0 = si * NCS
                    beta_h, nbeta_h, g_h = gA[:, c0:c0 + NCS, h], gB[:, c0:c0 + NCS, h], gC[:, c0:c0 + NCS, h]
                decay(seq, g_h, strict=(L > 1))
                def qkv_lockstep(seq=seq, si=si, L=L, n=n, nch=nch, beta_h=beta_h, h=h):
                    ub = [ubuf, ubufK, ubufV]; rw = [tmpA, rawK, rawV]; rv = [tmpC, rinvK]
                    for i in range(3):
                        proj_fm(seq, wh, i * 128, 128, lambda ps, b0, b1, i=i: P.copy("act", ub[i][:, 3 + b0:3 + b1], ps))
                    taps3 = []
                    for i in range(3):
                        if not seq.sample:
                            if si == 0:
                                P.memset("pool", ub[i][:, 0:3], 0.0)
                            else:
                                P.copy("pool", ub[i][:, 0:3], pfx[:, i, :])
                            taps3.append([ub[i][:, j:j + n] for j in range(4)])
                        else:
                            for j in range(3):
                                ps = pj()
                                P.tr(ps[:, 0:NS], scv[0:NS, i, j, :], ident[0:NS, 0:NS])
                                P.copy("act", bufT3[:, i, j, :], ps[:, 0:NS])
                            taps3.append([bufT3[:, i, 0, :], bufT3[:, i, 1, :], bufT3[:, i, 2, :], ub[i][:, 3:3 + NS]])
                            ps = pj()
                            P.tr(ps[0:NS, 0:128], ub[i][:, 3:3 + NS], ident[:])
                            P.copy("act", urow3[:, i, :], ps[0:NS, 0:128])
                            P.dma("sp", "uro", s_conv[:, 2, i * 1024 + h * 128:i * 1024 + (h + 1) * 128], urow3[:, i, :], final_only=True)
                    for i in range(3):
                        P.ts("dve", rw[i][:, 0:n], taps3[i][0], cw[:, i * 4:i * 4 + 1], ALU.mult)
                        for j in range(1, 4):
                            P.stt(rw[i][:, 0:n], taps3[i][j], cw[:, i * 4 + j:i * 4 + j + 1], rw[i][:, 0:n], ALU.mult, ALU.add)
                        if not seq.sample:
                            P.copy("pool", pfx[:, i, :], ub[i][:, n:n + 3])
                            if seq is segs[-1]:
                                P.dma("sp", "pcv", p_conv[:, i * 1024 + h * 128:i * 1024 + (h + 1) * 128].rearrange("j p -> p j"), ub[i][:, n:n + 3], allow_slow_non_contiguous=True, final_only=True)
                    for i in range(3):
                        P.act(rw[i][:, 0:n], rw[i][:, 0:n], AF.Silu)
                    for i in range(2):
                        P.act(sqb2[i][:, 0:n], rw[i][:, 0:n], AF.Square)
                    pss = []
                    for i in range(2):
                        ps = pj()
                        P.mm(ps[:, 0:n], onesb[:, :], sqb2[i][:, 0:n])
                        P.ts("dve", rv[i][:, 0:n], ps[:, 0:n], EPS, ALU.add)
                    for i in range(2):
                        P.act(rv[i][:, 0:n], rv[i][:, 0:n], AF.Ln)
                    for i in range(2):
                        P.act(rv[i][:, 0:n], rv[i][:, 0:n], AF.Exp, scale=-0.5)
                    P.stt(qT[:, 0:n], rw[0][:, 0:n], 128 ** -0.5, rv[0][:, 0:n], ALU.mult, ALU.mult)
                    P.tt("dve", kT32[:, 0:n], rw[1][:, 0:n], rv[1][:, 0:n], ALU.mult)
                    P.copy("act", kT[:, 0:n], kT32[:, 0:n])

                def vT_piece(seq=seq, L=L, nch=nch, beta_h=beta_h):
                    for c in range(nch):
                        ps = pj()
                        P.tr(ps[0:L, 0:128], rawV[:, c * L:(c + 1) * L], ident[:])
                        P.ts("dve", bv[0:L, c, :], ps[0:L, 0:128], beta_h[:, c:c + 1], ALU.mult)

                def misc_piece(seq=seq, L=L, n=n, nch=nch, nbeta_h=nbeta_h):
                    P.tt("dve", qgT[:, 0:n], qT[:, 0:n], gamb[:, 0:n], ALU.mult)
                    P.act(gtok[0:L, 0:nch], bht[0:L, 0:nch], AF.Exp)
                    P.tt("dve", nbg[0:L, 0:nch], gtok[0:L, 0:nch], nbeta_h, ALU.mult)
                    make_ktok(seq, 128, kT, kdec(seq))

                def z_piece(seq=seq):
                    tm(seq, wh, 384, 128, [(szw, 0, 128, 128, fgate(0, 128))])

                def doubling(L=L, nch=nch, nbeta_h=nbeta_h):
                    pX, pY, pZ = B[3], B[5], B[6]
                    for b in range(nch // 8):
                        for i in range(8):
                            c = b * 8 + i
                            P.mm(pX[0:64, i * 64:(i + 1) * 64], kT32[:, c * L:(c + 1) * L], kT32[:, c * L:(c + 1) * L])
                        P.tt("dve", Nn[:], v3(pX[0:64, :], 64), decS[:, b * 8:(b + 1) * 8, :], ALU.mult)
                        P.tt("dve", Nn[:], Nn[:], nbeta_h[:, b * 8:(b + 1) * 8].unsqueeze(2).broadcast_to([64, 8, 64]), ALU.mult)
                        for i in range(8):
                            P.tr(pY[0:64, i * 64:(i + 1) * 64], Nn[:, i, :], ident[0:64, 0:64])
                        P.copy("act", Mm[:], v3(pY[0:64, :], 64))
                        P.tt("dve", Xx[:], Mm[:], ident[0:64, 0:64].unsqueeze(1).broadcast_to([64, 8, 64]), ALU.add)
                        yield
                        for lvl in range(5):
                            for i in range(8):
                                P.mm(pX[0:64, i * 64:(i + 1) * 64], Mm[:, i, :], Nn[:, i, :])
                            if lvl < 4:
                                for i in range(8):
                                    P.mm(pY[0:64, i * 64:(i + 1) * 64], Nn[:, i, :], Mm[:, i, :])
                            P.copy("act", Nn[:], v3(pX[0:64, :], 64))
                            if lvl < 4:
                                P.copy("dve", Mm[:], v3(pY[0:64, :], 64))
                            for i in range(8):
                                P.mm(pZ[0:64, i * 64:(i + 1) * 64], Nn[:, i, :], Xx[:, i, :])
                            P.tt("dve", Xx[:], Xx[:], v3(pZ[0:64, :], 64), ALU.add)
                            if lvl < 4:
                                yield
                        P.copy("act", TT[:, b * 8:(b + 1) * 8, :], Xx[:])

                qkv_lockstep()
                flush_pending()
                pieces = [vT_piece, misc_piece, z_piece]
                if L > 1:
                    for _ in doubling():
                        if pieces:
                            pieces.pop(0)()
                while pieces:
                    pieces.pop(0)()
                chunk_loop(seq, 128, 128, 128, lambda c, L=L: decT[0:L, c, 0:L], eL_gam(seq, 128), post_std(seq, 128, h), S_, gdn=True,
                           st_in=lambda c, S, h=h: P.dma("sp", ("sti", c % 2), S[:, 0:128], sg[c, h]),
                           st_out=lambda c, S, h=h: P.dma("pool", ("sto", c % 2), s_gdn[c, h], S[:, 0:128], final_only=True))
                if seq is segs[-1]:
                    P.dma("sp", "pst", p_gdn[h], S_[:, 0:128], final_only=True)

        flush_pending()
        tm(full, wsm, 16, 16, [(gates, 0, 16, 16, lambda o, i: P.tt("dve", o, i, bcst[i.base_partition():i.base_partition() + i.shape[0], 16:32], ALU.add))])
        tm(samp, wsm, 16, 16, [(gates_s, 0, 16, 16, lambda o, i: P.tt("dve", o, i, bcst[i.base_partition():i.base_partition() + i.shape[0], 16:32], ALU.add))])

        def logsig(o, i):
            P.act(o, i, AF.Exp, scale=-1.0)
            P.ts("dve", o, o, 1.0, ALU.add)
            P.act(o, o, AF.Ln)
            P.ts("dve", o, o, -1.0, ALU.mult)
        logsig(gA[:], gates[:, :, 8:16])
        P.act(gB[:], gates[:, :, 0:8], AF.Exp); P.ts("dve", gB[:], gB[:], 0.125, ALU.mult)
        for bi in range(4):
            for (cofs, dst, gcol) in ((16, iT, 0), (24, fTt, 1)):
                ps = pj()
                for c in range(8):
                    P.mm(ps[0:8, :], wsm[:, c, cofs:cofs + 8], hT[:, c, bi * 512:(bi + 1) * 512], start=(c == 0), stop=(c == 7))
                P.ts("dve", dst, ps[0:8, :], gb8[:, gcol:gcol + 1], ALU.add)
            logsig(fTt, fTt)
            P.scan(mblk, fTt, iT, 0.0 if bi == 0 else mlast[:, 0:1], ALU.add, ALU.max)
            P.copy("dve", mlast[:], tmpC[0:8, 511:512])
        ps = pj()
        P.tr(ps[0:1, 0:8], mlast[0:8, 0:1], ident[0:8, 0:8])
        P.copy("act", mrow[:], ps[0:1, 0:8])
        P.dma("sp", "pmm", p_mm, mrow[:], final_only=True)
        P.act(mrow[:], mrow[:], AF.Exp, scale=-1.0)
        ps = pj()
        P.mm(ps[0:64, 0:8], ones32[0:1, 0:64], mrow[0:1, :])
        P.copy("act", fsc[:, 128:136], ps[0:64, 0:8])
        P.dma("sp", "sm0", sm0[:].rearrange("o s h -> o (s h)"), smm)
        logsig(sA[:], gates_s[:, :, 8:16])
        P.tt("dve", sB[:], gates_s[:, :, 0:8], sm0[:], ALU.subtract)
        P.act(sB[:], sB[:], AF.Exp); P.ts("dve", sB[:], sB[:], 0.125, ALU.mult)
        P.tt("dve", sC[:], sA[:], sm0[:], ALU.add)
        P.tt("dve", sC[:], sC[:], gates_s[:, :, 0:8], ALU.max)
        P.dma("sp", "smo", s_mm, sC[:].rearrange("o s h -> o (s h)"), final_only=True)
        P.tt("dve", sD[:], sm0[:], sC[:], ALU.subtract)
        P.act(sD[:], sD[:], AF.Exp)
        P.act(sE[:], sm0[:], AF.Exp, scale=-1.0)
        ps = pj()
        P.mm(ps[0:64, 0:128], ones32[0:1, 0:64], sD[:].rearrange("o s h -> o (s h)"))
        P.copy("act", fsc[:, 0:128], ps[0:64, 0:128])
        load_nw(ml_nw, 128)
        for h in range(8):
            wload(wh[:, :, 0:64], W_in[:, 3088 + h * 64:3088 + (h + 1) * 64]); wload(wh[:, :, 64:128], W_in[:, 3600 + h * 64:3600 + (h + 1) * 64])
            wload(wh[:, :, 128:256], W_in[:, 4112 + h * 128:4112 + (h + 1) * 128]); wload(wh[:, :, 256:384], W_in[:, 5152 + h * 128:5152 + (h + 1) * 128])
            wload(wh[:, :, 384:512], W_in[:, 7200 + h * 128:7200 + (h + 1) * 128])
            S_ = Sf[0]
            P.memset("pool", S_[:, 0:130], 0.0); P.memset("pool", Sb[:, 0:130], 0.0)
            for si, seq in enumerate(segs + [samp]):
                L, n, nch = seq.L, seq.n, seq.nch
                if seq.sample:
                    lf_h, cs_h = sA[0:1, :, h], sB[0:1, :, h]
                else:
                    c0 = si * NCS
                    lf_h, cs_h = gA[:, c0:c0 + NCS, h], gB[:, c0:c0 + NCS, h]
                decay(seq, lf_h)
                proj_fm(seq, wh, 0, 64, lambda ps, b0, b1: P.copy("act", qT[0:64, b0:b1], ps))
                proj_fm(seq, wh, 64, 64, lambda ps, b0, b1: P.copy("act", kT[0:64, b0:b1], ps))
                flush_pending()
                P.tt("dve", qgT[0:64, 0:n], qT[0:64, 0:n], gamb[0:64, 0:n], ALU.mult)
                P.tt("dve", decT[0:L, 0:nch, 0:L], decT[0:L, 0:nch, 0:L], cs_h.unsqueeze(2).broadcast_to([L, nch, L]), ALU.mult)
                make_ktok(seq, 64, kT, kdec(seq))
                tm(seq, wh, 128, 384, [(vtok, 0, 128, 128, fcopy()), (szw, 128, 384, 128, fgate(0, 128, sig=True))])
                P.memset("pool", vtok[0:L, 0:nch, 128:129], 1.0)

                def post_ml(c0, G, seq=seq, h=h, L=L):
                    dn = colg[0:L, 16:16 + G]
                    if seq.sample:
                        P.tt("dve", dn, dn, sE[0:1, c0:c0 + G, h], ALU.max)
                    else:
                        P.ts("dve", dn, dn, 1.0, ALU.max)
                    P.recip(dn, dn)
                    t_ = colg[0:L, 24:24 + G]
                    P.tt("dve", t_, dn, dn, ALU.mult)
                    P.tt("dve", t_, t_, colg[0:L, 0:G], ALU.mult)
                    rms_scale(colg[0:L, 8:8 + G], t_, 128)
                    P.tt("dve", colg[0:L, 8:8 + G], colg[0:L, 8:8 + G], dn, ALU.mult)
                    post_rows(seq, c0, G, 128, 8 + h, colg[0:L, 8:8 + G])

                def st_in(c, S, h=h):
                    P.dma("sp", ("sti", c % 2), S[0:64, 0:129], smcn[c, h])

                def st_out(c, S, h=h):
                    So_ = So4[c % 4]
                    P.ts("dve", So_[0:64, 0:129], S[0:64, 0:129], fsc[0:64, c * 8 + h:c * 8 + h + 1], ALU.mult)
                    P.dma("pool", ("sto", 0), s_mcn[c, h], So_[0:64, 0:129], final_only=True)
                chunk_loop(seq, 64, 128, 129, lambda c, L=L: decT[0:L, c, 0:L], eL_gam(seq, 64), post_ml, S_, st_in=st_in, st_out=st_out, den=True)
                if seq is segs[-1]:
                    P.ts("dve", So[0:64, 0:129], S_[0:64, 0:129], fsc[0:64, 128 + h:129 + h], ALU.mult)
                    P.dma("sp", "pst", p_mcn[h], So[0:64, 0:129], final_only=True)
        flush_pending()

    for layer in range(2):
        m = P.scope_begin()
        mod_phase(layer, m)
        phase0(layer, xp if layer == 0 else x1)
        P.scope_end(m)
        m = P.scope_begin()
        mixers(layer)
        P.scope_end(m)
        m = P.scope_begin()
        phaseB(layer, ev_w_out if layer == 0 else od_w_out, xp if layer == 0 else x1, layer == 1)
        P.scope_end(m)
    P.emit()
    P.close()
    return nc

_NC_CACHE = {}


def _consts():
    ident = np.eye(128, dtype=np.float32)
    ii = np.arange(64)
    tri = (ii[:, None] <= ii[None, :]).astype(np.float32)
    negU = np.where(ii[:, None] <= ii[None, :], 0.0, NEG).astype(np.float32)
    negLs = np.where(ii[None, :] < ii[:, None], 0.0, NEG).astype(np.float32)
    half = 64
    inv = (np.float32(10000.0) ** (-np.arange(half, dtype=np.float32) / np.float32(half))).astype(np.float32)

    def tabs(pos):
        ang = (pos.astype(np.float32)[None, :] * inv[:, None]).astype(np.float32)
        c = np.cos(ang).astype(np.float32); s = np.sin(ang).astype(np.float32)
        return np.concatenate([c, c], 0), np.concatenate([s, -s], 0)
    cosT, sinT = tabs(np.arange(T))
    cosS, sinS = tabs(np.full((NS,), 16384))
    sel = np.zeros((17, 128), np.float32); sel[16, :] = 1.0
    return dict(c_ident=ident, c_tri=tri, c_negU=negU, c_negLs=negLs, c_cos=np.ascontiguousarray(cosT), c_sin=np.ascontiguousarray(sinT),
                c_cosS=np.ascontiguousarray(cosS), c_sinS=np.ascontiguousarray(sinS), c_sel=sel)


def kernel(x_prompt, x_sample, c_prompt, c_sample, state_gdn, state_gdn_conv, state_mlstm_c, state_mlstm_n, state_mlstm_m,
           state_gla, state_ret, ada_w, ada_b, norm_w, ev_w_in, ev_w_out, gdn_conv_w, gdn_a_log, gdn_dt_bias, gdn_norm_w,
           mlstm_gate_b, mlstm_norm_w, od_w_in, od_w_out, gla_w2, gla_b2, gla_norm_w, ret_norm_w, final_norm_w):
    f = lambda a: np.ascontiguousarray(np.asarray(a, dtype=np.float32))
    if "nc" not in _NC_CACHE:
        _NC_CACHE["nc"] = build_nc()
    nc = _NC_CACHE["nc"]
    cst = _consts()
    shared = dict(ada_w=f(ada_w), ada_b=f(ada_b), norm_w=f(norm_w), ev_w_in=f(ev_w_in[0]), ev_w_out=f(ev_w_out[0]), conv_w=f(gdn_conv_w[0]),
                  a_log=f(gdn_a_log), dt_bias=f(gdn_dt_bias), gdn_nw=f(gdn_norm_w), gate_b=f(mlstm_gate_b), ml_nw=f(mlstm_norm_w),
                  od_w_in=f(od_w_in[0]), od_w_out=f(od_w_out[0]), gla_w2=f(gla_w2[0]), gla_b2=f(gla_b2), gla_nw=f(gla_norm_w), ret_nw=f(ret_norm_w),
                  fin_nw=f(np.asarray(final_norm_w).reshape(1, D)))
    shared.update(cst)
    in_maps = []
    for b in range(8):
        sl = slice(b * NS, (b + 1) * NS)
        m = dict(shared)
        m.update(xp=f(x_prompt[b]), xs=f(np.asarray(x_sample)[sl, 0]), cc=f(np.concatenate([np.asarray(c_sample)[sl], np.asarray(c_prompt)[b:b + 1]], 0)),
                 sg=f(state_gdn[0][sl]), sconv=f(state_gdn_conv[0][sl]), smcn=f(np.concatenate([np.asarray(state_mlstm_c[0][sl]), np.asarray(state_mlstm_n[0][sl])[..., None]], -1)),
                 smm=f(np.asarray(state_mlstm_m[0][sl]).reshape(1, 128)), sgla=f(state_gla[0][sl]), sret=f(state_ret[0][sl]))
        in_maps.append(m)
    res = run_bass_kernel_spmd(nc, in_maps, core_ids=list(range(8)))
    R = res.results
    st = lambda k: np.stack([np.asarray(r[k], dtype=np.float32) for r in R], 0)
    cat = lambda k: np.concatenate([np.asarray(r[k], dtype=np.float32) for r in R], 0)
    y_prompt = st("yp")
    y_sample = cat("ys").reshape(128, 1, D)
    outs = (y_prompt, y_sample,
            st("p_gdn")[None], st("p_conv")[None], st("p_mcn")[None][..., 0:128], st("p_mcn")[None][..., 128], st("p_mm").reshape(8, 8)[None],
            st("p_gla")[None], st("p_ret")[None],
            cat("s_gdn")[None], cat("s_conv")[None], cat("s_mcn")[None][..., 0:128], cat("s_mcn")[None][..., 128], cat("s_mm").reshape(128, 8)[None],
            cat("s_gla")[None], cat("s_ret")[None])
    return tuple(np.ascontiguousarray(o.astype(np.float32)) for o in outs)
```

```python
import numpy as np
import concourse.bass as bass
import concourse.mybir as mybir

F32 = mybir.dt.float32
BF16 = mybir.dt.bfloat16
ALU = mybir.AluOpType
AF = mybir.ActivationFunctionType
AX = mybir.AxisListType

ENGS = ("pe", "dve", "act", "pool", "sp")


class Prog:
    def __init__(self, nc, same_engine_sync=True):
        self.nc = nc
        self.q = {e: [] for e in ENGS}
        self.cnt = {e: 0 for e in ENGS}
        self.seen = {e: {} for e in ENGS}
        self.res = {}
        self.dcnt = {}
        self.sems = {}
        self._ctx = []
        self._semctx = []
        self.same = same_engine_sync
        self.out_events = []
        for e in ENGS:
            self._sem(("eng", e))

    def _enter(self, cm):
        v = cm.__enter__()
        self._ctx.append(cm)
        return v

    def _sem(self, key):
        if key not in self.sems:
            cm = self.nc.semaphore("s_" + "_".join(str(k) for k in key).replace("(", "").replace(")", "").replace(",", "_").replace(" ", "").replace("'", ""))
            self.sems[key] = cm.__enter__()
            self._semctx.append(cm)
        return self.sems[key]

    def sb(self, name, shape, dt=F32):
        return self._enter(self.nc.sbuf_tensor(name, list(shape), dt))

    def ps(self, name, shape, dt=F32):
        return self._enter(self.nc.psum_tensor(name, list(shape), dt))

    def scope_begin(self):
        return len(self._ctx)

    def barrier(self):
        evs = []
        for e in ENGS:
            if self.cnt[e] > 0:
                evs.append((("eng", e), self.cnt[e]))
        for key, c in self.dcnt.items():
            evs.append((("dma", key), 16 * c))
        for e in ENGS:
            waits = []
            for sk, val in evs:
                if sk == ("eng", e):
                    continue
                if self.seen[e].get(sk, 0) >= val:
                    continue
                self.seen[e][sk] = val
                waits.append((sk, val))
            if waits:
                self.q[e].append((waits, None, None, 0))

    def scope_end(self, marker):
        self.barrier()
        while len(self._ctx) > marker:
            self._ctx.pop().__exit__(None, None, None)

    def close(self):
        for cm in reversed(self._ctx):
            cm.__exit__(None, None, None)
        self._ctx = []
        for cm in reversed(self._semctx):
            cm.__exit__(None, None, None)
        self._semctx = []

    @staticmethod
    def _nm(ap):
        return ap.tensor.name

    def _deps(self, eng, reads, writes):
        waits = {}

        def need(ev):
            if ev is None:
                return
            sk, val, src = ev
            if src == eng and (eng == "pe" or not self.same) and sk == ("eng", eng):
                return
            if self.seen[eng].get(sk, 0) >= val:
                return
            if waits.get(sk, 0) < val:
                waits[sk] = val

        for r in reads:
            st = self.res.get(r)
            if st:
                need(st["w"])
        for w in writes:
            st = self.res.get(w)
            if st:
                need(st["w"])
                for ev in st["r"].values():
                    need(ev)
        for sk, val in waits.items():
            self.seen[eng][sk] = val
        return list(waits.items())

    def _commit(self, ev, reads, writes):
        for r in reads:
            st = self.res.setdefault(r, {"w": None, "r": {}})
            st["r"][ev[0]] = ev
        for w in writes:
            self.res[w] = {"w": ev, "r": {}}

    def op(self, eng, fn, reads, writes):
        rn = [r if isinstance(r, str) else self._nm(r) for r in reads if r is not None and not isinstance(r, (int, float))]
        wn = [w if isinstance(w, str) else self._nm(w) for w in writes]
        waits = self._deps(eng, rn, wn)
        self.cnt[eng] += 1
        sk = ("eng", eng)
        ev = (sk, self.cnt[eng], eng)
        self.q[eng].append((waits, fn, sk, 1))
        self._commit(ev, rn, wn)
        return ev

    def dma(self, eng, key, out, in_, final_only=False, **kw):
        key = (self._nm(out) + "__" + self._nm(in_)) if final_only else self._nm(out)
        sk = ("dma", key)
        self._sem(sk)
        rn = [self._nm(in_)]
        wn = [] if final_only else [self._nm(out)]
        waits = self._deps(eng, rn, wn)
        self.dcnt[key] = self.dcnt.get(key, 0) + 1
        ev = (sk, 16 * self.dcnt[key], "dma")
        self.q[eng].append((waits, lambda e: e.dma_start(out=out, in_=in_, **kw), sk, 16))
        self._commit(ev, rn, wn)
        return ev

    def mm(self, out, lhsT, rhs, start=True, stop=True, **kw):
        return self.op("pe", lambda e: e.matmul(out, lhsT, rhs, start=start, stop=stop, **kw), [lhsT, rhs], [out])

    def tr(self, out, in_, ident):
        return self.op("pe", lambda e: e.transpose(out, in_, ident), [in_, ident], [out])

    def tt(self, eng, out, in0, in1, op):
        return self.op(eng, lambda e: e.tensor_tensor(out, in0, in1, op), [in0, in1], [out])

    def ts(self, eng, out, in0, s1, op0, s2=None, op1=None, accum_out=None):
        rd = [in0] + [s for s in (s1, s2) if s is not None and not isinstance(s, (int, float))]
        wr = [out] + ([accum_out] if accum_out is not None else [])
        if op1 is None:
            return self.op(eng, lambda e: e.tensor_scalar(out, in0, s1, None, op0), rd, wr)
        if accum_out is None:
            return self.op(eng, lambda e: e.tensor_scalar(out, in0, s1, s2, op0, op1), rd, wr)
        return self.op(eng, lambda e: e.tensor_scalar(out, in0, s1, s2, op0, op1, accum_out), rd, wr)

    def stt(self, out, in0, scalar, in1, op0, op1, accum_out=None, eng="dve"):
        rd = [in0, in1] + ([scalar] if not isinstance(scalar, (int, float)) else [])
        wr = [out] + ([accum_out] if accum_out is not None else [])
        if accum_out is None:
            return self.op(eng, lambda e: e.scalar_tensor_tensor(out, in0, scalar, in1, op0, op1), rd, wr)
        return self.op(eng, lambda e: e.scalar_tensor_tensor(out, in0, scalar, in1, op0, op1, accum_out), rd, wr)

    def act(self, out, in_, func, bias=None, scale=None, accum_out=None):
        rd = [in_] + [s for s in (bias, scale) if s is not None and not isinstance(s, (int, float))]
        wr = [out] + ([accum_out] if accum_out is not None else [])
        kw = {}
        if bias is not None:
            kw["bias"] = bias
        if scale is not None:
            kw["scale"] = scale
        if accum_out is not None:
            kw["accum_out"] = accum_out
        return self.op("act", lambda e: e.activation(out, in_, func, **kw), rd, wr)

    def copy(self, eng, out, in_):
        if eng == "act":
            return self.op("act", lambda e: e.copy(out, in_), [in_], [out])
        return self.op(eng, lambda e: e.tensor_copy(out, in_), [in_], [out])

    def memset(self, eng, ap, val):
        return self.op(eng, lambda e: e.memset(ap, val), [], [ap])

    def recip(self, out, in_):
        return self.op("dve", lambda e: e.reciprocal(out, in_), [in_], [out])

    def scan(self, out, d0, d1, init, op0, op1):
        rd = [d0, d1] + ([init] if not isinstance(init, (int, float)) else [])
        return self.op("dve", lambda e: e.tensor_tensor_scan(out, d0, d1, init, op0, op1), rd, [out])

    def treduce(self, eng, out, in_, axis, op):
        return self.op(eng, lambda e: e.tensor_reduce(out, in_, axis, op), [in_], [out])

    def emit(self, final_waits_engine="sp"):
        nc = self.nc
        fin = []
        for key, c in self.dcnt.items():
            fin.append((("dma", key), 16 * c))
        for e in ENGS:
            if e != final_waits_engine and self.cnt[e] > 0:
                fin.append((("eng", e), self.cnt[e]))
        sems = self.sems
        q = self.q
        with nc.Block() as block:
            def run(ename, handle, extra=None):
                for waits, fn, sk, inc in q[ename]:
                    for wk, wv in waits:
                        handle.wait_ge(sems[wk], wv)
                    if fn is not None:
                        fn(handle).then_inc(sems[sk], inc)
                if extra:
                    for wk, wv in extra:
                        handle.wait_ge(sems[wk], wv)

            @block.sync
            def _(h):
                run("sp", h, fin if final_waits_engine == "sp" else None)

            @block.tensor
            def _(h):
                run("pe", h)

            @block.vector
            def _(h):
                run("dve", h)

            @block.scalar
            def _(h):
                run("act", h)

            @block.gpsimd
            def _(h):
                run("pool", h)

from concourse.bass_utils import run_bass_kernel_spmd

import math

D = 1024
T = 2048
NS = 16
LCH = 64
SEG = 512
EPS = 1e-6
NEG = -30000.0
import os
TM128 = os.environ.get('KV_TM128', '1') == '1'
EXPGATE = os.environ.get('KV_EXPGATE', '1') == '1'


class Seq:
    def __init__(self, n, L, hT, sample, base=0):
        self.n, self.L, self.hT, self.sample, self.base = n, L, hT, sample, base
        self.nch = n // L
        self.blocks = [(b, min(b + 512, n)) for b in range(0, n, 512)]


def v3(ap, L):
    return ap.rearrange("p (c l) -> p c l", l=L)


def build_nc():
    nc = bass.Bass("TRN2", target_bir_lowering=False)
    P = Prog(nc, same_engine_sync=True)

    def din(name, shape):
        return nc.dram_tensor(name, list(shape), F32, kind="ExternalInput").ap()

    def dout(name, shape):
        return nc.dram_tensor(name, list(shape), F32, kind="ExternalOutput").ap()

    xp = din("xp", [T, D]); xs_d = din("xs", [NS, D]); cc = din("cc", [17, D])
    sg = din("sg", [NS, 8, 128, 128]); sconv = din("sconv", [NS, 3, 3072]); smcn = din("smcn", [NS, 8, 64, 129])
    smm = din("smm", [1, 128]); sgla = din("sgla", [NS, 4, 128, 256]); sret = din("sret", [NS, 4, 128, 256])
    ada_w = din("ada_w", [2, D, 3072]); ada_b = din("ada_b", [2, 3072]); norm_w = din("norm_w", [2, D])
    ev_w_in = din("ev_w_in", [D, 8224]); ev_w_out = din("ev_w_out", [2048, D]); conv_w = din("conv_w", [4, 3072])
    a_log = din("a_log", [1, 8]); dt_bias = din("dt_bias", [1, 8]); gdn_nw = din("gdn_nw", [1, 128]); gate_b = din("gate_b", [1, 16])
    ml_nw = din("ml_nw", [1, 128]); od_w_in = din("od_w_in", [D, 6160]); od_w_out = din("od_w_out", [2048, D])
    gla_w2 = din("gla_w2", [16, 512]); gla_b2 = din("gla_b2", [1, 512]); gla_nw = din("gla_nw", [1, 256]); ret_nw = din("ret_nw", [1, 256])
    fin_nw = din("fin_nw", [1, D])
    c_ident = din("c_ident", [128, 128]); c_tri = din("c_tri", [64, 64]); c_negU = din("c_negU", [64, 64]); c_negLs = din("c_negLs", [64, 64])
    c_cos = din("c_cos", [128, T]); c_sin = din("c_sin", [128, T]); c_cosS = din("c_cosS", [128, NS]); c_sinS = din("c_sinS", [128, NS])
    c_sel = din("c_sel", [17, 128])

    yp = dout("yp", [T, D]); ys = dout("ys", [NS, D])
    p_gdn = dout("p_gdn", [8, 128, 128]); p_conv = dout("p_conv", [3, 3072]); p_mcn = dout("p_mcn", [8, 64, 129])
    p_mm = dout("p_mm", [1, 8]); p_gla = dout("p_gla", [4, 128, 256]); p_ret = dout("p_ret", [4, 128, 256])
    s_gdn = dout("s_gdn", [NS, 8, 128, 128]); s_conv = dout("s_conv", [NS, 3, 3072]); s_mcn = dout("s_mcn", [NS, 8, 64, 129])
    s_mm = dout("s_mm", [1, 128]); s_gla = dout("s_gla", [NS, 4, 128, 256]); s_ret = dout("s_ret", [NS, 4, 128, 256])
    x1 = nc.dram_tensor("x1scr", [T, D], F32).ap()
    scrF = [nc.dram_tensor("scrF%d" % i, [16, 512], F32).ap() for i in range(4)]
    scrB = [nc.dram_tensor("scrB%d" % i, [16, 512], BF16).ap() for i in range(4)]
    scr_i = [0]

    ident = P.sb("ident", [128, 128]); identb = P.sb("identb", [128, 128], BF16); ones32 = P.sb("ones32", [128, 128]); onesb = P.sb("onesb", [128, 128], BF16)
    tri = P.sb("tri", [64, 64]); negU = P.sb("negU", [64, 64]); negLs = P.sb("negLs", [64, 64]); sel = P.sb("sel", [17, 128])
    hT = P.sb("hT", [128, 8, T], BF16); hsT = P.sb("hsT", [128, 8, NS], BF16)
    yT = P.sb("yT", [128, 16, T], BF16); ysT = P.sb("ysT", [128, 16, NS], BF16)
    xs = P.sb("xs_sb", [NS, D]); grow = P.sb("grow", [17, D]); modh = [None]
    wpT = P.sb("wpT", [128, 8]); shT = P.sb("shT", [128, 8]); nwT = P.sb("nwT", [128, 8])
    B = [P.ps("B%d" % i, [128, 512]) for i in range(7)]
    ptb = P.ps("ptb", [128, 1024], BF16)
    pj_i = [0]

    def pj():
        pj_i[0] ^= 1
        return B[pj_i[0]]

    psb = [B[2], B[3]]
    pm = B[4]; po = B[5]; pst = B[6]
    pk = pm[:, 128:256]; pu = pm[:, 256:384]; pa = pm[:, 384:448]

    for t_, d_ in ((ident, c_ident), (tri, c_tri), (negU, c_negU), (negLs, c_negLs), (sel, c_sel), (xs, xs_d)):
        P.dma("sp", "c0", t_[:], d_)
    P.copy("dve", identb[:], ident[:])
    P.memset("pool", ones32[:], 1.0)
    P.memset("pool", onesb[:], 1.0)

    def rms_scale(out_col, ssq_col, n):
        P.ts("dve", out_col, ssq_col, 1.0 / n, ALU.mult, EPS, ALU.add)
        P.act(out_col, out_col, AF.Ln)
        P.act(out_col, out_col, AF.Exp, scale=-0.5)

    def wload(dst, src2d):
        P.dma("pool", "w_" + dst.tensor.name, dst, src2d.rearrange("(c p) n -> p c n", p=128))

    def mod_phase(layer, sc):
        mod = P.sb("mod%d" % layer, [17, 3072]); modh[0] = mod
        cs_ = P.sb("cs%d" % layer, [17, D]); csT = P.sb("csT%d" % layer, [128, 8, 17], BF16)
        wa = P.sb("wa%d" % layer, [128, 8, 512], BF16); ab = P.sb("ab%d" % layer, [17, 512]); modT = P.sb("modT%d" % layer, [128, 16, 17])
        P.dma("sp", "cc", cs_[:], cc)
        P.act(cs_[:], cs_[:], AF.Silu)
        for j in range(8):
            P.tr(B[0][:, 0:17], cs_[0:17, j * 128:(j + 1) * 128], ident[0:17, 0:17])
            P.copy("dve", csT[:, j, :], B[0][:, 0:17])
        for blk in range(6):
            wload(wa[:], ada_w[layer][:, blk * 512:(blk + 1) * 512])
            P.dma("sp", "ab", ab[:], ada_b[layer:layer + 1, blk * 512:(blk + 1) * 512].broadcast_to([17, 512]))
            ps = pj()
            for c in range(8):
                P.mm(ps[0:17, :], csT[:, c, :], wa[:, c, :], start=(c == 0), stop=(c == 7))
            P.tt("dve", mod[0:17, blk * 512:(blk + 1) * 512], ps[0:17, :], ab[:], ALU.add)
        for j in range(16):
            P.tr(B[0][:, 0:17], mod[0:17, j * 128:(j + 1) * 128], ident[0:17, 0:17])
            P.copy("dve", modT[:, j, :], B[0][:, 0:17])
        P.dma("sp", "nwT", nwT[:], norm_w[layer].rearrange("(c p) -> p c", p=128), allow_slow_non_contiguous=True)
        P.stt(wpT[:], modT[:, 8:16, 16], 1.0, nwT[:], ALU.add, ALU.mult)
        P.copy("dve", shT[:], modT[:, 0:8, 16])
        P.copy("dve", grow[:], mod[0:17, 2048:3072])

    def phase0(layer, src):
        xt = [P.sb("p0x%d_%d" % (layer, i), [128, D]) for i in range(2)]
        junk = P.sb("p0j%d" % layer, [128, D], BF16); col = P.sb("p0c%d" % layer, [128, 2]); nwb = P.sb("p0n%d" % layer, [NS, D])
        hs = P.sb("p0h%d" % layer, [NS, D])
        for i in range(16):
            x_ = xt[i % 2]
            P.dma("sp", ("p0", i % 2), x_[:], src[i * 128:(i + 1) * 128, :])
            P.act(junk[:], x_[:], AF.Square, accum_out=col[:, 0:1])
            rms_scale(col[:, 1:2], col[:, 0:1], D)
            P.ts("dve", x_[:], x_[:], col[:, 1:2], ALU.mult)
            for j in range(8):
                ps = pj()
                P.tr(ps[:, 0:128], x_[:, j * 128:(j + 1) * 128], ident[:])
                if j % 2:
                    P.ts("dve", hT[:, j, i * 128:(i + 1) * 128], ps[:, 0:128], wpT[:, j:j + 1], ALU.mult, shT[:, j:j + 1], ALU.add)
                else:
                    P.act(hT[:, j, i * 128:(i + 1) * 128], ps[:, 0:128], AF.Identity, bias=shT[:, j:j + 1], scale=wpT[:, j:j + 1])
        P.dma("sp", "p0n", nwb[:], norm_w[layer:layer + 1, :].broadcast_to([NS, D]))
        P.act(junk[0:NS, :], xs[:], AF.Square, accum_out=col[0:NS, 0:1])
        rms_scale(col[0:NS, 1:2], col[0:NS, 0:1], D)
        P.ts("dve", hs[:], xs[:], col[0:NS, 1:2], ALU.mult)
        P.tt("dve", hs[:], hs[:], nwb[:], ALU.mult)
        mod = modh[0]
        P.ts("dve", nwb[:], mod[0:NS, 1024:2048], 1.0, ALU.add)
        P.tt("dve", hs[:], hs[:], nwb[:], ALU.mult)
        P.tt("dve", hs[:], hs[:], mod[0:NS, 0:1024], ALU.add)
        for j in range(8):
            ps = pj()
            P.tr(ps[:, 0:NS], hs[0:NS, j * 128:(j + 1) * 128], ident[0:NS, 0:NS])
            P.copy("dve", hsT[:, j, :], ps[:, 0:NS])

    def phaseB(layer, w_out, src, last):
        wo = P.sb("wo%d" % layer, [128, 16, D], BF16)
        xt = [P.sb("pbx%d_%d" % (layer, i), [128, D]) for i in range(2)]
        tmp = P.sb("pbt%d" % layer, [128, 512]); junk = P.sb("pbj%d" % layer, [128, D], BF16); col = P.sb("pbc%d" % layer, [128, 2])
        fnb = P.sb("pbf%d" % layer, [128, D]); gate_bc = P.sb("gate_bc%d" % layer, [128, D])
        for half in range(2):
            ps = pj()
            P.mm(ps[:, :], sel[0:17, :], grow[0:17, half * 512:(half + 1) * 512])
            P.copy("act", gate_bc[:, half * 512:(half + 1) * 512], ps[:, :])
        for q4 in range(4):
            P.dma("pool", ("wo", q4), wo[:, q4 * 4:(q4 + 1) * 4, :], w_out[q4 * 512:(q4 + 1) * 512, :].rearrange("(c p) n -> p c n", p=128))
        if last:
            P.dma("sp", "fnb", fnb[:], fin_nw[0:1, :].broadcast_to([128, D]))
        for i in range(16):
            x_ = xt[i % 2]
            P.dma("sp", ("pb", i % 2), x_[:], src[i * 128:(i + 1) * 128, :])
            for half in range(2):
                ps = pj()
                for f in range(16):
                    P.mm(ps[:, :], yT[:, f, i * 128:(i + 1) * 128], wo[:, f, half * 512:(half + 1) * 512], start=(f == 0), stop=(f == 15))
                P.tt("dve", tmp[:], ps[:, :], gate_bc[:, half * 512:(half + 1) * 512], ALU.mult)
                P.tt("dve", x_[:, half * 512:(half + 1) * 512], x_[:, half * 512:(half + 1) * 512], tmp[:], ALU.add)
            if not last:
                P.dma("sp", ("pbo", i % 2), x1[i * 128:(i + 1) * 128, :], x_[:])
            else:
                P.act(junk[:], x_[:], AF.Square, accum_out=col[:, 0:1])
                rms_scale(col[:, 1:2], col[:, 0:1], D)
                P.stt(x_[:], x_[:], col[:, 1:2], fnb[:], ALU.mult, ALU.mult)
                P.dma("sp", ("pbo", i % 2), yp[i * 128:(i + 1) * 128, :], x_[:], final_only=True)
        for half in range(2):
            ps = pj()
            for f in range(16):
                P.mm(ps[0:NS, :], ysT[:, f, :], wo[:, f, half * 512:(half + 1) * 512], start=(f == 0), stop=(f == 15))
            P.tt("dve", tmp[0:NS, :], ps[0:NS, :], grow[0:NS, half * 512:(half + 1) * 512], ALU.mult)
            P.tt("dve", xs[:, half * 512:(half + 1) * 512], xs[:, half * 512:(half + 1) * 512], tmp[0:NS, :], ALU.add)
        if last:
            P.act(junk[0:NS, :], xs[:], AF.Square, accum_out=col[0:NS, 0:1])
            rms_scale(col[0:NS, 1:2], col[0:NS, 0:1], D)
            P.stt(tmp2 := xt[0][0:NS, :], xs[:], col[0:NS, 1:2], fnb[0:NS, :], ALU.mult, ALU.mult)
            P.dma("sp", "yso", ys, tmp2, final_only=True)

    def mixers(layer):
        SEGL = SEG if layer == 0 else 1024
        W_in = ev_w_in if layer == 0 else od_w_in
        DVM = 128 if layer == 0 else 256
        wh = P.sb("wh%d" % layer, [128, 8, 512 if layer == 0 else 768], BF16); wsm = P.sb("wsm%d" % layer, [128, 8, 32], BF16)
        qT = P.sb("qT%d" % layer, [128, SEGL], BF16); qgT = P.sb("qgT%d" % layer, [128, SEGL], BF16); kT = P.sb("kT%d" % layer, [128, SEGL], BF16)
        khT = P.sb("khT%d" % layer, [128, SEGL if layer == 1 else 2], BF16); kT32 = P.sb("kT32%d" % layer, [128, SEGL if layer == 0 else 2])
        ktok = P.sb("ktok%d" % layer, [64, 16, 128], BF16); vtok = P.sb("vtok%d" % layer, [64, 16, DVM + 2], BF16)
        szw = P.sb("szw%d" % layer, [64, 16, DVM], BF16); bv = P.sb("bv%d" % layer, [64, 16, 128 if layer == 0 else 2], BF16)
        G0 = 64 if layer == 0 else 2
        decT = P.sb("decT%d" % layer, [64, 16, 64]); decS = P.sb("decS%d" % layer, [64, 8, G0]); TT = P.sb("TT%d" % layer, [64, 8, G0])
        Nn = P.sb("Nn%d" % layer, [64, 8, G0]); Mm = P.sb("Mm%d" % layer, [64, 8, G0]); Xx = P.sb("Xx%d" % layer, [64, 8, G0])
        tmpA = P.sb("tmpA%d" % layer, [128, SEGL]); tmpB = P.sb("tmpB%d" % layer, [128, SEGL]); tmpC = P.sb("tmpC%d" % layer, [128, SEGL])
        gamb = P.sb("gamb%d" % layer, [128, SEGL]); ubuf = P.sb("ubuf%d" % layer, [128, (SEGL + 8) if layer == 0 else 2])
        bht = P.sb("bht%d" % layer, [64, 16]); nbg = P.sb("nbg%d" % layer, [64, 16]); gtok = P.sb("gtok%d" % layer, [64, 16])
        Sf = [P.sb("Sf%d_%d" % (layer, i), [128, DVM + 2]) for i in range(4)]; Sb2 = [P.sb("Sb%d_%d" % (layer, i), [128, DVM + 2], BF16) for i in range(2)]
        Sb = Sb2[0]; So4 = [P.sb("So%d_%d" % (layer, i), [64, 130 if layer == 0 else 2]) for i in range(4)]; So = So4[0]
        PT2 = [P.sb("PT%d_%d" % (layer, i), [64, 64], BF16) for i in range(2)]; Rp = P.sb("Rp%d" % layer, [64, 128])
        ot = P.sb("ot%d" % layer, [64, 8, DVM], BF16); colg = P.sb("colg%d" % layer, [64, 32])
        patt2 = [B[0], B[1]]; po2 = [B[5], B[3]]; pst2 = [B[6], B[2]]; pk2 = [B[4][:, 128:256], B[1][:, 128:256]]
        junk = P.sb("mj%d" % layer, [64, 256], BF16); tz = P.sb("tz%d" % layer, [64, 256]); tz2 = P.sb("tz2_%d" % layer, [64, 128])
        t16f = P.sb("t16f%d" % layer, [16, 32]); t16b = P.sb("t16b%d" % layer, [16, 512], BF16)
        nwbc = P.sb("nwbc%d" % layer, [128, 256]); cst = P.sb("cst%d" % layer, [128, SEGL if layer == 1 else 2]); snt = P.sb("snt%d" % layer, [128, SEGL if layer == 1 else 2])
        G1 = 32 if layer == 0 else 1
        gates = P.sb("gates%d" % layer, [64, G1, 16]); gA = P.sb("gA%d" % layer, [64, G1, 8]); gB = P.sb("gB%d" % layer, [64, G1, 8]); gC = P.sb("gC%d" % layer, [64, G1, 8])
        gates_s = P.sb("gates_s%d" % layer, [1, 16, 16]); sA = P.sb("sA%d" % layer, [1, 16, 8]); sB = P.sb("sB%d" % layer, [1, 16, 8]); sC = P.sb("sC%d" % layer, [1, 16, 8])
        sD = P.sb("sD%d" % layer, [1, 16, 8]); sE = P.sb("sE%d" % layer, [1, 16, 8]); sm0 = P.sb("sm0%d" % layer, [1, 16, 8])
        small = P.sb("small%d" % layer, [128, 64]); fsc = P.sb("fsc%d" % layer, [64, 136])

        full = Seq(T, LCH, hT[:, :, :], False)
        NCS = SEGL // LCH
        segs = [Seq(SEGL, LCH, hT[:, :, s * SEGL:(s + 1) * SEGL], False, base=s * SEGL) for s in range(T // SEGL)]
        samp = Seq(NS, 1, hsT[:, :, :], True)

        def proj_fm(seq, wt, c0, M, evac):
            for (b0, b1) in seq.blocks:
                ps = pj()
                for c in range(8):
                    P.mm(ps[0:M, 0:b1 - b0], wt[:, c, c0:c0 + M], seq.hT[:, c, b0:b1], start=(c == 0), stop=(c == 7))
                evac(ps[0:M, 0:b1 - b0], b0, b1)

        def tm(seq, wt, c0, ncols, outs):
            if not seq.sample and not TM128:
                for ch in range(seq.nch):
                    ps = pj()
                    for c in range(8):
                        P.mm(ps[0:64, 0:ncols], seq.hT[:, c, ch * 64:(ch + 1) * 64], wt[:, c, c0:c0 + ncols], start=(c == 0), stop=(c == 7))
                    for dst, lo, hi, w, fn in outs:
                        fn(dst[0:64, ch, 0:w], ps[0:64, lo:hi])
            elif not seq.sample:
                for cp in range(seq.nch // 2):
                    ps = pj()
                    for c in range(8):
                        P.mm(ps[0:128, 0:ncols], seq.hT[:, c, cp * 128:(cp + 1) * 128], wt[:, c, c0:c0 + ncols], start=(c == 0), stop=(c == 7))
                    for half in range(2):
                        for dst, lo, hi, w, fn in outs:
                            fn(dst[0:64, 2 * cp + half, 0:w], ps[half * 64:(half + 1) * 64, lo:hi])
            else:
                ps = pj()
                for c in range(8):
                    P.mm(ps[0:NS, 0:ncols], seq.hT[:, c, 0:NS], wt[:, c, c0:c0 + ncols], start=(c == 0), stop=(c == 7))
                for dst, lo, hi, w, fn in outs:
                    isb = dst.dtype == BF16
                    t16 = t16b if isb else t16f
                    fn(t16[0:NS, 0:w], ps[0:NS, lo:hi])
                    k = scr_i[0] % 4; scr_i[0] += 1
                    scr = (scrB if isb else scrF)[k]
                    P.dma("sp", ("bo", isb, k), scr[0:NS, 0:w], t16[0:NS, 0:w])
                    P.dma("sp", ("bi", isb, k), dst[0:1, 0:NS, 0:w], scr[0:NS, 0:w].unsqueeze(0))

        def fcopy(eng="act"):
            return lambda o, i: P.copy(eng, o, i)

        def fgate(nw_lo, w, sig=False):
            def fn(o, i):
                r = i.shape[0]
                bp = i.base_partition()
                if sig and not EXPGATE:
                    P.act(tz[0:r, 0:w], i[:, 0:w], AF.Sigmoid)
                    P.act(tz[0:r, 128:128 + w], i[:, w:2 * w], AF.Silu)
                    P.tt("dve", tz[0:r, 0:w], tz[0:r, 0:w], tz[0:r, 128:128 + w], ALU.mult)
                    P.tt("dve", o, tz[0:r, 0:w], nwbc[0:r, nw_lo:nw_lo + w], ALU.mult)
                elif sig:
                    P.copy("act", tz2[0:r, 0:w], i[:, w:2 * w])
                    P.act(tz[0:r, 0:2 * w], i[:, 0:2 * w], AF.Exp, scale=-1.0)
                    P.tt("dve", tz2[0:r, 0:w], tz2[0:r, 0:w], nwbc[0:r, nw_lo:nw_lo + w], ALU.mult)
                    P.ts("dve", tz[0:r, 0:2 * w], tz[0:r, 0:2 * w], 1.0, ALU.add)
                    P.tt("dve", tz[0:r, 0:w], tz[0:r, 0:w], tz[0:r, w:2 * w], ALU.mult)
                    P.recip(tz[0:r, 0:w], tz[0:r, 0:w])
                    P.tt("dve", o, tz[0:r, 0:w], tz2[0:r, 0:w], ALU.mult)
                else:
                    P.act(tz[0:r, 0:w], i, AF.Silu)
                    P.tt("dve", o, tz[0:r, 0:w], nwbc[0:r, nw_lo:nw_lo + w], ALU.mult)
            return fn

        def decay(seq, g_tok, strict=False):
            L, nch, n = seq.L, seq.nch, seq.n
            P.tt("dve", v3(tmpB[0:L, 0:n], L), tri[0:L, 0:L].unsqueeze(1).broadcast_to([L, nch, L]),
                 g_tok.unsqueeze(2).broadcast_to([L, nch, L]), ALU.mult)
            for bi, (b0, b1) in enumerate(seq.blocks):
                P.mm(psb[bi][:, 0:b1 - b0], ones32[0:L, :], tmpB[0:L, b0:b1])
            P.mm(pa[0:L, 0:nch], tri[0:L, 0:L], g_tok)
            P.copy("act", bht[0:L, 0:nch], pa[0:L, 0:nch])
            for bi, (b0, b1) in enumerate(seq.blocks):
                c0, c1 = b0 // L, b1 // L
                pv = v3(psb[bi][0:L, 0:b1 - b0], L)
                bb = bht[0:L, c0:c1].unsqueeze(2).broadcast_to([L, c1 - c0, L])
                d = v3(tmpC[0:L, b0:b1], L)
                P.tt("dve", d, pv, bb, ALU.subtract)
                P.tt("dve", d, d, negU[0:L, 0:L].unsqueeze(1).broadcast_to([L, c1 - c0, L]), ALU.min)
                P.act(decT[0:L, c0:c1, 0:L], d, AF.Exp)
                if strict:
                    P.stt(d, pv, -1.0, bb, ALU.mult, ALU.add)
                    P.tt("dve", d, d, negLs[0:L, 0:L].unsqueeze(1).broadcast_to([L, c1 - c0, L]), ALU.min)
                    P.act(decS[0:L, c0:c1, 0:L], d, AF.Exp)
                P.act(gamb[:, b0:b1], psb[bi][:, 0:b1 - b0], AF.Exp)

        NB = 4

        def make_ktok(seq, dk, src, scale_fn):
            L = seq.L
            G = 512 // dk
            for g0 in range(0, seq.nch, G):
                g1 = min(seq.nch, g0 + G)
                for c in range(g0, g1):
                    P.tr(ptb[0:L, (c - g0) * dk:(c - g0 + 1) * dk], src[0:dk, c * L:(c + 1) * L], identb[0:dk, 0:dk])
                pv = ptb[0:L, 0:(g1 - g0) * dk].rearrange("p (c d) -> p c d", d=dk)
                if scale_fn is None:
                    P.copy("act", ktok[0:L, g0:g1, 0:dk], pv)
                else:
                    P.tt("dve", ktok[0:L, g0:g1, 0:dk], pv, scale_fn(g0, g1).unsqueeze(2).broadcast_to([L, g1 - g0, dk]), ALU.mult)

        GP = 8

        def post_rows(seq, c0, G, dv, f0, scale_ap):
            L = seq.L
            ov = ot[0:L, 0:G, 0:dv]
            P.tt("dve", ov, ov, scale_ap.unsqueeze(2).broadcast_to([L, G, dv]), ALU.mult)
            P.tt("dve", ov, ov, szw[0:L, c0:c0 + G, 0:dv], ALU.mult)
            nj = dv // 128
            LL = max(L, 2)
            for j in range(nj):
                for g in range(G):
                    o_ = (j * G + g) * LL
                    P.tr(ptb[:, o_:o_ + L], ot[0:L, g, j * 128:(j + 1) * 128], identb[0:L, 0:L])
            src = ptb[:, 0:nj * G * LL].rearrange("p (j g l) -> p j g l", j=nj, g=G)[:, :, :, 0:L]
            if seq.sample:
                P.copy("act", ysT[:, f0:f0 + nj, c0:c0 + G].unsqueeze(3), src)
            else:
                P.copy("act", yT[:, f0:f0 + nj, seq.base + c0 * L:seq.base + (c0 + G) * L].rearrange("p j (g l) -> p j g l", l=L), src)

        def post_std(seq, dv, f0):
            def fn(c0, G):
                L = seq.L
                rms_scale(colg[0:L, 8:8 + G], colg[0:L, 0:G], dv)
                post_rows(seq, c0, G, dv, f0, colg[0:L, 8:8 + G])
            return fn

        def chunk_loop(seq, dk, dv, dvp, wt_fn, eL_fn, post_fn, S_, gdn=False, st_in=None, st_out=None, den=False):
            L, n = seq.L, seq.nch

            def stA(c):
                cb = slice(c * L, (c + 1) * L)
                P.mm(patt2[c % 2][0:L, 0:L], kT[0:dk, cb], qT[0:dk, cb])
                P.tt("dve", PT2[c % 2][0:L, 0:L], patt2[c % 2][0:L, 0:L], wt_fn(c), ALU.mult)

            def stB(c):
                cb = slice(c * L, (c + 1) * L)
                if seq.sample:
                    S = Sf[c % NB]; Sb_ = Sb2[c % 2]
                    P.copy("act", Sb_[0:dk, 0:dvp], S[0:dk, 0:dvp])
                else:
                    S = S_; Sb_ = Sb2[0]
                pst_ = pst2[c % 2] if seq.sample else pst
                if gdn:
                    pk_ = pk2[c % 2] if seq.sample else pk
                    P.mm(pk_[0:L, 0:dv], kT[0:dk, cb], Sb_[0:dk, 0:dv])
                    if L == 1:
                        P.stt(vtok[0:L, c, 0:dv], pk_[0:L, 0:dv], nbg[0:L, c:c + 1], bv[0:L, c, 0:dv], ALU.mult, ALU.add)
                    else:
                        P.stt(Rp[0:L, 0:dv], pk_[0:L, 0:dv], nbg[0:L, c:c + 1], bv[0:L, c, 0:dv], ALU.mult, ALU.add)
                        P.mm(pu[0:L, 0:dv], TT[0:L, c, 0:L], Rp[0:L, 0:dv])
                        P.copy("act", vtok[0:L, c, 0:dv], pu[0:L, 0:dv])
                po_ = po2[c % 2]
                P.mm(po_[0:L, 0:dvp], qgT[0:dk, cb], Sb_[0:dk, 0:dvp], start=True, stop=False)
                P.mm(po_[0:L, 0:dvp], PT2[c % 2][0:L, 0:L], vtok[0:L, c, 0:dvp], start=False, stop=True)
                P.mm(pst_[0:dk, 0:dvp], ktok[0:L, c, 0:dk], vtok[0:L, c, 0:dvp])
                if not seq.sample:
                    P.stt(Sb_[0:dk, 0:dvp], S[0:dk, 0:dvp], eL_fn(c), pst_[0:dk, 0:dvp], ALU.mult, ALU.add)
                    P.stt(S[0:dk, 0:dvp], S[0:dk, 0:dvp], eL_fn(c), pst_[0:dk, 0:dvp], ALU.mult, ALU.add)
                else:
                    P.stt(S[0:dk, 0:dvp], S[0:dk, 0:dvp], eL_fn(c), pst_[0:dk, 0:dvp], ALU.mult, ALU.add)
                    st_out(c, S)
                g = c % GP
                P.copy("act", ot[0:L, g, 0:dv], po_[0:L, 0:dv])
                P.act(junk[0:L, 0:dv], po_[0:L, 0:dv], AF.Square, accum_out=colg[0:L, g:g + 1])
                if den:
                    P.act(colg[0:L, 16 + g:17 + g], po_[0:L, dv:dv + 1], AF.Abs)

            if seq.sample:
                for c in range(min(NB - 1, n)):
                    st_in(c, Sf[c % NB])
            stA(0)
            for c in range(n):
                if seq.sample and c + NB - 1 < n:
                    st_in(c + NB - 1, Sf[(c + NB - 1) % NB])
                if c + 1 < n:
                    stA(c + 1)
                stB(c)
                if c % GP == GP - 1:
                    post_fn(c - GP + 1, GP)

        def eL_gam(seq, dk):
            return lambda c: gamb[0:dk, c * seq.L + seq.L - 1:c * seq.L + seq.L]

        def kdec(seq):
            return lambda g0, g1: decT[0:seq.L, g0:g1, seq.L - 1]

        def load_nw(src, w):
            P.dma("sp", "nwbc", nwbc[:, 0:w], src[0:1, 0:w].broadcast_to([128, w]))

        def rot(ps, scale, b0, b1, out):
            n = b1 - b0
            P.stt(tmpA[:, 0:n], ps, scale, cst[:, b0:b1], ALU.mult, ALU.mult)
            P.stt(tmpB[0:64, 0:n], ps[64:128, :], scale, snt[64:128, b0:b1], ALU.mult, ALU.mult)
            P.stt(tmpB[64:128, 0:n], ps[0:64, :], scale, snt[0:64, b0:b1], ALU.mult, ALU.mult)
            P.tt("dve", out, tmpA[:, 0:n], tmpB[:, 0:n], ALU.add)

        if layer == 1:
            w2 = P.sb("glaw2", [16, 512]); nb2T = P.sb("nb2T", [128, 4]); glrT = P.sb("glrT", [16, SEGL]); eblc = P.sb("eblc", [128, 16])
            gcon = P.sb("gcon", [64, 16])
            P.dma("sp", "lw", w2[:], gla_w2)
            P.dma("sp", "lw", nb2T[:], gla_b2[0].rearrange("(h p) -> p h", p=128), allow_slow_non_contiguous=True)
            P.ts("dve", nb2T[:], nb2T[:], -1.0, ALU.mult)
            wload(wsm[:, :, 0:16], W_in[:, 2048:2064])
            load_nw(gla_nw, 256)
            for h in range(4):
                wload(wh[:, :, 0:128], W_in[:, h * 128:(h + 1) * 128]); wload(wh[:, :, 128:256], W_in[:, 512 + h * 128:512 + (h + 1) * 128])
                wload(wh[:, :, 256:512], W_in[:, 1024 + h * 256:1024 + (h + 1) * 256]); wload(wh[:, :, 512:768], W_in[:, 4112 + h * 256:4112 + (h + 1) * 256])
                S_ = Sf[0]
                P.memset("pool", S_[:, 0:256], 0.0); P.memset("pool", Sb[:, 0:256], 0.0)
                for seq in segs + [samp]:
                    L, n = seq.L, seq.n
                    proj_fm(seq, wsm, 0, 16, lambda ps, b0, b1: P.copy("act", glrT[0:16, b0:b1], ps))
                    for bi, (b0, b1) in enumerate(seq.blocks):
                        ps = psb[bi]
                        P.mm(ps[:, 0:b1 - b0], w2[0:16, h * 128:(h + 1) * 128], glrT[0:16, b0:b1])
                        P.act(tmpA[:, b0:b1], ps[:, 0:b1 - b0], AF.Exp, bias=nb2T[:, h:h + 1], scale=-1.0)
                        P.ts("dve", tmpA[:, b0:b1], tmpA[:, b0:b1], 1.0, ALU.add)
                        P.act(tmpA[:, b0:b1], tmpA[:, b0:b1], AF.Ln)
                        P.ts("dve", tmpC[:, b0:b1], tmpA[:, b0:b1], -1.0 / 16.0, ALU.mult)
                    if L > 1:
                        for c in range(seq.nch):
                            P.scan(tmpB[:, c * L:(c + 1) * L], ones32[:, 0:L], tmpC[:, c * L:(c + 1) * L], 0.0, ALU.mult, ALU.add)
                    else:
                        P.copy("dve", tmpB[:, 0:n], tmpC[:, 0:n])
                    P.act(gamb[:, 0:n], tmpB[:, 0:n], AF.Exp)
                    P.act(tmpC[:, 0:n], tmpB[:, 0:n], AF.Exp, scale=-1.0)
                    P.copy("dve", eblc[:, 0:seq.nch], v3(gamb[:, 0:n], L)[:, :, L - 1])
                    proj_fm(seq, wh, 0, 128, lambda ps, b0, b1: P.stt(qT[:, b0:b1], ps, 128 ** -0.5, gamb[:, b0:b1], ALU.mult, ALU.mult))
                    proj_fm(seq, wh, 128, 128, lambda ps, b0, b1: P.tt("dve", kT[:, b0:b1], ps, tmpC[:, b0:b1], ALU.mult))
                    P.tt("dve", v3(khT[:, 0:n], L), v3(kT[:, 0:n], L), eblc[:, 0:seq.nch].unsqueeze(2).broadcast_to([128, seq.nch, L]), ALU.mult)
                    make_ktok(seq, 128, khT, None)
                    tm(seq, wh, 256, 512, [(vtok, 0, 256, 256, fcopy()), (szw, 256, 512, 256, fgate(0, 256))])
                    _qg = qgT
                    chunk_loop_q = qT
                    P.copy("pool", qgT[:, 0:n], qT[:, 0:n])
                    chunk_loop(seq, 128, 256, 256, lambda c, L=L: tri[0:L, 0:L], lambda c: eblc[:, c:c + 1], post_std(seq, 256, h * 2), S_,
                               st_in=lambda c, S, h=h: P.dma("sp", ("sti", c % 2), S[:, 0:256], sgla[c, h]),
                               st_out=lambda c, S, h=h: P.dma("pool", ("sto", c % 2), s_gla[c, h], S[:, 0:256], final_only=True))
                    if seq is segs[-1]:
                        P.dma("sp", "pst", p_gla[h], S_[:, 0:256], final_only=True)
            load_nw(ret_nw, 256)
            for h in range(4):
                o = 2064
                wload(wh[:, :, 0:128], W_in[:, o + h * 128:o + (h + 1) * 128]); wload(wh[:, :, 128:256], W_in[:, 2576 + h * 128:2576 + (h + 1) * 128])
                wload(wh[:, :, 256:512], W_in[:, 3088 + h * 256:3088 + (h + 1) * 256]); wload(wh[:, :, 512:768], W_in[:, 5136 + h * 256:5136 + (h + 1) * 256])
                S_ = Sf[0]
                P.memset("pool", S_[:, 0:256], 0.0); P.memset("pool", Sb[:, 0:256], 0.0)
                P.memset("pool", gcon[:], math.log(1.0 - 2.0 ** (-5.0 - h)))
                for seq in segs + [samp]:
                    L, n = seq.L, seq.n
                    if seq.sample:
                        P.dma("sp", "cs", cst[:, 0:NS], c_cosS); P.dma("sp", "cs", snt[:, 0:NS], c_sinS)
                    else:
                        P.dma("sp", "cs", cst[:, :], c_cos[:, seq.base:seq.base + SEGL]); P.dma("sp", "cs", snt[:, :], c_sin[:, seq.base:seq.base + SEGL])
                    if seq.sample or seq is segs[0]:
                        decay(seq, gcon[0:L, 0:seq.nch])
                    proj_fm(seq, wh, 0, 128, lambda ps, b0, b1: rot(ps, 1.0, b0, b1, qT[:, b0:b1]))
                    proj_fm(seq, wh, 128, 128, lambda ps, b0, b1: rot(ps, 128 ** -0.5, b0, b1, kT[:, b0:b1]))
                    P.tt("dve", qgT[:, 0:n], qT[:, 0:n], gamb[:, 0:n], ALU.mult)
                    make_ktok(seq, 128, kT, kdec(seq))
                    tm(seq, wh, 256, 512, [(vtok, 0, 256, 256, fcopy()), (szw, 256, 512, 256, fgate(0, 256))])
                    chunk_loop(seq, 128, 256, 256, lambda c, L=L: decT[0:L, c, 0:L], eL_gam(seq, 128), post_std(seq, 256, 8 + h * 2), S_,
                               st_in=lambda c, S, h=h: P.dma("sp", ("sti", c % 2), S[:, 0:256], sret[c, h]),
                               st_out=lambda c, S, h=h: P.dma("pool", ("sto", c % 2), s_ret[c, h], S[:, 0:256], final_only=True))
                    if seq is segs[-1]:
                        P.dma("sp", "pst", p_ret[h], S_[:, 0:256], final_only=True)
            return

        sqb2 = [P.sb("sqb%d" % i, [128, SEGL], BF16) for i in range(2)]
        ubufK = P.sb("ubufK", [128, SEGL + 8]); ubufV = P.sb("ubufV", [128, SEGL + 8]); rawK = P.sb("rawK", [128, SEGL]); rawV = P.sb("rawV", [128, SEGL])
        rinvK = P.sb("rinvK", [128, SEGL]); bufT3 = P.sb("bufT3", [128, 3, 3, 16]); urow3 = P.sb("urow3", [16, 3, 128])
        bcst = P.sb("bcst", [128, 40]); cw = P.sb("cw", [128, 12]); pfx = P.sb("pfx", [128, 3, 3]); scv = P.sb("scv", [16, 3, 3, 128])
        gb8 = P.sb("gb8", [8, 2]); mblk = tmpC[0:8, 0:512]
        iT = tmpA[0:8, 0:512]; fTt = tmpB[0:8, 0:512]; mlast = P.sb("mlast", [8, 1]); mrow = P.sb("mrow", [1, 8])
        P.dma("sp", "lw", bcst[:, 0:8], a_log[0:1, :].broadcast_to([128, 8])); P.dma("sp", "lw", bcst[:, 8:16], dt_bias[0:1, :].broadcast_to([128, 8]))
        P.dma("sp", "lw", bcst[:, 16:32], gate_b[0:1, :].broadcast_to([128, 16]))
        P.dma("sp", "lw", gb8[:, 0:1], gate_b[0, 0:8].rearrange("(p o) -> p o", o=1), allow_slow_non_contiguous=True)
        P.dma("sp", "lw", gb8[:, 1:2], gate_b[0, 8:16].rearrange("(p o) -> p o", o=1), allow_slow_non_contiguous=True)
        P.act(bcst[:, 32:40], bcst[:, 0:8], AF.Exp)
        P.dma("sp", "scv", s_conv[:, 0:2, :], sconv[:, 1:3, :], final_only=True)
        wload(wsm[:, :, 0:16], W_in[:, 3072:3088])
        wload(wsm[:, :, 16:32], W_in[:, 5136:5152])
        tm(full, wsm, 0, 16, [(gates, 0, 16, 16, fcopy("dve"))])
        tm(samp, wsm, 0, 16, [(gates_s, 0, 16, 16, fcopy("dve"))])

        def gdn_gates(gt, r, nch, beta_o, nbeta_o, g_o):
            P.act(beta_o, gt[0:r, 0:nch, 0:8], AF.Sigmoid)
            P.ts("dve", nbeta_o, beta_o, -1.0, ALU.mult)
            P.tt("dve", g_o, gt[0:r, 0:nch, 8:16], bcst[0:r, 8:16].unsqueeze(1).broadcast_to([r, nch, 8]), ALU.add)
            P.act(g_o, g_o, AF.Exp)
            P.ts("dve", g_o, g_o, 1.0, ALU.add)
            P.act(g_o, g_o, AF.Ln)
            P.tt("dve", g_o, g_o, bcst[0:r, 32:40].unsqueeze(1).broadcast_to([r, nch, 8]), ALU.mult)
            P.ts("dve", g_o, g_o, -1.0, ALU.mult)
        gdn_gates(gates, 64, 32, gA[:], gB[:], gC[:])
        gdn_gates(gates_s, 1, 16, sA[:], sB[:], sC[:])
        load_nw(gdn_nw, 128)
        for h in range(8):
            for i in range(3):
                wload(wh[:, :, i * 128:(i + 1) * 128], W_in[:, i * 1024 + h * 128:i * 1024 + (h + 1) * 128])
                P.dma("sp", "cw", cw[:, i * 4:(i + 1) * 4], conv_w[:, i * 1024 + h * 128:i * 1024 + (h + 1) * 128].rearrange("j p -> p j"), allow_slow_non_contiguous=True)
                P.dma("sp", "scvl", scv[:, i, :, :], sconv[:, :, i * 1024 + h * 128:i * 1024 + (h + 1) * 128])
            wload(wh[:, :, 384:512], W_in[:, 6176 + h * 128:6176 + (h + 1) * 128])
            S_ = Sf[0]
            P.memset("pool", S_[:, 0:128], 0.0); P.memset("pool", Sb[:, 0:128], 0.0)
            for si, seq in enumerate(segs + [samp]):
                L, n, nch = seq.L, seq.n, seq.nch
                if seq.sample:
                    beta_h, nbeta_h, g_h = sA[0:1, :, h], sB[0:1, :, h], sC[0:1, :, h]
                else:
                    c0 = si * NCS
                    beta_h, nbeta_h, g_h = gA[:, c0:c0 + NCS, h], gB[:, c0:c0 + NCS, h], gC[:, c0:c0 + NCS, h]
                decay(seq, g_h, strict=(L > 1))
                def qkv_lockstep(seq=seq, si=si, L=L, n=n, nch=nch, beta_h=beta_h, h=h):
                    ub = [ubuf, ubufK, ubufV]; rw = [tmpA, rawK, rawV]; rv = [tmpC, rinvK]
                    for i in range(3):
                        proj_fm(seq, wh, i * 128, 128, lambda ps, b0, b1, i=i: P.copy("act", ub[i][:, 3 + b0:3 + b1], ps))
                    taps3 = []
                    for i in range(3):
                        if not seq.sample:
                            if si == 0:
                                P.memset("pool", ub[i][:, 0:3], 0.0)
                            else:
                                P.copy("pool", ub[i][:, 0:3], pfx[:, i, :])
                            taps3.append([ub[i][:, j:j + n] for j in range(4)])
                        else:
                            for j in range(3):
                                ps = pj()
                                P.tr(ps[:, 0:NS], scv[0:NS, i, j, :], ident[0:NS, 0:NS])
                                P.copy("act", bufT3[:, i, j, :], ps[:, 0:NS])
                            taps3.append([bufT3[:, i, 0, :], bufT3[:, i, 1, :], bufT3[:, i, 2, :], ub[i][:, 3:3 + NS]])
                            ps = pj()
                            P.tr(ps[0:NS, 0:128], ub[i][:, 3:3 + NS], ident[:])
                            P.copy("act", urow3[:, i, :], ps[0:NS, 0:128])
                            P.dma("sp", "uro", s_conv[:, 2, i * 1024 + h * 128:i * 1024 + (h + 1) * 128], urow3[:, i, :], final_only=True)
                    for i in range(3):
                        P.ts("dve", rw[i][:, 0:n], taps3[i][0], cw[:, i * 4:i * 4 + 1], ALU.mult)
                        for j in range(1, 4):
                            P.stt(rw[i][:, 0:n], taps3[i][j], cw[:, i * 4 + j:i * 4 + j + 1], rw[i][:, 0:n], ALU.mult, ALU.add)
                        if not seq.sample:
                            P.copy("pool", pfx[:, i, :], ub[i][:, n:n + 3])
                            if seq is segs[-1]:
                                P.dma("sp", "pcv", p_conv[:, i * 1024 + h * 128:i * 1024 + (h + 1) * 128].rearrange("j p -> p j"), ub[i][:, n:n + 3], allow_slow_non_contiguous=True, final_only=True)
                    for i in range(3):
                        P.act(rw[i][:, 0:n], rw[i][:, 0:n], AF.Silu)
                    for i in range(2):
                        P.act(sqb2[i][:, 0:n], rw[i][:, 0:n], AF.Square)
                    pss = []
                    for i in range(2):
                        ps = pj()
                        P.mm(ps[:, 0:n], onesb[:, :], sqb2[i][:, 0:n])
                        P.ts("dve", rv[i][:, 0:n], ps[:, 0:n], EPS, ALU.add)
                    for i in range(2):
                        P.act(rv[i][:, 0:n], rv[i][:, 0:n], AF.Ln)
                    for i in range(2):
                        P.act(rv[i][:, 0:n], rv[i][:, 0:n], AF.Exp, scale=-0.5)
                    P.stt(qT[:, 0:n], rw[0][:, 0:n], 128 ** -0.5, rv[0][:, 0:n], ALU.mult, ALU.mult)
                    P.tt("dve", kT32[:, 0:n], rw[1][:, 0:n], rv[1][:, 0:n], ALU.mult)
                    P.copy("act", kT[:, 0:n], kT32[:, 0:n])

                def vT_piece(seq=seq, L=L, nch=nch, beta_h=beta_h):
                    for c in range(nch):
                        ps = pj()
                        P.tr(ps[0:L, 0:128], rawV[:, c * L:(c + 1) * L], ident[:])
                        P.ts("dve", bv[0:L, c, :], ps[0:L, 0:128], beta_h[:, c:c + 1], ALU.mult)

                def misc_piece(seq=seq, L=L, n=n, nch=nch, nbeta_h=nbeta_h):
                    P.tt("dve", qgT[:, 0:n], qT[:, 0:n], gamb[:, 0:n], ALU.mult)
                    P.act(gtok[0:L, 0:nch], bht[0:L, 0:nch], AF.Exp)
                    P.tt("dve", nbg[0:L, 0:nch], gtok[0:L, 0:nch], nbeta_h, ALU.mult)
                    make_ktok(seq, 128, kT, kdec(seq))

                def z_piece(seq=seq):
                    tm(seq, wh, 384, 128, [(szw, 0, 128, 128, fgate(0, 128))])

                def doubling(L=L, nch=nch, nbeta_h=nbeta_h):
                    pX, pY, pZ = B[3], B[5], B[6]
                    for b in range(nch // 8):
                        for i in range(8):
                            c = b * 8 + i
                            P.mm(pX[0:64, i * 64:(i + 1) * 64], kT32[:, c * L:(c + 1) * L], kT32[:, c * L:(c + 1) * L])
                        P.tt("dve", Nn[:], v3(pX[0:64, :], 64), decS[:, b * 8:(b + 1) * 8, :], ALU.mult)
                        P.tt("dve", Nn[:], Nn[:], nbeta_h[:, b * 8:(b + 1) * 8].unsqueeze(2).broadcast_to([64, 8, 64]), ALU.mult)
                        for i in range(8):
                            P.tr(pY[0:64, i * 64:(i + 1) * 64], Nn[:, i, :], ident[0:64, 0:64])
                        P.copy("act", Mm[:], v3(pY[0:64, :], 64))
                        P.tt("dve", Xx[:], Mm[:], ident[0:64, 0:64].unsqueeze(1).broadcast_to([64, 8, 64]), ALU.add)
                        yield
                        for lvl in range(5):
                            for i in range(8):
                                P.mm(pX[0:64, i * 64:(i + 1) * 64], Mm[:, i, :], Nn[:, i, :])
                            if lvl < 4:
                                for i in range(8):
                                    P.mm(pY[0:64, i * 64:(i + 1) * 64], Nn[:, i, :], Mm[:, i, :])
                            P.copy("act", Nn[:], v3(pX[0:64, :], 64))
                            if lvl < 4:
                                P.copy("dve", Mm[:], v3(pY[0:64, :], 64))
                            for i in range(8):
                                P.mm(pZ[0:64, i * 64:(i + 1) * 64], Nn[:, i, :], Xx[:, i, :])
                            P.tt("dve", Xx[:], Xx[:], v3(pZ[0:64, :], 64), ALU.add)
                            if lvl < 4:
                                yield
                        P.copy("act", TT[:, b * 8:(b + 1) * 8, :], Xx[:])

                qkv_lockstep()
                pieces = [vT_piece, misc_piece, z_piece]
                if L > 1:
                    for _ in doubling():
                        if pieces:
                            pieces.pop(0)()
                while pieces:
                    pieces.pop(0)()
                chunk_loop(seq, 128, 128, 128, lambda c, L=L: decT[0:L, c, 0:L], eL_gam(seq, 128), post_std(seq, 128, h), S_, gdn=True,
                           st_in=lambda c, S, h=h: P.dma("sp", ("sti", c % 2), S[:, 0:128], sg[c, h]),
                           st_out=lambda c, S, h=h: P.dma("pool", ("sto", c % 2), s_gdn[c, h], S[:, 0:128], final_only=True))
                if seq is segs[-1]:
                    P.dma("sp", "pst", p_gdn[h], S_[:, 0:128], final_only=True)

        tm(full, wsm, 16, 16, [(gates, 0, 16, 16, lambda o, i: P.tt("dve", o, i, bcst[i.base_partition():i.base_partition() + i.shape[0], 16:32], ALU.add))])
        tm(samp, wsm, 16, 16, [(gates_s, 0, 16, 16, lambda o, i: P.tt("dve", o, i, bcst[i.base_partition():i.base_partition() + i.shape[0], 16:32], ALU.add))])

        def logsig(o, i):
            P.act(o, i, AF.Exp, scale=-1.0)
            P.ts("dve", o, o, 1.0, ALU.add)
            P.act(o, o, AF.Ln)
            P.ts("dve", o, o, -1.0, ALU.mult)
        logsig(gA[:], gates[:, :, 8:16])
        P.act(gB[:], gates[:, :, 0:8], AF.Exp); P.ts("dve", gB[:], gB[:], 0.125, ALU.mult)
        for bi in range(4):
            for (cofs, dst, gcol) in ((16, iT, 0), (24, fTt, 1)):
                ps = pj()
                for c in range(8):
                    P.mm(ps[0:8, :], wsm[:, c, cofs:cofs + 8], hT[:, c, bi * 512:(bi + 1) * 512], start=(c == 0), stop=(c == 7))
                P.ts("dve", dst, ps[0:8, :], gb8[:, gcol:gcol + 1], ALU.add)
            logsig(fTt, fTt)
            P.scan(mblk, fTt, iT, 0.0 if bi == 0 else mlast[:, 0:1], ALU.add, ALU.max)
            P.copy("dve", mlast[:], tmpC[0:8, 511:512])
        ps = pj()
        P.tr(ps[0:1, 0:8], mlast[0:8, 0:1], ident[0:8, 0:8])
        P.copy("act", mrow[:], ps[0:1, 0:8])
        P.dma("sp", "pmm", p_mm, mrow[:], final_only=True)
        P.act(mrow[:], mrow[:], AF.Exp, scale=-1.0)
        ps = pj()
        P.mm(ps[0:64, 0:8], ones32[0:1, 0:64], mrow[0:1, :])
        P.copy("act", fsc[:, 128:136], ps[0:64, 0:8])
        P.dma("sp", "sm0", sm0[:].rearrange("o s h -> o (s h)"), smm)
        logsig(sA[:], gates_s[:, :, 8:16])
        P.tt("dve", sB[:], gates_s[:, :, 0:8], sm0[:], ALU.subtract)
        P.act(sB[:], sB[:], AF.Exp); P.ts("dve", sB[:], sB[:], 0.125, ALU.mult)
        P.tt("dve", sC[:], sA[:], sm0[:], ALU.add)
        P.tt("dve", sC[:], sC[:], gates_s[:, :, 0:8], ALU.max)
        P.dma("sp", "smo", s_mm, sC[:].rearrange("o s h -> o (s h)"), final_only=True)
        P.tt("dve", sD[:], sm0[:], sC[:], ALU.subtract)
        P.act(sD[:], sD[:], AF.Exp)
        P.act(sE[:], sm0[:], AF.Exp, scale=-1.0)
        ps = pj()
        P.mm(ps[0:64, 0:128], ones32[0:1, 0:64], sD[:].rearrange("o s h -> o (s h)"))
        P.copy("act", fsc[:, 0:128], ps[0:64, 0:128])
        load_nw(ml_nw, 128)
        for h in range(8):
            wload(wh[:, :, 0:64], W_in[:, 3088 + h * 64:3088 + (h + 1) * 64]); wload(wh[:, :, 64:128], W_in[:, 3600 + h * 64:3600 + (h + 1) * 64])
            wload(wh[:, :, 128:256], W_in[:, 4112 + h * 128:4112 + (h + 1) * 128]); wload(wh[:, :, 256:384], W_in[:, 5152 + h * 128:5152 + (h + 1) * 128])
            wload(wh[:, :, 384:512], W_in[:, 7200 + h * 128:7200 + (h + 1) * 128])
            S_ = Sf[0]
            P.memset("pool", S_[:, 0:130], 0.0); P.memset("pool", Sb[:, 0:130], 0.0)
            for si, seq in enumerate(segs + [samp]):
                L, n, nch = seq.L, seq.n, seq.nch
                if seq.sample:
                    lf_h, cs_h = sA[0:1, :, h], sB[0:1, :, h]
                else:
                    c0 = si * NCS
                    lf_h, cs_h = gA[:, c0:c0 + NCS, h], gB[:, c0:c0 + NCS, h]
                decay(seq, lf_h)
                proj_fm(seq, wh, 0, 64, lambda ps, b0, b1: P.copy("act", qT[0:64, b0:b1], ps))
                proj_fm(seq, wh, 64, 64, lambda ps, b0, b1: P.copy("act", kT[0:64, b0:b1], ps))
                P.tt("dve", qgT[0:64, 0:n], qT[0:64, 0:n], gamb[0:64, 0:n], ALU.mult)
                P.tt("dve", decT[0:L, 0:nch, 0:L], decT[0:L, 0:nch, 0:L], cs_h.unsqueeze(2).broadcast_to([L, nch, L]), ALU.mult)
                make_ktok(seq, 64, kT, kdec(seq))
                tm(seq, wh, 128, 384, [(vtok, 0, 128, 128, fcopy()), (szw, 128, 384, 128, fgate(0, 128, sig=True))])
                P.memset("pool", vtok[0:L, 0:nch, 128:129], 1.0)

                def post_ml(c0, G, seq=seq, h=h, L=L):
                    dn = colg[0:L, 16:16 + G]
                    if seq.sample:
                        P.tt("dve", dn, dn, sE[0:1, c0:c0 + G, h], ALU.max)
                    else:
                        P.ts("dve", dn, dn, 1.0, ALU.max)
                    P.recip(dn, dn)
                    t_ = colg[0:L, 24:24 + G]
                    P.tt("dve", t_, dn, dn, ALU.mult)
                    P.tt("dve", t_, t_, colg[0:L, 0:G], ALU.mult)
                    rms_scale(colg[0:L, 8:8 + G], t_, 128)
                    P.tt("dve", colg[0:L, 8:8 + G], colg[0:L, 8:8 + G], dn, ALU.mult)
                    post_rows(seq, c0, G, 128, 8 + h, colg[0:L, 8:8 + G])

                def st_in(c, S, h=h):
                    P.dma("sp", ("sti", c % 2), S[0:64, 0:129], smcn[c, h])

                def st_out(c, S, h=h):
                    So_ = So4[c % 4]
                    P.ts("dve", So_[0:64, 0:129], S[0:64, 0:129], fsc[0:64, c * 8 + h:c * 8 + h + 1], ALU.mult)
                    P.dma("pool", ("sto", 0), s_mcn[c, h], So_[0:64, 0:129], final_only=True)
                chunk_loop(seq, 64, 128, 129, lambda c, L=L: decT[0:L, c, 0:L], eL_gam(seq, 64), post_ml, S_, st_in=st_in, st_out=st_out, den=True)
                if seq is segs[-1]:
                    P.ts("dve", So[0:64, 0:129], S_[0:64, 0:129], fsc[0:64, 128 + h:129 + h], ALU.mult)
                    P.dma("sp", "pst", p_mcn[h], So[0:64, 0:129], final_only=True)

    for layer in range(2):
        m = P.scope_begin()
        mod_phase(layer, m)
        phase0(layer, xp if layer == 0 else x1)
        P.scope_end(m)
        m = P.scope_begin()
        mixers(layer)
        P.scope_end(m)
        m = P.scope_begin()
        phaseB(layer, ev_w_out if layer == 0 else od_w_out, xp if layer == 0 else x1, layer == 1)
        P.scope_end(m)
    P.emit()
    P.close()
    return nc

_NC_CACHE = {}


def _consts():
    ident = np.eye(128, dtype=np.float32)
    ii = np.arange(64)
    tri = (ii[:, None] <= ii[None, :]).astype(np.float32)
    negU = np.where(ii[:, None] <= ii[None, :], 0.0, NEG).astype(np.float32)
    negLs = np.where(ii[None, :] < ii[:, None], 0.0, NEG).astype(np.float32)
    half = 64
    inv = (np.float32(10000.0) ** (-np.arange(half, dtype=np.float32) / np.float32(half))).astype(np.float32)

    def tabs(pos):
        ang = (pos.astype(np.float32)[None, :] * inv[:, None]).astype(np.float32)
        c = np.cos(ang).astype(np.float32); s = np.sin(ang).astype(np.float32)
        return np.concatenate([c, c], 0), np.concatenate([s, -s], 0)
    cosT, sinT = tabs(np.arange(T))
    cosS, sinS = tabs(np.full((NS,), 16384))
    sel = np.zeros((17, 128), np.float32); sel[16, :] = 1.0
    return dict(c_ident=ident, c_tri=tri, c_negU=negU, c_negLs=negLs, c_cos=np.ascontiguousarray(cosT), c_sin=np.ascontiguousarray(sinT),
                c_cosS=np.ascontiguousarray(cosS), c_sinS=np.ascontiguousarray(sinS), c_sel=sel)


def kernel(x_prompt, x_sample, c_prompt, c_sample, state_gdn, state_gdn_conv, state_mlstm_c, state_mlstm_n, state_mlstm_m,
           state_gla, state_ret, ada_w, ada_b, norm_w, ev_w_in, ev_w_out, gdn_conv_w, gdn_a_log, gdn_dt_bias, gdn_norm_w,
           mlstm_gate_b, mlstm_norm_w, od_w_in, od_w_out, gla_w2, gla_b2, gla_norm_w, ret_norm_w, final_norm_w):
    f = lambda a: np.ascontiguousarray(np.asarray(a, dtype=np.float32))
    if "nc" not in _NC_CACHE:
        _NC_CACHE["nc"] = build_nc()
    nc = _NC_CACHE["nc"]
    cst = _consts()
    shared = dict(ada_w=f(ada_w), ada_b=f(ada_b), norm_w=f(norm_w), ev_w_in=f(ev_w_in[0]), ev_w_out=f(ev_w_out[0]), conv_w=f(gdn_conv_w[0]),
                  a_log=f(gdn_a_log), dt_bias=f(gdn_dt_bias), gdn_nw=f(gdn_norm_w), gate_b=f(mlstm_gate_b), ml_nw=f(mlstm_norm_w),
                  od_w_in=f(od_w_in[0]), od_w_out=f(od_w_out[0]), gla_w2=f(gla_w2[0]), gla_b2=f(gla_b2), gla_nw=f(gla_norm_w), ret_nw=f(ret_norm_w),
                  fin_nw=f(np.asarray(final_norm_w).reshape(1, D)))
    shared.update(cst)
    in_maps = []
    for b in range(8):
        sl = slice(b * NS, (b + 1) * NS)
        m = dict(shared)
        m.update(xp=f(x_prompt[b]), xs=f(np.asarray(x_sample)[sl, 0]), cc=f(np.concatenate([np.asarray(c_sample)[sl], np.asarray(c_prompt)[b:b + 1]], 0)),
                 sg=f(state_gdn[0][sl]), sconv=f(state_gdn_conv[0][sl]), smcn=f(np.concatenate([np.asarray(state_mlstm_c[0][sl]), np.asarray(state_mlstm_n[0][sl])[..., None]], -1)),
                 smm=f(np.asarray(state_mlstm_m[0][sl]).reshape(1, 128)), sgla=f(state_gla[0][sl]), sret=f(state_ret[0][sl]))
        in_maps.append(m)
    res = run_bass_kernel_spmd(nc, in_maps, core_ids=list(range(8)))
    R = res.results
    st = lambda k: np.stack([np.asarray(r[k], dtype=np.float32) for r in R], 0)
    cat = lambda k: np.concatenate([np.asarray(r[k], dtype=np.float32) for r in R], 0)
    y_prompt = st("yp")
    y_sample = cat("ys").reshape(128, 1, D)
    outs = (y_prompt, y_sample,
            st("p_gdn")[None], st("p_conv")[None], st("p_mcn")[None][..., 0:128], st("p_mcn")[None][..., 128], st("p_mm").reshape(8, 8)[None],
            st("p_gla")[None], st("p_ret")[None],
            cat("s_gdn")[None], cat("s_conv")[None], cat("s_mcn")[None][..., 0:128], cat("s_mcn")[None][..., 128], cat("s_mm").reshape(128, 8)[None],
            cat("s_gla")[None], cat("s_ret")[None])
    return tuple(np.ascontiguousarray(o.astype(np.float32)) for o in outs)
```

```python
import numpy as np
import concourse.bass as bass
import concourse.mybir as mybir

F32 = mybir.dt.float32
BF16 = mybir.dt.bfloat16
ALU = mybir.AluOpType
AF = mybir.ActivationFunctionType
AX = mybir.AxisListType

ENGS = ("pe", "dve", "act", "pool", "sp")


class Prog:
    def __init__(self, nc, same_engine_sync=True):
        self.nc = nc
        self.q = {e: [] for e in ENGS}
        self.cnt = {e: 0 for e in ENGS}
        self.seen = {e: {} for e in ENGS}
        self.res = {}
        self.dcnt = {}
        self.sems = {}
        self._ctx = []
        self._semctx = []
        self.same = same_engine_sync
        self.out_events = []
        for e in ENGS:
            self._sem(("eng", e))

    def _enter(self, cm):
        v = cm.__enter__()
        self._ctx.append(cm)
        return v

    def _sem(self, key):
        if key not in self.sems:
            cm = self.nc.semaphore("s_" + "_".join(str(k) for k in key).replace("(", "").replace(")", "").replace(",", "_").replace(" ", "").replace("'", ""))
            self.sems[key] = cm.__enter__()
            self._semctx.append(cm)
        return self.sems[key]

    def sb(self, name, shape, dt=F32):
        return self._enter(self.nc.sbuf_tensor(name, list(shape), dt))

    def ps(self, name, shape, dt=F32):
        return self._enter(self.nc.psum_tensor(name, list(shape), dt))

    def scope_begin(self):
        return len(self._ctx)

    def barrier(self):
        evs = []
        for e in ENGS:
            if self.cnt[e] > 0:
                evs.append((("eng", e), self.cnt[e]))
        for key, c in self.dcnt.items():
            evs.append((("dma", key), 16 * c))
        for e in ENGS:
            waits = []
            for sk, val in evs:
                if sk == ("eng", e):
                    continue
                if self.seen[e].get(sk, 0) >= val:
                    continue
                self.seen[e][sk] = val
                waits.append((sk, val))
            if waits:
                self.q[e].append((waits, None, None, 0))

    def scope_end(self, marker):
        self.barrier()
        while len(self._ctx) > marker:
            self._ctx.pop().__exit__(None, None, None)

    def close(self):
        for cm in reversed(self._ctx):
            cm.__exit__(None, None, None)
        self._ctx = []
        for cm in reversed(self._semctx):
            cm.__exit__(None, None, None)
        self._semctx = []

    @staticmethod
    def _nm(ap):
        return ap.tensor.name

    def _deps(self, eng, reads, writes):
        waits = {}

        def need(ev):
            if ev is None:
                return
            sk, val, src = ev
            if src == eng and (eng == "pe" or not self.same) and sk == ("eng", eng):
                return
            if self.seen[eng].get(sk, 0) >= val:
                return
            if waits.get(sk, 0) < val:
                waits[sk] = val

        for r in reads:
            st = self.res.get(r)
            if st:
                need(st["w"])
        for w in writes:
            st = self.res.get(w)
            if st:
                need(st["w"])
                for ev in st["r"].values():
                    need(ev)
        for sk, val in waits.items():
            self.seen[eng][sk] = val
        return list(waits.items())

    def _commit(self, ev, reads, writes):
        for r in reads:
            st = self.res.setdefault(r, {"w": None, "r": {}})
            st["r"][ev[0]] = ev
        for w in writes:
            self.res[w] = {"w": ev, "r": {}}

    def op(self, eng, fn, reads, writes):
        rn = [r if isinstance(r, str) else self._nm(r) for r in reads if r is not None and not isinstance(r, (int, float))]
        wn = [w if isinstance(w, str) else self._nm(w) for w in writes]
        waits = self._deps(eng, rn, wn)
        self.cnt[eng] += 1
        sk = ("eng", eng)
        ev = (sk, self.cnt[eng], eng)
        self.q[eng].append((waits, fn, sk, 1))
        self._commit(ev, rn, wn)
        return ev

    def dma(self, eng, key, out, in_, final_only=False, **kw):
        key = (self._nm(out) + "__" + self._nm(in_)) if final_only else self._nm(out)
        sk = ("dma", key)
        self._sem(sk)
        rn = [self._nm(in_)]
        wn = [] if final_only else [self._nm(out)]
        waits = self._deps(eng, rn, wn)
        self.dcnt[key] = self.dcnt.get(key, 0) + 1
        ev = (sk, 16 * self.dcnt[key], "dma")
        self.q[eng].append((waits, lambda e: e.dma_start(out=out, in_=in_, **kw), sk, 16))
        self._commit(ev, rn, wn)
        return ev

    def mm(self, out, lhsT, rhs, start=True, stop=True, **kw):
        return self.op("pe", lambda e: e.matmul(out, lhsT, rhs, start=start, stop=stop, **kw), [lhsT, rhs], [out])

    def tr(self, out, in_, ident):
        return self.op("pe", lambda e: e.transpose(out, in_, ident), [in_, ident], [out])

    def tt(self, eng, out, in0, in1, op):
        return self.op(eng, lambda e: e.tensor_tensor(out, in0, in1, op), [in0, in1], [out])

    def ts(self, eng, out, in0, s1, op0, s2=None, op1=None, accum_out=None):
        rd = [in0] + [s for s in (s1, s2) if s is not None and not isinstance(s, (int, float))]
        wr = [out] + ([accum_out] if accum_out is not None else [])
        if op1 is None:
            return self.op(eng, lambda e: e.tensor_scalar(out, in0, s1, None, op0), rd, wr)
        if accum_out is None:
            return self.op(eng, lambda e: e.tensor_scalar(out, in0, s1, s2, op0, op1), rd, wr)
        return self.op(eng, lambda e: e.tensor_scalar(out, in0, s1, s2, op0, op1, accum_out), rd, wr)

    def stt(self, out, in0, scalar, in1, op0, op1, accum_out=None, eng="dve"):
        rd = [in0, in1] + ([scalar] if not isinstance(scalar, (int, float)) else [])
        wr = [out] + ([accum_out] if accum_out is not None else [])
        if accum_out is None:
            return self.op(eng, lambda e: e.scalar_tensor_tensor(out, in0, scalar, in1, op0, op1), rd, wr)
        return self.op(eng, lambda e: e.scalar_tensor_tensor(out, in0, scalar, in1, op0, op1, accum_out), rd, wr)

    def act(self, out, in_, func, bias=None, scale=None, accum_out=None):
        rd = [in_] + [s for s in (bias, scale) if s is not None and not isinstance(s, (int, float))]
        wr = [out] + ([accum_out] if accum_out is not None else [])
        kw = {}
        if bias is not None:
            kw["bias"] = bias
        if scale is not None:
            kw["scale"] = scale
        if accum_out is not None:
            kw["accum_out"] = accum_out
        return self.op("act", lambda e: e.activation(out, in_, func, **kw), rd, wr)

    def copy(self, eng, out, in_):
        if eng == "act":
            return self.op("act", lambda e: e.copy(out, in_), [in_], [out])
        return self.op(eng, lambda e: e.tensor_copy(out, in_), [in_], [out])

    def memset(self, eng, ap, val):
        return self.op(eng, lambda e: e.memset(ap, val), [], [ap])

    def recip(self, out, in_):
        return self.op("dve", lambda e: e.reciprocal(out, in_), [in_], [out])

    def scan(self, out, d0, d1, init, op0, op1):
        rd = [d0, d1] + ([init] if not isinstance(init, (int, float)) else [])
        return self.op("dve", lambda e: e.tensor_tensor_scan(out, d0, d1, init, op0, op1), rd, [out])

    def treduce(self, eng, out, in_, axis, op):
        return self.op(eng, lambda e: e.tensor_reduce(out, in_, axis, op), [in_], [out])

    def emit(self, final_waits_engine="sp"):
        nc = self.nc
        fin = []
        for key, c in self.dcnt.items():
            fin.append((("dma", key), 16 * c))
        for e in ENGS:
            if e != final_waits_engine and self.cnt[e] > 0:
                fin.append((("eng", e), self.cnt[e]))
        sems = self.sems
        q = self.q
        with nc.Block() as block:
            def run(ename, handle, extra=None):
                for waits, fn, sk, inc in q[ename]:
                    for wk, wv in waits:
                        handle.wait_ge(sems[wk], wv)
                    if fn is not None:
                        fn(handle).then_inc(sems[sk], inc)
                if extra:
                    for wk, wv in extra:
                        handle.wait_ge(sems[wk], wv)

            @block.sync
            def _(h):
                run("sp", h, fin if final_waits_engine == "sp" else None)

            @block.tensor
            def _(h):
                run("pe", h)

            @block.vector
            def _(h):
                run("dve", h)

            @block.scalar
            def _(h):
                run("act", h)

            @block.gpsimd
            def _(h):
                run("pool", h)

from concourse.bass_utils import run_bass_kernel_spmd

import math

D = 1024
T = 2048
NS = 16
LCH = 64
SEG = 512
EPS = 1e-6
NEG = -30000.0
import os
TM128 = os.environ.get('KV_TM128', '1') == '1'
EXPGATE = os.environ.get('KV_EXPGATE', '1') == '1'


class Seq:
    def __init__(self, n, L, hT, sample, base=0):
        self.n, self.L, self.hT, self.sample, self.base = n, L, hT, sample, base
        self.nch = n // L
        self.blocks = [(b, min(b + 512, n)) for b in range(0, n, 512)]


def v3(ap, L):
    return ap.rearrange("p (c l) -> p c l", l=L)


def build_nc():
    nc = bass.Bass("TRN2", target_bir_lowering=False)
    P = Prog(nc, same_engine_sync=True)

    def din(name, shape):
        return nc.dram_tensor(name, list(shape), F32, kind="ExternalInput").ap()

    def dout(name, shape):
        return nc.dram_tensor(name, list(shape), F32, kind="ExternalOutput").ap()

    xp = din("xp", [T, D]); xs_d = din("xs", [NS, D]); cc = din("cc", [17, D])
    sg = din("sg", [NS, 8, 128, 128]); sconv = din("sconv", [NS, 3, 3072]); smcn = din("smcn", [NS, 8, 64, 129])
    smm = din("smm", [1, 128]); sgla = din("sgla", [NS, 4, 128, 256]); sret = din("sret", [NS, 4, 128, 256])
    ada_w = din("ada_w", [2, D, 3072]); ada_b = din("ada_b", [2, 3072]); norm_w = din("norm_w", [2, D])
    ev_w_in = din("ev_w_in", [D, 8224]); ev_w_out = din("ev_w_out", [2048, D]); conv_w = din("conv_w", [4, 3072])
    a_log = din("a_log", [1, 8]); dt_bias = din("dt_bias", [1, 8]); gdn_nw = din("gdn_nw", [1, 128]); gate_b = din("gate_b", [1, 16])
    ml_nw = din("ml_nw", [1, 128]); od_w_in = din("od_w_in", [D, 6160]); od_w_out = din("od_w_out", [2048, D])
    gla_w2 = din("gla_w2", [16, 512]); gla_b2 = din("gla_b2", [1, 512]); gla_nw = din("gla_nw", [1, 256]); ret_nw = din("ret_nw", [1, 256])
    fin_nw = din("fin_nw", [1, D])
    c_ident = din("c_ident", [128, 128]); c_tri = din("c_tri", [64, 64]); c_negU = din("c_negU", [64, 64]); c_negLs = din("c_negLs", [64, 64])
    c_cos = din("c_cos", [128, T]); c_sin = din("c_sin", [128, T]); c_cosS = din("c_cosS", [128, NS]); c_sinS = din("c_sinS", [128, NS])
    c_sel = din("c_sel", [17, 128])

    yp = dout("yp", [T, D]); ys = dout("ys", [NS, D])
    p_gdn = dout("p_gdn", [8, 128, 128]); p_conv = dout("p_conv", [3, 3072]); p_mcn = dout("p_mcn", [8, 64, 129])
    p_mm = dout("p_mm", [1, 8]); p_gla = dout("p_gla", [4, 128, 256]); p_ret = dout("p_ret", [4, 128, 256])
    s_gdn = dout("s_gdn", [NS, 8, 128, 128]); s_conv = dout("s_conv", [NS, 3, 3072]); s_mcn = dout("s_mcn", [NS, 8, 64, 129])
    s_mm = dout("s_mm", [1, 128]); s_gla = dout("s_gla", [NS, 4, 128, 256]); s_ret = dout("s_ret", [NS, 4, 128, 256])
    x1 = nc.dram_tensor("x1scr", [T, D], F32).ap()
    scrF = [nc.dram_tensor("scrF%d" % i, [16, 512], F32).ap() for i in range(4)]
    scrB = [nc.dram_tensor("scrB%d" % i, [16, 512], BF16).ap() for i in range(4)]
    scr_i = [0]

    ident = P.sb("ident", [128, 128]); identb = P.sb("identb", [128, 128], BF16); ones32 = P.sb("ones32", [128, 128]); onesb = P.sb("onesb", [128, 128], BF16)
    tri = P.sb("tri", [64, 64]); negU = P.sb("negU", [64, 64]); negLs = P.sb("negLs", [64, 64]); sel = P.sb("sel", [17, 128])
    hT = P.sb("hT", [128, 8, T], BF16); hsT = P.sb("hsT", [128, 8, NS], BF16)
    yT = P.sb("yT", [128, 16, T], BF16); ysT = P.sb("ysT", [128, 16, NS], BF16)
    xs = P.sb("xs_sb", [NS, D]); grow = P.sb("grow", [17, D]); modh = [None]
    wpT = P.sb("wpT", [128, 8]); shT = P.sb("shT", [128, 8]); nwT = P.sb("nwT", [128, 8])
    B = [P.ps("B%d" % i, [128, 512]) for i in range(7)]
    ptb = P.ps("ptb", [128, 1024], BF16)
    pj_i = [0]

    def pj():
        pj_i[0] ^= 1
        return B[pj_i[0]]

    psb = [B[2], B[3]]
    pm = B[4]; po = B[5]; pst = B[6]
    pk = pm[:, 128:256]; pu = pm[:, 256:384]; pa = pm[:, 384:448]

    for t_, d_ in ((ident, c_ident), (tri, c_tri), (negU, c_negU), (negLs, c_negLs), (sel, c_sel), (xs, xs_d)):
        P.dma("sp", "c0", t_[:], d_)
    P.copy("dve", identb[:], ident[:])
    P.memset("pool", ones32[:], 1.0)
    P.memset("pool", onesb[:], 1.0)

    def rms_scale(out_col, ssq_col, n):
        P.ts("dve", out_col, ssq_col, 1.0 / n, ALU.mult, EPS, ALU.add)
        P.act(out_col, out_col, AF.Ln)
        P.act(out_col, out_col, AF.Exp, scale=-0.5)

    def wload(dst, src2d):
        P.dma("pool", "w_" + dst.tensor.name, dst, src2d.rearrange("(c p) n -> p c n", p=128))

    def mod_phase(layer, sc):
        mod = P.sb("mod%d" % layer, [17, 3072]); modh[0] = mod
        cs_ = P.sb("cs%d" % layer, [17, D]); csT = P.sb("csT%d" % layer, [128, 8, 17], BF16)
        wa = P.sb("wa%d" % layer, [128, 8, 512], BF16); ab = P.sb("ab%d" % layer, [17, 512]); modT = P.sb("modT%d" % layer, [128, 16, 17])
        P.dma("sp", "cc", cs_[:], cc)
        P.act(cs_[:], cs_[:], AF.Silu)
        for j in range(8):
            P.tr(B[0][:, 0:17], cs_[0:17, j * 128:(j + 1) * 128], ident[0:17, 0:17])
            P.copy("dve", csT[:, j, :], B[0][:, 0:17])
        for blk in range(6):
            wload(wa[:], ada_w[layer][:, blk * 512:(blk + 1) * 512])
            P.dma("sp", "ab", ab[:], ada_b[layer:layer + 1, blk * 512:(blk + 1) * 512].broadcast_to([17, 512]))
            ps = pj()
            for c in range(8):
                P.mm(ps[0:17, :], csT[:, c, :], wa[:, c, :], start=(c == 0), stop=(c == 7))
            P.tt("dve", mod[0:17, blk * 512:(blk + 1) * 512], ps[0:17, :], ab[:], ALU.add)
        for j in range(16):
            P.tr(B[0][:, 0:17], mod[0:17, j * 128:(j + 1) * 128], ident[0:17, 0:17])
            P.copy("dve", modT[:, j, :], B[0][:, 0:17])
        P.dma("sp", "nwT", nwT[:], norm_w[layer].rearrange("(c p) -> p c", p=128), allow_slow_non_contiguous=True)
        P.stt(wpT[:], modT[:, 8:16, 16], 1.0, nwT[:], ALU.add, ALU.mult)
        P.copy("dve", shT[:], modT[:, 0:8, 16])
        P.copy("dve", grow[:], mod[0:17, 2048:3072])

    def phase0(layer, src):
        xt = [P.sb("p0x%d_%d" % (layer, i), [128, D]) for i in range(2)]
        junk = P.sb("p0j%d" % layer, [128, D], BF16); col = P.sb("p0c%d" % layer, [128, 2]); nwb = P.sb("p0n%d" % layer, [NS, D])
        hs = P.sb("p0h%d" % layer, [NS, D])
        for i in range(16):
            x_ = xt[i % 2]
            P.dma("sp", ("p0", i % 2), x_[:], src[i * 128:(i + 1) * 128, :])
            P.act(junk[:], x_[:], AF.Square, accum_out=col[:, 0:1])
            rms_scale(col[:, 1:2], col[:, 0:1], D)
            P.ts("dve", x_[:], x_[:], col[:, 1:2], ALU.mult)
            for j in range(8):
                ps = pj()
                P.tr(ps[:, 0:128], x_[:, j * 128:(j + 1) * 128], ident[:])
                if j % 2:
                    P.ts("dve", hT[:, j, i * 128:(i + 1) * 128], ps[:, 0:128], wpT[:, j:j + 1], ALU.mult, shT[:, j:j + 1], ALU.add)
                else:
                    P.act(hT[:, j, i * 128:(i + 1) * 128], ps[:, 0:128], AF.Identity, bias=shT[:, j:j + 1], scale=wpT[:, j:j + 1])
        P.dma("sp", "p0n", nwb[:], norm_w[layer:layer + 1, :].broadcast_to([NS, D]))
        P.act(junk[0:NS, :], xs[:], AF.Square, accum_out=col[0:NS, 0:1])
        rms_scale(col[0:NS, 1:2], col[0:NS, 0:1], D)
        P.ts("dve", hs[:], xs[:], col[0:NS, 1:2], ALU.mult)
        P.tt("dve", hs[:], hs[:], nwb[:], ALU.mult)
        mod = modh[0]
        P.ts("dve", nwb[:], mod[0:NS, 1024:2048], 1.0, ALU.add)
        P.tt("dve", hs[:], hs[:], nwb[:], ALU.mult)
        P.tt("dve", hs[:], hs[:], mod[0:NS, 0:1024], ALU.add)
        for j in range(8):
            ps = pj()
            P.tr(ps[:, 0:NS], hs[0:NS, j * 128:(j + 1) * 128], ident[0:NS, 0:NS])
            P.copy("dve", hsT[:, j, :], ps[:, 0:NS])

    def phaseB(layer, w_out, src, last):
        wo = P.sb("wo%d" % layer, [128, 16, D], BF16)
        xt = [P.sb("pbx%d_%d" % (layer, i), [128, D]) for i in range(2)]
        tmp = P.sb("pbt%d" % layer, [128, 512]); junk = P.sb("pbj%d" % layer, [128, D], BF16); col = P.sb("pbc%d" % layer, [128, 2])
        fnb = P.sb("pbf%d" % layer, [128, D]); gate_bc = P.sb("gate_bc%d" % layer, [128, D])
        for half in range(2):
            ps = pj()
            P.mm(ps[:, :], sel[0:17, :], grow[0:17, half * 512:(half + 1) * 512])
            P.copy("act", gate_bc[:, half * 512:(half + 1) * 512], ps[:, :])
        for q4 in range(4):
            P.dma("pool", ("wo", q4), wo[:, q4 * 4:(q4 + 1) * 4, :], w_out[q4 * 512:(q4 + 1) * 512, :].rearrange("(c p) n -> p c n", p=128))
        if last:
            P.dma("sp", "fnb", fnb[:], fin_nw[0:1, :].broadcast_to([128, D]))
        for i in range(16):
            x_ = xt[i % 2]
            P.dma("sp", ("pb", i % 2), x_[:], src[i * 128:(i + 1) * 128, :])
            for half in range(2):
                ps = pj()
                for f in range(16):
                    P.mm(ps[:, :], yT[:, f, i * 128:(i + 1) * 128], wo[:, f, half * 512:(half + 1) * 512], start=(f == 0), stop=(f == 15))
                P.tt("dve", tmp[:], ps[:, :], gate_bc[:, half * 512:(half + 1) * 512], ALU.mult)
                P.tt("dve", x_[:, half * 512:(half + 1) * 512], x_[:, half * 512:(half + 1) * 512], tmp[:], ALU.add)
            if not last:
                P.dma("sp", ("pbo", i % 2), x1[i * 128:(i + 1) * 128, :], x_[:])
            else:
                P.act(junk[:], x_[:], AF.Square, accum_out=col[:, 0:1])
                rms_scale(col[:, 1:2], col[:, 0:1], D)
                P.stt(x_[:], x_[:], col[:, 1:2], fnb[:], ALU.mult, ALU.mult)
                P.dma("sp", ("pbo", i % 2), yp[i * 128:(i + 1) * 128, :], x_[:], final_only=True)
        for half in range(2):
            ps = pj()
            for f in range(16):
                P.mm(ps[0:NS, :], ysT[:, f, :], wo[:, f, half * 512:(half + 1) * 512], start=(f == 0), stop=(f == 15))
            P.tt("dve", tmp[0:NS, :], ps[0:NS, :], grow[0:NS, half * 512:(half + 1) * 512], ALU.mult)
            P.tt("dve", xs[:, half * 512:(half + 1) * 512], xs[:, half * 512:(half + 1) * 512], tmp[0:NS, :], ALU.add)
        if last:
            P.act(junk[0:NS, :], xs[:], AF.Square, accum_out=col[0:NS, 0:1])
            rms_scale(col[0:NS, 1:2], col[0:NS, 0:1], D)
            P.stt(tmp2 := xt[0][0:NS, :], xs[:], col[0:NS, 1:2], fnb[0:NS, :], ALU.mult, ALU.mult)
            P.dma("sp", "yso", ys, tmp2, final_only=True)

    def mixers(layer):
        SEGL = SEG if layer == 0 else 1024
        W_in = ev_w_in if layer == 0 else od_w_in
        DVM = 128 if layer == 0 else 256
        wh = P.sb("wh%d" % layer, [128, 8, 512 if layer == 0 else 768], BF16); wsm = P.sb("wsm%d" % layer, [128, 8, 32], BF16)
        qT = P.sb("qT%d" % layer, [128, SEGL], BF16); qgT = P.sb("qgT%d" % layer, [128, SEGL], BF16); kT = P.sb("kT%d" % layer, [128, SEGL], BF16)
        khT = P.sb("khT%d" % layer, [128, SEGL if layer == 1 else 2], BF16); kT32 = P.sb("kT32%d" % layer, [128, SEGL if layer == 0 else 2])
        ktok = P.sb("ktok%d" % layer, [64, 16, 128], BF16); vtok = P.sb("vtok%d" % layer, [64, 16, DVM + 2], BF16)
        szw = P.sb("szw%d" % layer, [64, 16, DVM], BF16); bv = P.sb("bv%d" % layer, [64, 16, 128 if layer == 0 else 2], BF16)
        G0 = 64 if layer == 0 else 2
        decT = P.sb("decT%d" % layer, [64, 16, 64]); decS = P.sb("decS%d" % layer, [64, 8, G0]); TT = P.sb("TT%d" % layer, [64, 8, G0])
        Nn = P.sb("Nn%d" % layer, [64, 8, G0]); Mm = P.sb("Mm%d" % layer, [64, 8, G0]); Xx = P.sb("Xx%d" % layer, [64, 8, G0])
        tmpA = P.sb("tmpA%d" % layer, [128, SEGL]); tmpB = P.sb("tmpB%d" % layer, [128, SEGL]); tmpC = P.sb("tmpC%d" % layer, [128, SEGL])
        gamb = P.sb("gamb%d" % layer, [128, SEGL]); ubuf = P.sb("ubuf%d" % layer, [128, (SEGL + 8) if layer == 0 else 2])
        bht = P.sb("bht%d" % layer, [64, 16]); nbg = P.sb("nbg%d" % layer, [64, 16]); gtok = P.sb("gtok%d" % layer, [64, 16])
        Sf = [P.sb("Sf%d_%d" % (layer, i), [128, DVM + 2]) for i in range(4)]; Sb2 = [P.sb("Sb%d_%d" % (layer, i), [128, DVM + 2], BF16) for i in range(2)]
        Sb = Sb2[0]; So4 = [P.sb("So%d_%d" % (layer, i), [64, 130 if layer == 0 else 2]) for i in range(4)]; So = So4[0]
        PT2 = [P.sb("PT%d_%d" % (layer, i), [64, 64], BF16) for i in range(2)]; Rp = P.sb("Rp%d" % layer, [64, 128])
        ot = P.sb("ot%d" % layer, [64, 8, DVM], BF16); colg = P.sb("colg%d" % layer, [64, 32])
        patt2 = [B[0], B[1]]; po2 = [B[5], B[3]]; pst2 = [B[6], B[2]]; pk2 = [B[4][:, 128:256], B[1][:, 128:256]]
        junk = P.sb("mj%d" % layer, [64, 256], BF16); tz = P.sb("tz%d" % layer, [64, 256]); tz2 = P.sb("tz2_%d" % layer, [64, 128])
        t16f = P.sb("t16f%d" % layer, [16, 32]); t16b = P.sb("t16b%d" % layer, [16, 512], BF16)
        nwbc = P.sb("nwbc%d" % layer, [128, 256]); cst = P.sb("cst%d" % layer, [128, SEGL if layer == 1 else 2]); snt = P.sb("snt%d" % layer, [128, SEGL if layer == 1 else 2])
        G1 = 32 if layer == 0 else 1
        gates = P.sb("gates%d" % layer, [64, G1, 16]); gA = P.sb("gA%d" % layer, [64, G1, 8]); gB = P.sb("gB%d" % layer, [64, G1, 8]); gC = P.sb("gC%d" % layer, [64, G1, 8])
        gates_s = P.sb("gates_s%d" % layer, [1, 16, 16]); sA = P.sb("sA%d" % layer, [1, 16, 8]); sB = P.sb("sB%d" % layer, [1, 16, 8]); sC = P.sb("sC%d" % layer, [1, 16, 8])
        sD = P.sb("sD%d" % layer, [1, 16, 8]); sE = P.sb("sE%d" % layer, [1, 16, 8]); sm0 = P.sb("sm0%d" % layer, [1, 16, 8])
        small = P.sb("small%d" % layer, [128, 64]); fsc = P.sb("fsc%d" % layer, [64, 136])

        full = Seq(T, LCH, hT[:, :, :], False)
        NCS = SEGL // LCH
        segs = [Seq(SEGL, LCH, hT[:, :, s * SEGL:(s + 1) * SEGL], False, base=s * SEGL) for s in range(T // SEGL)]
        samp = Seq(NS, 1, hsT[:, :, :], True)

        def proj_fm(seq, wt, c0, M, evac):
            for (b0, b1) in seq.blocks:
                ps = pj()
                for c in range(8):
                    P.mm(ps[0:M, 0:b1 - b0], wt[:, c, c0:c0 + M], seq.hT[:, c, b0:b1], start=(c == 0), stop=(c == 7))
                evac(ps[0:M, 0:b1 - b0], b0, b1)

        def tm(seq, wt, c0, ncols, outs):
            if not seq.sample and not TM128:
                for ch in range(seq.nch):
                    ps = pj()
                    for c in range(8):
                        P.mm(ps[0:64, 0:ncols], seq.hT[:, c, ch * 64:(ch + 1) * 64], wt[:, c, c0:c0 + ncols], start=(c == 0), stop=(c == 7))
                    for dst, lo, hi, w, fn in outs:
                        fn(dst[0:64, ch, 0:w], ps[0:64, lo:hi])
            elif not seq.sample:
                for cp in range(seq.nch // 2):
                    ps = pj()
                    for c in range(8):
                        P.mm(ps[0:128, 0:ncols], seq.hT[:, c, cp * 128:(cp + 1) * 128], wt[:, c, c0:c0 + ncols], start=(c == 0), stop=(c == 7))
                    for half in range(2):
                        for dst, lo, hi, w, fn in outs:
                            fn(dst[0:64, 2 * cp + half, 0:w], ps[half * 64:(half + 1) * 64, lo:hi])
            else:
                ps = pj()
                for c in range(8):
                    P.mm(ps[0:NS, 0:ncols], seq.hT[:, c, 0:NS], wt[:, c, c0:c0 + ncols], start=(c == 0), stop=(c == 7))
                for dst, lo, hi, w, fn in outs:
                    isb = dst.dtype == BF16
                    t16 = t16b if isb else t16f
                    fn(t16[0:NS, 0:w], ps[0:NS, lo:hi])
                    k = scr_i[0] % 4; scr_i[0] += 1
                    scr = (scrB if isb else scrF)[k]
                    P.dma("sp", ("bo", isb, k), scr[0:NS, 0:w], t16[0:NS, 0:w])
                    P.dma("sp", ("bi", isb, k), dst[0:1, 0:NS, 0:w], scr[0:NS, 0:w].unsqueeze(0))

        def fcopy(eng="act"):
            return lambda o, i: P.copy(eng, o, i)

        def fgate(nw_lo, w, sig=False):
            def fn(o, i):
                r = i.shape[0]
                bp = i.base_partition()
                if sig and not EXPGATE:
                    P.act(tz[0:r, 0:w], i[:, 0:w], AF.Sigmoid)
                    P.act(tz[0:r, 128:128 + w], i[:, w:2 * w], AF.Silu)
                    P.tt("dve", tz[0:r, 0:w], tz[0:r, 0:w], tz[0:r, 128:128 + w], ALU.mult)
                    P.tt("dve", o, tz[0:r, 0:w], nwbc[0:r, nw_lo:nw_lo + w], ALU.mult)
                elif sig:
                    P.copy("act", tz2[0:r, 0:w], i[:, w:2 * w])
                    P.act(tz[0:r, 0:2 * w], i[:, 0:2 * w], AF.Exp, scale=-1.0)
                    P.tt("dve", tz2[0:r, 0:w], tz2[0:r, 0:w], nwbc[0:r, nw_lo:nw_lo + w], ALU.mult)
                    P.ts("dve", tz[0:r, 0:2 * w], tz[0:r, 0:2 * w], 1.0, ALU.add)
                    P.tt("dve", tz[0:r, 0:w], tz[0:r, 0:w], tz[0:r, w:2 * w], ALU.mult)
                    P.recip(tz[0:r, 0:w], tz[0:r, 0:w])
                    P.tt("dve", o, tz[0:r, 0:w], tz2[0:r, 0:w], ALU.mult)
                else:
                    P.act(tz[0:r, 0:w], i, AF.Silu)
                    P.tt("dve", o, tz[0:r, 0:w], nwbc[0:r, nw_lo:nw_lo + w], ALU.mult)
            return fn

        def decay(seq, g_tok, strict=False):
            L, nch, n = seq.L, seq.nch, seq.n
            P.tt("dve", v3(tmpB[0:L, 0:n], L), tri[0:L, 0:L].unsqueeze(1).broadcast_to([L, nch, L]),
                 g_tok.unsqueeze(2).broadcast_to([L, nch, L]), ALU.mult)
            for bi, (b0, b1) in enumerate(seq.blocks):
                P.mm(psb[bi][:, 0:b1 - b0], ones32[0:L, :], tmpB[0:L, b0:b1])
            P.mm(pa[0:L, 0:nch], tri[0:L, 0:L], g_tok)
            P.copy("act", bht[0:L, 0:nch], pa[0:L, 0:nch])
            for bi, (b0, b1) in enumerate(seq.blocks):
                c0, c1 = b0 // L, b1 // L
                pv = v3(psb[bi][0:L, 0:b1 - b0], L)
                bb = bht[0:L, c0:c1].unsqueeze(2).broadcast_to([L, c1 - c0, L])
                d = v3(tmpC[0:L, b0:b1], L)
                P.tt("dve", d, pv, bb, ALU.subtract)
                P.tt("dve", d, d, negU[0:L, 0:L].unsqueeze(1).broadcast_to([L, c1 - c0, L]), ALU.min)
                P.act(decT[0:L, c0:c1, 0:L], d, AF.Exp)
                if strict:
                    P.stt(d, pv, -1.0, bb, ALU.mult, ALU.add)
                    P.tt("dve", d, d, negLs[0:L, 0:L].unsqueeze(1).broadcast_to([L, c1 - c0, L]), ALU.min)
                    P.act(decS[0:L, c0:c1, 0:L], d, AF.Exp)
                P.act(gamb[:, b0:b1], psb[bi][:, 0:b1 - b0], AF.Exp)

        NB = 4

        def make_ktok(seq, dk, src, scale_fn):
            L = seq.L
            G = 512 // dk
            for g0 in range(0, seq.nch, G):
                g1 = min(seq.nch, g0 + G)
                for c in range(g0, g1):
                    P.tr(ptb[0:L, (c - g0) * dk:(c - g0 + 1) * dk], src[0:dk, c * L:(c + 1) * L], identb[0:dk, 0:dk])
                pv = ptb[0:L, 0:(g1 - g0) * dk].rearrange("p (c d) -> p c d", d=dk)
                if scale_fn is None:
                    P.copy("act", ktok[0:L, g0:g1, 0:dk], pv)
                else:
                    P.tt("dve", ktok[0:L, g0:g1, 0:dk], pv, scale_fn(g0, g1).unsqueeze(2).broadcast_to([L, g1 - g0, dk]), ALU.mult)

        GP = 8

        def post_rows(seq, c0, G, dv, f0, scale_ap):
            L = seq.L
            ov = ot[0:L, 0:G, 0:dv]
            P.tt("dve", ov, ov, scale_ap.unsqueeze(2).broadcast_to([L, G, dv]), ALU.mult)
            P.tt("dve", ov, ov, szw[0:L, c0:c0 + G, 0:dv], ALU.mult)
            nj = dv // 128
            LL = max(L, 2)
            for j in range(nj):
                for g in range(G):
                    o_ = (j * G + g) * LL
                    P.tr(ptb[:, o_:o_ + L], ot[0:L, g, j * 128:(j + 1) * 128], identb[0:L, 0:L])
            src = ptb[:, 0:nj * G * LL].rearrange("p (j g l) -> p j g l", j=nj, g=G)[:, :, :, 0:L]
            if seq.sample:
                P.copy("act", ysT[:, f0:f0 + nj, c0:c0 + G].unsqueeze(3), src)
            else:
                P.copy("act", yT[:, f0:f0 + nj, seq.base + c0 * L:seq.base + (c0 + G) * L].rearrange("p j (g l) -> p j g l", l=L), src)

        def post_std(seq, dv, f0):
            def fn(c0, G):
                L = seq.L
                rms_scale(colg[0:L, 8:8 + G], colg[0:L, 0:G], dv)
                post_rows(seq, c0, G, dv, f0, colg[0:L, 8:8 + G])
            return fn

        def chunk_loop(seq, dk, dv, dvp, wt_fn, eL_fn, post_fn, S_, gdn=False, st_in=None, st_out=None, den=False):
            L, n = seq.L, seq.nch

            def stA(c):
                cb = slice(c * L, (c + 1) * L)
                P.mm(patt2[c % 2][0:L, 0:L], kT[0:dk, cb], qT[0:dk, cb])
                P.tt("dve", PT2[c % 2][0:L, 0:L], patt2[c % 2][0:L, 0:L], wt_fn(c), ALU.mult)
                if fast:
                    P.mm(pst2[c % 2][0:dk, 0:dvp], ktok[0:L, c, 0:dk], vtok[0:L, c, 0:dvp])

            fast = (not gdn) and (not seq.sample)

            def stB_fast(c):
                cb = slice(c * L, (c + 1) * L)
                Sb_r = Sb2[c % 2]; Sb_w = Sb2[(c + 1) % 2]
                po_ = po2[c % 2]
                P.mm(po_[0:L, 0:dvp], qgT[0:dk, cb], Sb_r[0:dk, 0:dvp], start=True, stop=False)
                P.mm(po_[0:L, 0:dvp], PT2[c % 2][0:L, 0:L], vtok[0:L, c, 0:dvp], start=False, stop=True)
                P.stt(Sb_w[0:dk, 0:dvp], S_[0:dk, 0:dvp], eL_fn(c), pst2[c % 2][0:dk, 0:dvp], ALU.mult, ALU.add)
                P.stt(S_[0:dk, 0:dvp], S_[0:dk, 0:dvp], eL_fn(c), pst2[c % 2][0:dk, 0:dvp], ALU.mult, ALU.add)
                g = c % GP
                P.copy("act", ot[0:L, g, 0:dv], po_[0:L, 0:dv])
                P.act(junk[0:L, 0:dv], po_[0:L, 0:dv], AF.Square, accum_out=colg[0:L, g:g + 1])
                if den:
                    P.act(colg[0:L, 16 + g:17 + g], po_[0:L, dv:dv + 1], AF.Abs)

            def stB(c):
                if fast:
                    return stB_fast(c)
                cb = slice(c * L, (c + 1) * L)
                if seq.sample:
                    S = Sf[c % NB]; Sb_ = Sb2[c % 2]
                    P.copy("act", Sb_[0:dk, 0:dvp], S[0:dk, 0:dvp])
                else:
                    S = S_; Sb_ = Sb2[0]
                pst_ = pst2[c % 2] if seq.sample else pst
                if gdn:
                    pk_ = pk2[c % 2] if seq.sample else pk
                    P.mm(pk_[0:L, 0:dv], kT[0:dk, cb], Sb_[0:dk, 0:dv])
                    if L == 1:
                        P.stt(vtok[0:L, c, 0:dv], pk_[0:L, 0:dv], nbg[0:L, c:c + 1], bv[0:L, c, 0:dv], ALU.mult, ALU.add)
                    else:
                        P.stt(Rp[0:L, 0:dv], pk_[0:L, 0:dv], nbg[0:L, c:c + 1], bv[0:L, c, 0:dv], ALU.mult, ALU.add)
                        P.mm(pu[0:L, 0:dv], TT[0:L, c, 0:L], Rp[0:L, 0:dv])
                        P.copy("act", vtok[0:L, c, 0:dv], pu[0:L, 0:dv])
                po_ = po2[c % 2]
                P.mm(po_[0:L, 0:dvp], qgT[0:dk, cb], Sb_[0:dk, 0:dvp], start=True, stop=False)
                P.mm(po_[0:L, 0:dvp], PT2[c % 2][0:L, 0:L], vtok[0:L, c, 0:dvp], start=False, stop=True)
                P.mm(pst_[0:dk, 0:dvp], ktok[0:L, c, 0:dk], vtok[0:L, c, 0:dvp])
                if not seq.sample:
                    P.stt(Sb_[0:dk, 0:dvp], S[0:dk, 0:dvp], eL_fn(c), pst_[0:dk, 0:dvp], ALU.mult, ALU.add)
                    P.stt(S[0:dk, 0:dvp], S[0:dk, 0:dvp], eL_fn(c), pst_[0:dk, 0:dvp], ALU.mult, ALU.add)
                else:
                    P.stt(S[0:dk, 0:dvp], S[0:dk, 0:dvp], eL_fn(c), pst_[0:dk, 0:dvp], ALU.mult, ALU.add)
                    st_out(c, S)
                g = c % GP
                P.copy("act", ot[0:L, g, 0:dv], po_[0:L, 0:dv])
                P.act(junk[0:L, 0:dv], po_[0:L, 0:dv], AF.Square, accum_out=colg[0:L, g:g + 1])
                if den:
                    P.act(colg[0:L, 16 + g:17 + g], po_[0:L, dv:dv + 1], AF.Abs)

            if seq.sample:
                for c in range(min(NB - 1, n)):
                    st_in(c, Sf[c % NB])
            stA(0)
            for c in range(n):
                if seq.sample and c + NB - 1 < n:
                    st_in(c + NB - 1, Sf[(c + NB - 1) % NB])
                if c + 1 < n:
                    stA(c + 1)
                stB(c)
                if c % GP == GP - 1:
                    post_fn(c - GP + 1, GP)

        def eL_gam(seq, dk):
            return lambda c: gamb[0:dk, c * seq.L + seq.L - 1:c * seq.L + seq.L]

        def kdec(seq):
            return lambda g0, g1: decT[0:seq.L, g0:g1, seq.L - 1]

        def load_nw(src, w):
            P.dma("sp", "nwbc", nwbc[:, 0:w], src[0:1, 0:w].broadcast_to([128, w]))

        def rot(ps, scale, b0, b1, out):
            n = b1 - b0
            P.stt(tmpA[:, 0:n], ps, scale, cst[:, b0:b1], ALU.mult, ALU.mult)
            P.stt(tmpB[0:64, 0:n], ps[64:128, :], scale, snt[64:128, b0:b1], ALU.mult, ALU.mult)
            P.stt(tmpB[64:128, 0:n], ps[0:64, :], scale, snt[0:64, b0:b1], ALU.mult, ALU.mult)
            P.tt("dve", out, tmpA[:, 0:n], tmpB[:, 0:n], ALU.add)

        if layer == 1:
            w2 = P.sb("glaw2", [16, 512]); nb2T = P.sb("nb2T", [128, 4]); glrT = P.sb("glrT", [16, SEGL]); eblc = P.sb("eblc", [128, 16])
            gcon = P.sb("gcon", [64, 16])
            P.dma("sp", "lw", w2[:], gla_w2)
            P.dma("sp", "lw", nb2T[:], gla_b2[0].rearrange("(h p) -> p h", p=128), allow_slow_non_contiguous=True)
            P.ts("dve", nb2T[:], nb2T[:], -1.0, ALU.mult)
            wload(wsm[:, :, 0:16], W_in[:, 2048:2064])
            load_nw(gla_nw, 256)
            for h in range(4):
                wload(wh[:, :, 0:128], W_in[:, h * 128:(h + 1) * 128]); wload(wh[:, :, 128:256], W_in[:, 512 + h * 128:512 + (h + 1) * 128])
                wload(wh[:, :, 256:512], W_in[:, 1024 + h * 256:1024 + (h + 1) * 256]); wload(wh[:, :, 512:768], W_in[:, 4112 + h * 256:4112 + (h + 1) * 256])
                S_ = Sf[0]
                P.memset("pool", S_[:, 0:256], 0.0); P.memset("pool", Sb[:, 0:256], 0.0)
                for seq in segs + [samp]:
                    L, n = seq.L, seq.n
                    proj_fm(seq, wsm, 0, 16, lambda ps, b0, b1: P.copy("act", glrT[0:16, b0:b1], ps))
                    for bi, (b0, b1) in enumerate(seq.blocks):
                        ps = psb[bi]
                        P.mm(ps[:, 0:b1 - b0], w2[0:16, h * 128:(h + 1) * 128], glrT[0:16, b0:b1])
                        P.act(tmpA[:, b0:b1], ps[:, 0:b1 - b0], AF.Exp, bias=nb2T[:, h:h + 1], scale=-1.0)
                        P.ts("dve", tmpA[:, b0:b1], tmpA[:, b0:b1], 1.0, ALU.add)
                        P.act(tmpA[:, b0:b1], tmpA[:, b0:b1], AF.Ln)
                        P.ts("dve", tmpC[:, b0:b1], tmpA[:, b0:b1], -1.0 / 16.0, ALU.mult)
                    if L > 1:
                        for c in range(seq.nch):
                            P.scan(tmpB[:, c * L:(c + 1) * L], ones32[:, 0:L], tmpC[:, c * L:(c + 1) * L], 0.0, ALU.mult, ALU.add)
                    else:
                        P.copy("dve", tmpB[:, 0:n], tmpC[:, 0:n])
                    P.act(gamb[:, 0:n], tmpB[:, 0:n], AF.Exp)
                    P.act(tmpC[:, 0:n], tmpB[:, 0:n], AF.Exp, scale=-1.0)
                    P.copy("dve", eblc[:, 0:seq.nch], v3(gamb[:, 0:n], L)[:, :, L - 1])
                    proj_fm(seq, wh, 0, 128, lambda ps, b0, b1: P.stt(qT[:, b0:b1], ps, 128 ** -0.5, gamb[:, b0:b1], ALU.mult, ALU.mult))
                    proj_fm(seq, wh, 128, 128, lambda ps, b0, b1: P.tt("dve", kT[:, b0:b1], ps, tmpC[:, b0:b1], ALU.mult))
                    P.tt("dve", v3(khT[:, 0:n], L), v3(kT[:, 0:n], L), eblc[:, 0:seq.nch].unsqueeze(2).broadcast_to([128, seq.nch, L]), ALU.mult)
                    make_ktok(seq, 128, khT, None)
                    tm(seq, wh, 256, 512, [(vtok, 0, 256, 256, fcopy()), (szw, 256, 512, 256, fgate(0, 256))])
                    _qg = qgT
                    chunk_loop_q = qT
                    P.copy("pool", qgT[:, 0:n], qT[:, 0:n])
                    chunk_loop(seq, 128, 256, 256, lambda c, L=L: tri[0:L, 0:L], lambda c: eblc[:, c:c + 1], post_std(seq, 256, h * 2), S_,
                               st_in=lambda c, S, h=h: P.dma("sp", ("sti", c % 2), S[:, 0:256], sgla[c, h]),
                               st_out=lambda c, S, h=h: P.dma("pool", ("sto", c % 2), s_gla[c, h], S[:, 0:256], final_only=True))
                    if seq is segs[-1]:
                        P.dma("sp", "pst", p_gla[h], S_[:, 0:256], final_only=True)
            load_nw(ret_nw, 256)
            for h in range(4):
                o = 2064
                wload(wh[:, :, 0:128], W_in[:, o + h * 128:o + (h + 1) * 128]); wload(wh[:, :, 128:256], W_in[:, 2576 + h * 128:2576 + (h + 1) * 128])
                wload(wh[:, :, 256:512], W_in[:, 3088 + h * 256:3088 + (h + 1) * 256]); wload(wh[:, :, 512:768], W_in[:, 5136 + h * 256:5136 + (h + 1) * 256])
                S_ = Sf[0]
                P.memset("pool", S_[:, 0:256], 0.0); P.memset("pool", Sb[:, 0:256], 0.0)
                P.memset("pool", gcon[:], math.log(1.0 - 2.0 ** (-5.0 - h)))
                for seq in segs + [samp]:
                    L, n = seq.L, seq.n
                    if seq.sample:
                        P.dma("sp", "cs", cst[:, 0:NS], c_cosS); P.dma("sp", "cs", snt[:, 0:NS], c_sinS)
                    else:
                        P.dma("sp", "cs", cst[:, :], c_cos[:, seq.base:seq.base + SEGL]); P.dma("sp", "cs", snt[:, :], c_sin[:, seq.base:seq.base + SEGL])
                    if seq.sample or seq is segs[0]:
                        decay(seq, gcon[0:L, 0:seq.nch])
                    proj_fm(seq, wh, 0, 128, lambda ps, b0, b1: rot(ps, 1.0, b0, b1, qT[:, b0:b1]))
                    proj_fm(seq, wh, 128, 128, lambda ps, b0, b1: rot(ps, 128 ** -0.5, b0, b1, kT[:, b0:b1]))
                    P.tt("dve", qgT[:, 0:n], qT[:, 0:n], gamb[:, 0:n], ALU.mult)
                    make_ktok(seq, 128, kT, kdec(seq))
                    tm(seq, wh, 256, 512, [(vtok, 0, 256, 256, fcopy()), (szw, 256, 512, 256, fgate(0, 256))])
                    chunk_loop(seq, 128, 256, 256, lambda c, L=L: decT[0:L, c, 0:L], eL_gam(seq, 128), post_std(seq, 256, 8 + h * 2), S_,
                               st_in=lambda c, S, h=h: P.dma("sp", ("sti", c % 2), S[:, 0:256], sret[c, h]),
                               st_out=lambda c, S, h=h: P.dma("pool", ("sto", c % 2), s_ret[c, h], S[:, 0:256], final_only=True))
                    if seq is segs[-1]:
                        P.dma("sp", "pst", p_ret[h], S_[:, 0:256], final_only=True)
            return

        sqb2 = [P.sb("sqb%d" % i, [128, SEGL], BF16) for i in range(2)]
        ubufK = P.sb("ubufK", [128, SEGL + 8]); ubufV = P.sb("ubufV", [128, SEGL + 8]); rawK = P.sb("rawK", [128, SEGL]); rawV = P.sb("rawV", [128, SEGL])
        rinvK = P.sb("rinvK", [128, SEGL]); bufT3 = P.sb("bufT3", [128, 3, 3, 16]); urow3 = P.sb("urow3", [16, 3, 128])
        bcst = P.sb("bcst", [128, 40]); cw = P.sb("cw", [128, 12]); pfx = P.sb("pfx", [128, 3, 3]); scv = P.sb("scv", [16, 3, 3, 128])
        gb8 = P.sb("gb8", [8, 2]); mblk = tmpC[0:8, 0:512]
        iT = tmpA[0:8, 0:512]; fTt = tmpB[0:8, 0:512]; mlast = P.sb("mlast", [8, 1]); mrow = P.sb("mrow", [1, 8])
        P.dma("sp", "lw", bcst[:, 0:8], a_log[0:1, :].broadcast_to([128, 8])); P.dma("sp", "lw", bcst[:, 8:16], dt_bias[0:1, :].broadcast_to([128, 8]))
        P.dma("sp", "lw", bcst[:, 16:32], gate_b[0:1, :].broadcast_to([128, 16]))
        P.dma("sp", "lw", gb8[:, 0:1], gate_b[0, 0:8].rearrange("(p o) -> p o", o=1), allow_slow_non_contiguous=True)
        P.dma("sp", "lw", gb8[:, 1:2], gate_b[0, 8:16].rearrange("(p o) -> p o", o=1), allow_slow_non_contiguous=True)
        P.act(bcst[:, 32:40], bcst[:, 0:8], AF.Exp)
        P.dma("sp", "scv", s_conv[:, 0:2, :], sconv[:, 1:3, :], final_only=True)
        wload(wsm[:, :, 0:16], W_in[:, 3072:3088])
        wload(wsm[:, :, 16:32], W_in[:, 5136:5152])
        tm(full, wsm, 0, 16, [(gates, 0, 16, 16, fcopy("dve"))])
        tm(samp, wsm, 0, 16, [(gates_s, 0, 16, 16, fcopy("dve"))])

        def gdn_gates(gt, r, nch, beta_o, nbeta_o, g_o):
            P.act(beta_o, gt[0:r, 0:nch, 0:8], AF.Sigmoid)
            P.ts("dve", nbeta_o, beta_o, -1.0, ALU.mult)
            P.tt("dve", g_o, gt[0:r, 0:nch, 8:16], bcst[0:r, 8:16].unsqueeze(1).broadcast_to([r, nch, 8]), ALU.add)
            P.act(g_o, g_o, AF.Exp)
            P.ts("dve", g_o, g_o, 1.0, ALU.add)
            P.act(g_o, g_o, AF.Ln)
            P.tt("dve", g_o, g_o, bcst[0:r, 32:40].unsqueeze(1).broadcast_to([r, nch, 8]), ALU.mult)
            P.ts("dve", g_o, g_o, -1.0, ALU.mult)
        gdn_gates(gates, 64, 32, gA[:], gB[:], gC[:])
        gdn_gates(gates_s, 1, 16, sA[:], sB[:], sC[:])
        load_nw(gdn_nw, 128)
        for h in range(8):
            for i in range(3):
                wload(wh[:, :, i * 128:(i + 1) * 128], W_in[:, i * 1024 + h * 128:i * 1024 + (h + 1) * 128])
                P.dma("sp", "cw", cw[:, i * 4:(i + 1) * 4], conv_w[:, i * 1024 + h * 128:i * 1024 + (h + 1) * 128].rearrange("j p -> p j"), allow_slow_non_contiguous=True)
                P.dma("sp", "scvl", scv[:, i, :, :], sconv[:, :, i * 1024 + h * 128:i * 1024 + (h + 1) * 128])
            wload(wh[:, :, 384:512], W_in[:, 6176 + h * 128:6176 + (h + 1) * 128])
            S_ = Sf[0]
            P.memset("pool", S_[:, 0:128], 0.0); P.memset("pool", Sb[:, 0:128], 0.0)
            for si, seq in enumerate(segs + [samp]):
                L, n, nch = seq.L, seq.n, seq.nch
                if seq.sample:
                    beta_h, nbeta_h, g_h = sA[0:1, :, h], sB[0:1, :, h], sC[0:1, :, h]
                else:
                    c0 = si * NCS
                    beta_h, nbeta_h, g_h = gA[:, c0:c0 + NCS, h], gB[:, c0:c0 + NCS, h], gC[:, c0:c0 + NCS, h]
                decay(seq, g_h, strict=(L > 1))
                def qkv_lockstep(seq=seq, si=si, L=L, n=n, nch=nch, beta_h=beta_h, h=h):
                    ub = [ubuf, ubufK, ubufV]; rw = [tmpA, rawK, rawV]; rv = [tmpC, rinvK]
                    for i in range(3):
                        proj_fm(seq, wh, i * 128, 128, lambda ps, b0, b1, i=i: P.copy("act", ub[i][:, 3 + b0:3 + b1], ps))
                    taps3 = []
                    for i in range(3):
                        if not seq.sample:
                            if si == 0:
                                P.memset("pool", ub[i][:, 0:3], 0.0)
                            else:
                                P.copy("pool", ub[i][:, 0:3], pfx[:, i, :])
                            taps3.append([ub[i][:, j:j + n] for j in range(4)])
                        else:
                            for j in range(3):
                                ps = pj()
                                P.tr(ps[:, 0:NS], scv[0:NS, i, j, :], ident[0:NS, 0:NS])
                                P.copy("act", bufT3[:, i, j, :], ps[:, 0:NS])
                            taps3.append([bufT3[:, i, 0, :], bufT3[:, i, 1, :], bufT3[:, i, 2, :], ub[i][:, 3:3 + NS]])
                            ps = pj()
                            P.tr(ps[0:NS, 0:128], ub[i][:, 3:3 + NS], ident[:])
                            P.copy("act", urow3[:, i, :], ps[0:NS, 0:128])
                            P.dma("sp", "uro", s_conv[:, 2, i * 1024 + h * 128:i * 1024 + (h + 1) * 128], urow3[:, i, :], final_only=True)
                    for i in range(3):
                        P.ts("dve", rw[i][:, 0:n], taps3[i][0], cw[:, i * 4:i * 4 + 1], ALU.mult)
                        for j in range(1, 4):
                            P.stt(rw[i][:, 0:n], taps3[i][j], cw[:, i * 4 + j:i * 4 + j + 1], rw[i][:, 0:n], ALU.mult, ALU.add)
                        if not seq.sample:
                            P.copy("pool", pfx[:, i, :], ub[i][:, n:n + 3])
                            if seq is segs[-1]:
                                P.dma("sp", "pcv", p_conv[:, i * 1024 + h * 128:i * 1024 + (h + 1) * 128].rearrange("j p -> p j"), ub[i][:, n:n + 3], allow_slow_non_contiguous=True, final_only=True)
                    for i in range(3):
                        P.act(rw[i][:, 0:n], rw[i][:, 0:n], AF.Silu)
                    for i in range(2):
                        P.act(sqb2[i][:, 0:n], rw[i][:, 0:n], AF.Square)
                    pss = []
                    for i in range(2):
                        ps = pj()
                        P.mm(ps[:, 0:n], onesb[:, :], sqb2[i][:, 0:n])
                        P.ts("dve", rv[i][:, 0:n], ps[:, 0:n], EPS, ALU.add)
                    for i in range(2):
                        P.act(rv[i][:, 0:n], rv[i][:, 0:n], AF.Ln)
                    for i in range(2):
                        P.act(rv[i][:, 0:n], rv[i][:, 0:n], AF.Exp, scale=-0.5)
                    P.stt(qT[:, 0:n], rw[0][:, 0:n], 128 ** -0.5, rv[0][:, 0:n], ALU.mult, ALU.mult)
                    P.tt("dve", kT32[:, 0:n], rw[1][:, 0:n], rv[1][:, 0:n], ALU.mult)
                    P.copy("act", kT[:, 0:n], kT32[:, 0:n])

                def vT_piece(seq=seq, L=L, nch=nch, beta_h=beta_h):
                    for c in range(nch):
                        ps = pj()
                        P.tr(ps[0:L, 0:128], rawV[:, c * L:(c + 1) * L], ident[:])
                        P.ts("dve", bv[0:L, c, :], ps[0:L, 0:128], beta_h[:, c:c + 1], ALU.mult)

                def misc_piece(seq=seq, L=L, n=n, nch=nch, nbeta_h=nbeta_h):
                    P.tt("dve", qgT[:, 0:n], qT[:, 0:n], gamb[:, 0:n], ALU.mult)
                    P.act(gtok[0:L, 0:nch], bht[0:L, 0:nch], AF.Exp)
                    P.tt("dve", nbg[0:L, 0:nch], gtok[0:L, 0:nch], nbeta_h, ALU.mult)
                    make_ktok(seq, 128, kT, kdec(seq))

                def z_piece(seq=seq):
                    tm(seq, wh, 384, 128, [(szw, 0, 128, 128, fgate(0, 128))])

                def doubling(L=L, nch=nch, nbeta_h=nbeta_h):
                    pX, pY, pZ = B[3], B[5], B[6]
                    for b in range(nch // 8):
                        for i in range(8):
                            c = b * 8 + i
                            P.mm(pX[0:64, i * 64:(i + 1) * 64], kT32[:, c * L:(c + 1) * L], kT32[:, c * L:(c + 1) * L])
                        P.tt("dve", Nn[:], v3(pX[0:64, :], 64), decS[:, b * 8:(b + 1) * 8, :], ALU.mult)
                        P.tt("dve", Nn[:], Nn[:], nbeta_h[:, b * 8:(b + 1) * 8].unsqueeze(2).broadcast_to([64, 8, 64]), ALU.mult)
                        for i in range(8):
                            P.tr(pY[0:64, i * 64:(i + 1) * 64], Nn[:, i, :], ident[0:64, 0:64])
                        P.copy("act", Mm[:], v3(pY[0:64, :], 64))
                        P.tt("dve", Xx[:], Mm[:], ident[0:64, 0:64].unsqueeze(1).broadcast_to([64, 8, 64]), ALU.add)
                        yield
                        for lvl in range(5):
                            for i in range(8):
                                P.mm(pX[0:64, i * 64:(i + 1) * 64], Mm[:, i, :], Nn[:, i, :])
                            if lvl < 4:
                                for i in range(8):
                                    P.mm(pY[0:64, i * 64:(i + 1) * 64], Nn[:, i, :], Mm[:, i, :])
                            P.copy("act", Nn[:], v3(pX[0:64, :], 64))
                            if lvl < 4:
                                P.copy("dve", Mm[:], v3(pY[0:64, :], 64))
                            for i in range(8):
                                P.mm(pZ[0:64, i * 64:(i + 1) * 64], Nn[:, i, :], Xx[:, i, :])
                            P.tt("dve", Xx[:], Xx[:], v3(pZ[0:64, :], 64), ALU.add)
                            if lvl < 4:
                                yield
                        P.copy("act", TT[:, b * 8:(b + 1) * 8, :], Xx[:])

                qkv_lockstep()
                pieces = [vT_piece, misc_piece, z_piece]
                if L > 1:
                    for _ in doubling():
                        if pieces:
                            pieces.pop(0)()
                while pieces:
                    pieces.pop(0)()
                chunk_loop(seq, 128, 128, 128, lambda c, L=L: decT[0:L, c, 0:L], eL_gam(seq, 128), post_std(seq, 128, h), S_, gdn=True,
                           st_in=lambda c, S, h=h: P.dma("sp", ("sti", c % 2), S[:, 0:128], sg[c, h]),
                           st_out=lambda c, S, h=h: P.dma("pool", ("sto", c % 2), s_gdn[c, h], S[:, 0:128], final_only=True))
                if seq is segs[-1]:
                    P.dma("sp", "pst", p_gdn[h], S_[:, 0:128], final_only=True)

        tm(full, wsm, 16, 16, [(gates, 0, 16, 16, lambda o, i: P.tt("dve", o, i, bcst[i.base_partition():i.base_partition() + i.shape[0], 16:32], ALU.add))])
        tm(samp, wsm, 16, 16, [(gates_s, 0, 16, 16, lambda o, i: P.tt("dve", o, i, bcst[i.base_partition():i.base_partition() + i.shape[0], 16:32], ALU.add))])

        def logsig(o, i):
            P.act(o, i, AF.Exp, scale=-1.0)
            P.ts("dve", o, o, 1.0, ALU.add)
            P.act(o, o, AF.Ln)
            P.ts("dve", o, o, -1.0, ALU.mult)
        logsig(gA[:], gates[:, :, 8:16])
        P.act(gB[:], gates[:, :, 0:8], AF.Exp); P.ts("dve", gB[:], gB[:], 0.125, ALU.mult)
        for bi in range(4):
            for (cofs, dst, gcol) in ((16, iT, 0), (24, fTt, 1)):
                ps = pj()
                for c in range(8):
                    P.mm(ps[0:8, :], wsm[:, c, cofs:cofs + 8], hT[:, c, bi * 512:(bi + 1) * 512], start=(c == 0), stop=(c == 7))
                P.ts("dve", dst, ps[0:8, :], gb8[:, gcol:gcol + 1], ALU.add)
            logsig(fTt, fTt)
            P.scan(mblk, fTt, iT, 0.0 if bi == 0 else mlast[:, 0:1], ALU.add, ALU.max)
            P.copy("dve", mlast[:], tmpC[0:8, 511:512])
        ps = pj()
        P.tr(ps[0:1, 0:8], mlast[0:8, 0:1], ident[0:8, 0:8])
        P.copy("act", mrow[:], ps[0:1, 0:8])
        P.dma("sp", "pmm", p_mm, mrow[:], final_only=True)
        P.act(mrow[:], mrow[:], AF.Exp, scale=-1.0)
        ps = pj()
        P.mm(ps[0:64, 0:8], ones32[0:1, 0:64], mrow[0:1, :])
        P.copy("act", fsc[:, 128:136], ps[0:64, 0:8])
        P.dma("sp", "sm0", sm0[:].rearrange("o s h -> o (s h)"), smm)
        logsig(sA[:], gates_s[:, :, 8:16])
        P.tt("dve", sB[:], gates_s[:, :, 0:8], sm0[:], ALU.subtract)
        P.act(sB[:], sB[:], AF.Exp); P.ts("dve", sB[:], sB[:], 0.125, ALU.mult)
        P.tt("dve", sC[:], sA[:], sm0[:], ALU.add)
        P.tt("dve", sC[:], sC[:], gates_s[:, :, 0:8], ALU.max)
        P.dma("sp", "smo", s_mm, sC[:].rearrange("o s h -> o (s h)"), final_only=True)
        P.tt("dve", sD[:], sm0[:], sC[:], ALU.subtract)
        P.act(sD[:], sD[:], AF.Exp)
        P.act(sE[:], sm0[:], AF.Exp, scale=-1.0)
        ps = pj()
        P.mm(ps[0:64, 0:128], ones32[0:1, 0:64], sD[:].rearrange("o s h -> o (s h)"))
        P.copy("act", fsc[:, 0:128], ps[0:64, 0:128])
        load_nw(ml_nw, 128)
        for h in range(8):
            wload(wh[:, :, 0:64], W_in[:, 3088 + h * 64:3088 + (h + 1) * 64]); wload(wh[:, :, 64:128], W_in[:, 3600 + h * 64:3600 + (h + 1) * 64])
            wload(wh[:, :, 128:256], W_in[:, 4112 + h * 128:4112 + (h + 1) * 128]); wload(wh[:, :, 256:384], W_in[:, 5152 + h * 128:5152 + (h + 1) * 128])
            wload(wh[:, :, 384:512], W_in[:, 7200 + h * 128:7200 + (h + 1) * 128])
            S_ = Sf[0]
            P.memset("pool", S_[:, 0:130], 0.0); P.memset("pool", Sb[:, 0:130], 0.0)
            for si, seq in enumerate(segs + [samp]):
                L, n, nch = seq.L, seq.n, seq.nch
                if seq.sample:
                    lf_h, cs_h = sA[0:1, :, h], sB[0:1, :, h]
                else:
                    c0 = si * NCS
                    lf_h, cs_h = gA[:, c0:c0 + NCS, h], gB[:, c0:c0 + NCS, h]
                decay(seq, lf_h)
                proj_fm(seq, wh, 0, 64, lambda ps, b0, b1: P.copy("act", qT[0:64, b0:b1], ps))
                proj_fm(seq, wh, 64, 64, lambda ps, b0, b1: P.copy("act", kT[0:64, b0:b1], ps))
                P.tt("dve", qgT[0:64, 0:n], qT[0:64, 0:n], gamb[0:64, 0:n], ALU.mult)
                P.tt("dve", decT[0:L, 0:nch, 0:L], decT[0:L, 0:nch, 0:L], cs_h.unsqueeze(2).broadcast_to([L, nch, L]), ALU.mult)
                make_ktok(seq, 64, kT, kdec(seq))
                tm(seq, wh, 128, 384, [(vtok, 0, 128, 128, fcopy()), (szw, 128, 384, 128, fgate(0, 128, sig=True))])
                P.memset("pool", vtok[0:L, 0:nch, 128:129], 1.0)

                def post_ml(c0, G, seq=seq, h=h, L=L):
                    dn = colg[0:L, 16:16 + G]
                    if seq.sample:
                        P.tt("dve", dn, dn, sE[0:1, c0:c0 + G, h], ALU.max)
                    else:
                        P.ts("dve", dn, dn, 1.0, ALU.max)
                    P.recip(dn, dn)
                    t_ = colg[0:L, 24:24 + G]
                    P.tt("dve", t_, dn, dn, ALU.mult)
                    P.tt("dve", t_, t_, colg[0:L, 0:G], ALU.mult)
                    rms_scale(colg[0:L, 8:8 + G], t_, 128)
                    P.tt("dve", colg[0:L, 8:8 + G], colg[0:L, 8:8 + G], dn, ALU.mult)
                    post_rows(seq, c0, G, 128, 8 + h, colg[0:L, 8:8 + G])

                def st_in(c, S, h=h):
                    P.dma("sp", ("sti", c % 2), S[0:64, 0:129], smcn[c, h])

                def st_out(c, S, h=h):
                    So_ = So4[c % 4]
                    P.ts("dve", So_[0:64, 0:129], S[0:64, 0:129], fsc[0:64, c * 8 + h:c * 8 + h + 1], ALU.mult)
                    P.dma("pool", ("sto", 0), s_mcn[c, h], So_[0:64, 0:129], final_only=True)
                chunk_loop(seq, 64, 128, 129, lambda c, L=L: decT[0:L, c, 0:L], eL_gam(seq, 64), post_ml, S_, st_in=st_in, st_out=st_out, den=True)
                if seq is segs[-1]:
                    P.ts("dve", So[0:64, 0:129], S_[0:64, 0:129], fsc[0:64, 128 + h:129 + h], ALU.mult)
                    P.dma("sp", "pst", p_mcn[h], So[0:64, 0:129], final_only=True)

    for layer in range(2):
        m = P.scope_begin()
        mod_phase(layer, m)
        phase0(layer, xp if layer == 0 else x1)
        P.scope_end(m)
        m = P.scope_begin()
        mixers(layer)
        P.scope_end(m)
        m = P.scope_begin()
        phaseB(layer, ev_w_out if layer == 0 else od_w_out, xp if layer == 0 else x1, layer == 1)
        P.scope_end(m)
    P.emit()
    P.close()
    return nc

_NC_CACHE = {}


def _consts():
    ident = np.eye(128, dtype=np.float32)
    ii = np.arange(64)
    tri = (ii[:, None] <= ii[None, :]).astype(np.float32)
    negU = np.where(ii[:, None] <= ii[None, :], 0.0, NEG).astype(np.float32)
    negLs = np.where(ii[None, :] < ii[:, None], 0.0, NEG).astype(np.float32)
    half = 64
    inv = (np.float32(10000.0) ** (-np.arange(half, dtype=np.float32) / np.float32(half))).astype(np.float32)

    def tabs(pos):
        ang = (pos.astype(np.float32)[None, :] * inv[:, None]).astype(np.float32)
        c = np.cos(ang).astype(np.float32); s = np.sin(ang).astype(np.float32)
        return np.concatenate([c, c], 0), np.concatenate([s, -s], 0)
    cosT, sinT = tabs(np.arange(T))
    cosS, sinS = tabs(np.full((NS,), 16384))
    sel = np.zeros((17, 128), np.float32); sel[16, :] = 1.0
    return dict(c_ident=ident, c_tri=tri, c_negU=negU, c_negLs=negLs, c_cos=np.ascontiguousarray(cosT), c_sin=np.ascontiguousarray(sinT),
                c_cosS=np.ascontiguousarray(cosS), c_sinS=np.ascontiguousarray(sinS), c_sel=sel)


def kernel(x_prompt, x_sample, c_prompt, c_sample, state_gdn, state_gdn_conv, state_mlstm_c, state_mlstm_n, state_mlstm_m,
           state_gla, state_ret, ada_w, ada_b, norm_w, ev_w_in, ev_w_out, gdn_conv_w, gdn_a_log, gdn_dt_bias, gdn_norm_w,
           mlstm_gate_b, mlstm_norm_w, od_w_in, od_w_out, gla_w2, gla_b2, gla_norm_w, ret_norm_w, final_norm_w):
    f = lambda a: np.ascontiguousarray(np.asarray(a, dtype=np.float32))
    if "nc" not in _NC_CACHE:
        _NC_CACHE["nc"] = build_nc()
    nc = _NC_CACHE["nc"]
    cst = _consts()
    shared = dict(ada_w=f(ada_w), ada_b=f(ada_b), norm_w=f(norm_w), ev_w_in=f(ev_w_in[0]), ev_w_out=f(ev_w_out[0]), conv_w=f(gdn_conv_w[0]),
                  a_log=f(gdn_a_log), dt_bias=f(gdn_dt_bias), gdn_nw=f(gdn_norm_w), gate_b=f(mlstm_gate_b), ml_nw=f(mlstm_norm_w),
                  od_w_in=f(od_w_in[0]), od_w_out=f(od_w_out[0]), gla_w2=f(gla_w2[0]), gla_b2=f(gla_b2), gla_nw=f(gla_norm_w), ret_nw=f(ret_norm_w),
                  fin_nw=f(np.asarray(final_norm_w).reshape(1, D)))
    shared.update(cst)
    in_maps = []
    for b in range(8):
        sl = slice(b * NS, (b + 1) * NS)
        m = dict(shared)
        m.update(xp=f(x_prompt[b]), xs=f(np.asarray(x_sample)[sl, 0]), cc=f(np.concatenate([np.asarray(c_sample)[sl], np.asarray(c_prompt)[b:b + 1]], 0)),
                 sg=f(state_gdn[0][sl]), sconv=f(state_gdn_conv[0][sl]), smcn=f(np.concatenate([np.asarray(state_mlstm_c[0][sl]), np.asarray(state_mlstm_n[0][sl])[..., None]], -1)),
                 smm=f(np.asarray(state_mlstm_m[0][sl]).reshape(1, 128)), sgla=f(state_gla[0][sl]), sret=f(state_ret[0][sl]))
        in_maps.append(m)
    res = run_bass_kernel_spmd(nc, in_maps, core_ids=list(range(8)))
    R = res.results
    st = lambda k: np.stack([np.asarray(r[k], dtype=np.float32) for r in R], 0)
    cat = lambda k: np.concatenate([np.asarray(r[k], dtype=np.float32) for r in R], 0)
    y_prompt = st("yp")
    y_sample = cat("ys").reshape(128, 1, D)
    outs = (y_prompt, y_sample,
            st("p_gdn")[None], st("p_conv")[None], st("p_mcn")[None][..., 0:128], st("p_mcn")[None][..., 128], st("p_mm").reshape(8, 8)[None],
            st("p_gla")[None], st("p_ret")[None],
            cat("s_gdn")[None], cat("s_conv")[None], cat("s_mcn")[None][..., 0:128], cat("s_mcn")[None][..., 128], cat("s_mm").reshape(128, 8)[None],
            cat("s_gla")[None], cat("s_ret")[None])
    return tuple(np.ascontiguousarray(o.astype(np.float32)) for o in outs)
```
